# Optimizing a Trainium2 kernel written in Bass

```python
import math
import jax, jax.numpy as jnp
from jax import lax
import numpy as np

D_MODEL = 1024
BATCH = 16
SEQ = 2048
DEPTH = 2

N_MIXERS = 2
N_META = 16
N_HEADS = 16
HEAD_DIM = D_MODEL // N_HEADS
N_IDX_HEADS = 8
IDX_DIM = 64
TOPK_MAX = 256
Q_BLOCK = 128
REL_BUCKETS = 32
REL_MAX_DIST = 128
CONV_WIDTH = 31
D_FF = -(-8 * D_MODEL // (3 * 256)) * 256
LN_EPS = 1e-5
DEEPNORM_ALPHA = (2 * DEPTH) ** 0.25
DEEPNORM_BETA = (8 * DEPTH) ** -0.25

N_ATTN_LAYERS = (DEPTH + N_MIXERS - 1) // N_MIXERS
N_CONV_LAYERS = DEPTH // N_MIXERS

ATTN_W = N_HEADS * HEAD_DIM
SPLITS = [ATTN_W, 2 * ATTN_W, 3 * ATTN_W,
          3 * ATTN_W + N_IDX_HEADS * IDX_DIM,
          3 * ATTN_W + N_IDX_HEADS * IDX_DIM + IDX_DIM]
IN_W = SPLITS[-1] + N_IDX_HEADS

kernel_name = "hybrid_dsa_conformer_deepnorm"


def layer_norm(x, g, b):
    xf = x.astype(jnp.float32)
    mu = jnp.mean(xf, axis=-1, keepdims=True)
    var = jnp.mean(jnp.square(xf - mu), axis=-1, keepdims=True)
    return ((xf - mu) * lax.rsqrt(var + LN_EPS)).astype(x.dtype) * g + b


def t5_bucket(dist):
    max_exact = REL_BUCKETS // 2
    d = jnp.maximum(dist, 1).astype(jnp.float32)
    large = max_exact + (jnp.log(d / max_exact) / math.log(REL_MAX_DIST / max_exact)
                         * (REL_BUCKETS - max_exact)).astype(jnp.int32)
    large = jnp.minimum(large, REL_BUCKETS - 1)
    return jnp.where(dist < max_exact, dist, large)


def dsa_attention(h, w_in, w_o, rel_bias, k_top):
    B, T, _ = h.shape
    proj = h @ w_in
    q, k, v, qi, ki, wi = jnp.split(proj, SPLITS, axis=-1)
    q = q.reshape(B, T, N_HEADS, HEAD_DIM)
    k = k.reshape(B, T, N_HEADS, HEAD_DIM)
    v = v.reshape(B, T, N_HEADS, HEAD_DIM)
    qi = qi.reshape(B, T, N_IDX_HEADS, IDX_DIM)
    wi = wi * (N_IDX_HEADS ** -0.5 * IDX_DIM ** -0.5)
    n_blk = -(-T // Q_BLOCK)
    pad = n_blk * Q_BLOCK - T
    padq = lambda a: jnp.pad(a, [(0, 0), (0, pad)] + [(0, 0)] * (a.ndim - 2))
    q_p, qi_p, wi_p = padq(q), padq(qi), padq(wi)
    key_pos = jnp.arange(T)
    scale = HEAD_DIM ** -0.5
    gather = jax.vmap(lambda a, idx: a[idx])

    def block(i):
        q0 = i * Q_BLOCK
        qb = lax.dynamic_slice_in_dim(q_p, q0, Q_BLOCK, axis=1)
        qib = lax.dynamic_slice_in_dim(qi_p, q0, Q_BLOCK, axis=1)
        wib = lax.dynamic_slice_in_dim(wi_p, q0, Q_BLOCK, axis=1)
        t = q0 + jnp.arange(Q_BLOCK)
        causal = key_pos[None, :] <= t[:, None]
        idx_logits = jnp.einsum('bqhd,bsd->bqhs', qib, ki)
        score = jnp.einsum('bqhs,bqh->bqs', jax.nn.relu(idx_logits), wib).astype(jnp.float32)
        score = jnp.where(causal[None], score, -jnp.inf)
        top_val, top_idx = lax.top_k(score, k_top)
        valid = jnp.isfinite(top_val)
        k_sel = gather(k, top_idx)
        v_sel = gather(v, top_idx)
        dist = jnp.maximum(t[None, :, None] - top_idx, 0)
        bias = rel_bias[t5_bucket(dist)]
        logits = (jnp.einsum('bqhd,bqkhd->bqhk', qb, k_sel).astype(jnp.float32) * scale
                  + jnp.transpose(bias, (0, 1, 3, 2)).astype(jnp.float32))
        logits = jnp.where(valid[:, :, None, :], logits, -jnp.inf)
        p = jax.nn.softmax(logits, axis=-1).astype(v.dtype)
        return jnp.einsum('bqhk,bqkhd->bqhd', p, v_sel)

    out = lax.map(block, jnp.arange(n_blk))
    out = jnp.transpose(out, (1, 0, 2, 3, 4)).reshape(B, n_blk * Q_BLOCK, ATTN_W)[:, :T]
    return out @ w_o


def conformer_conv(h, w_pw1, b_pw1, w_dw, b_dw, ln_g, ln_b, w_pw2, b_pw2):
    D = h.shape[-1]
    a = h @ w_pw1 + b_pw1
    u = a[..., :D] * jax.nn.sigmoid(a[..., D:])
    y = lax.conv_general_dilated(u, w_dw[:, None, :], window_strides=(1,),
                                 padding=[(CONV_WIDTH - 1, 0)],
                                 dimension_numbers=('NWC', 'WIO', 'NWC'),
                                 feature_group_count=D) + b_dw
    y = jax.nn.silu(layer_norm(y, ln_g, ln_b))
    return y @ w_pw2 + b_pw2


def swiglu(h, w_gate, w_up, w_down):
    return (jax.nn.silu(h @ w_gate) * (h @ w_up)) @ w_down


def setup_inputs(seed: int = 0) -> dict:
    key = jax.random.key(seed)
    ks = jax.random.split(key, 24)
    nrm = lambda k, shape, s: jax.random.normal(k, shape, jnp.float32) * s
    D = D_MODEL
    col_scale = jnp.ones((IN_W,), jnp.float32).at[2 * ATTN_W:3 * ATTN_W].set(DEEPNORM_BETA)
    return {
        "x": nrm(ks[0], (BATCH, SEQ, D), 1.0),
        "meta_tokens": nrm(ks[1], (N_META, D), 1.0),
        "rel_bias": nrm(ks[2], (REL_BUCKETS, N_HEADS), 0.5),
        "w_in_attn": nrm(ks[3], (N_ATTN_LAYERS, D, IN_W), D ** -0.5) * col_scale,
        "w_o_attn": nrm(ks[4], (N_ATTN_LAYERS, ATTN_W, D), ATTN_W ** -0.5 * DEEPNORM_BETA),
        "w_pw1": nrm(ks[5], (N_CONV_LAYERS, D, 2 * D), D ** -0.5),
        "b_pw1": nrm(ks[6], (N_CONV_LAYERS, 2 * D), 0.02),
        "w_dw": nrm(ks[7], (N_CONV_LAYERS, CONV_WIDTH, D), CONV_WIDTH ** -0.5),
        "b_dw": nrm(ks[8], (N_CONV_LAYERS, D), 0.02),
        "conv_ln_g": 1.0 + nrm(ks[9], (N_CONV_LAYERS, D), 0.02),
        "conv_ln_b": nrm(ks[10], (N_CONV_LAYERS, D), 0.02),
        "w_pw2": nrm(ks[11], (N_CONV_LAYERS, D, D), D ** -0.5 * DEEPNORM_BETA),
        "b_pw2": nrm(ks[12], (N_CONV_LAYERS, D), 0.02),
        "ln1_g": 1.0 + nrm(ks[13], (DEPTH, D), 0.02),
        "ln1_b": nrm(ks[14], (DEPTH, D), 0.02),
        "ffn_w_gate": nrm(ks[15], (DEPTH, D, D_FF), D ** -0.5),
        "ffn_w_up": nrm(ks[16], (DEPTH, D, D_FF), D ** -0.5),
        "ffn_w_down": nrm(ks[17], (DEPTH, D_FF, D), D_FF ** -0.5 * DEEPNORM_BETA),
        "ln2_g": 1.0 + nrm(ks[18], (DEPTH, D), 0.02),
        "ln2_b": nrm(ks[19], (DEPTH, D), 0.02),
    }


def reference(x, meta_tokens, rel_bias, w_in_attn, w_o_attn, w_pw1, b_pw1, w_dw, b_dw,
              conv_ln_g, conv_ln_b, w_pw2, b_pw2, ln1_g, ln1_b, ffn_w_gate, ffn_w_up,
              ffn_w_down, ln2_g, ln2_b):
    B, S, D = x.shape
    k_top = min(TOPK_MAX, S // 4)
    meta = jnp.broadcast_to(meta_tokens[None].astype(x.dtype), (B, N_META, D))
    h = jnp.concatenate([meta, x], axis=1)
    for i in range(DEPTH):
        j = i // N_MIXERS
        if i % N_MIXERS == 0:
            mix = dsa_attention(h, w_in_attn[j], w_o_attn[j], rel_bias, k_top)
        else:
            mix = conformer_conv(h, w_pw1[j], b_pw1[j], w_dw[j], b_dw[j],
                                 conv_ln_g[j], conv_ln_b[j], w_pw2[j], b_pw2[j])
        h = layer_norm(DEEPNORM_ALPHA * h + mix, ln1_g[i], ln1_b[i])
        ffn = swiglu(h, ffn_w_gate[i], ffn_w_up[i], ffn_w_down[i])
        h = layer_norm(DEEPNORM_ALPHA * h + ffn, ln2_g[i], ln2_b[i])
    return h[:, N_META:]
```

```python
import math
import os
from functools import partial
from contextlib import ExitStack
import numpy as np
import concourse.bass as bass
import concourse.mybir as mybir
from concourse.bass_utils import run_bass_kernel_spmd

F32 = mybir.dt.float32
BF16 = mybir.dt.bfloat16
AF = mybir.ActivationFunctionType
ALU = mybir.AluOpType

D = 1024
KC = 8
NH = 16
NIH = 8
DFF = 2816
FC = 22
SEQ = 2048
NSLOT = 2176
NKT = 17
NT = 2
G = NT * 128
NGRP = SEQ // G
NS = 4
WSZ = 4096
NIT = 12
ALPHA = 4.0 ** 0.25
WI_SCALE = (NIH ** -0.5) * (64 ** -0.5)
ATT_SCALE = 0.125
LN_EPS = 1e-5
NEG = -1.0e30
SAME_ENG_SYNC = True
USE_POOL_POW = True
NCH = 64


class Op:
    __slots__ = ("eng", "fn", "is_dma", "deps", "flag", "ms", "dsem", "dval", "idx")


class Prog:
    def __init__(self, nc, es, n_dma_sems=24):
        self.nc = nc
        self.eng = {"pe": nc.tensor, "act": nc.scalar, "dve": nc.vector, "pool": nc.gpsimd, "sp": nc.sync}
        self.sem = {e: es.enter_context(nc.semaphore("ms_" + e)) for e in self.eng}
        self.dsems = [es.enter_context(nc.semaphore("dq%d" % i)) for i in range(n_dma_sems)]
        self.dcount = [0] * n_dma_sems
        self.dlast = [None] * n_dma_sems
        self.dnext = 0
        self.ops = []
        self.ms_count = {e: 0 for e in self.eng}
        self.waited = {}
        self.lw = {}
        self.rd = {}
        self.rd_dma = {}
        self.names = {}
        self.nfloor = {}
        self.gfloor = []
        self.floor_done = set()
        self.last_on_eng = {}
        self.nops = 0

    def _add(self, eng, fn, r, w, is_dma):
        o = Op()
        o.eng = eng
        o.fn = fn
        o.is_dma = is_dma
        o.flag = False
        o.ms = None
        o.dsem = None
        o.dval = 0
        o.idx = self.nops
        self.nops += 1
        deps = []
        for k in r:
            if k[0] in ("ps", "psb"):
                rr = self.rd.get(k)
                if rr:
                    deps.extend(o2 for e2, o2 in rr.items() if e2 != eng)
        if eng not in self.floor_done:
            deps.extend(self.gfloor)
            self.floor_done.add(eng)
        for k in r:
            d = self.lw.get(k)
            if d is not None:
                deps.append(d)
        for k in w:
            d = self.lw.get(k)
            if d is not None:
                deps.append(d)
            rr = self.rd.get(k)
            if rr:
                deps.extend(rr.values())
            rl = self.rd_dma.get(k)
            if rl:
                deps.extend(rl)
            nf = self.nfloor.get(k[0])
            if nf is not None and eng not in nf[1]:
                deps.extend(nf[0])
                nf[1].add(eng)
        if is_dma:
            s = self.dnext % len(self.dsems)
            self.dnext += 1
            if self.dlast[s] is not None:
                deps.append(self.dlast[s])
            self.dcount[s] += 1
            o.dsem = self.dsems[s]
            o.dval = 16 * self.dcount[s]
            self.dlast[s] = o
        fd = []
        seen = set()
        for d in deps:
            if d is o or id(d) in seen:
                continue
            seen.add(id(d))
            if not d.is_dma and d.eng == eng:
                if eng == "pe" or eng == "sp" or not SAME_ENG_SYNC:
                    continue
            if not d.is_dma:
                d.flag = True
            fd.append(d)
        o.deps = fd
        for k in w:
            self.lw[k] = o
            self.rd[k] = {}
            self.rd_dma[k] = []
            self.names.setdefault(k[0], set()).add(k)
        for k in r:
            if k in w:
                continue
            self.names.setdefault(k[0], set()).add(k)
            if is_dma:
                self.rd_dma.setdefault(k, []).append(o)
            else:
                self.rd.setdefault(k, {})[eng] = o
        self.ops.append(o)
        if not is_dma:
            self.last_on_eng[eng] = o
        return o

    def op(self, eng, _f, r=(), w=(), **kw):
        return self._add(eng, partial(_f, **kw), tuple(r), tuple(w), False)

    def dma(self, q, r=(), w=(), **kw):
        return self._add(q, partial(self.eng[q].dma_start, **kw), tuple(r), tuple(w), True)

    def fence(self, from_names, to_names):
        acc = []
        for n in from_names:
            for k in self.names.get(n, ()):
                d = self.lw.get(k)
                if d is not None:
                    acc.append(d)
                rr = self.rd.get(k)
                if rr:
                    acc.extend(rr.values())
                rl = self.rd_dma.get(k)
                if rl:
                    acc.extend(rl)
        best = {}
        dm = []
        for d in acc:
            if d.is_dma:
                dm.append(d)
            else:
                b = best.get(d.eng)
                if b is None or d.idx > b.idx:
                    best[d.eng] = d
        lst = list(best.values()) + dm
        for n in to_names:
            prev = self.nfloor.get(n)
            self.nfloor[n] = ((list(prev[0]) if prev else []) + lst, set())

    def flush(self):
        for e, o in self.last_on_eng.items():
            o.flag = True
        for o in self.ops:
            E = self.eng[o.eng]
            for d in o.deps:
                if d.is_dma:
                    sem, val = d.dsem, d.dval
                else:
                    assert d.ms is not None, "dep on unflagged op"
                    sem, val = self.sem[d.eng], d.ms
                key = (o.eng, id(sem))
                if self.waited.get(key, 0) < val:
                    E.wait_ge(sem, val)
                    self.waited[key] = val
            ins = o.fn()
            if o.is_dma:
                ins.then_inc(o.dsem, 16)
            elif o.flag:
                self.ms_count[o.eng] += 1
                o.ms = self.ms_count[o.eng]
                ins.then_inc(self.sem[o.eng], 1)
            o.fn = None
        self.gfloor = [o for o in self.last_on_eng.values()] + [d for d in self.dlast if d is not None]
        self.floor_done = set()
        self.ops = []
        self.lw = {}
        self.rd = {}
        self.rd_dma = {}
        self.names = {}
        self.nfloor = {}

    def final_wait(self):
        E = self.eng["sp"]
        for s, d in enumerate(self.dlast):
            if d is not None:
                E.wait_ge(d.dsem, d.dval)
        for e, o in self.last_on_eng.items():
            if o.ms is not None and e != "sp":
                E.wait_ge(self.sem[e], o.ms)


def chunk_table():
    ch = []
    ch.append(dict(kind="std", src="w_in", c0=3072, w=512, nk=8, nm="qi"))
    ch.append(dict(kind="kiwi", nk=8, w=136, nm="kiwi"))
    for nm, c0 in (("q0", 0), ("q1", 512), ("k0", 1024), ("k1", 1536)):
        ch.append(dict(kind="std", src="w_in", c0=c0, w=512, nk=8, nm=nm))
    ch.append(dict(kind="std", src="w_in", c0=2048, w=512, nk=8, nm="v0"))
    ch.append(dict(kind="std", src="w_in", c0=2560, w=512, nk=8, nm="v1"))
    ch.append(dict(kind="std", src="w_o", c0=0, w=512, nk=8, nm="wo0"))
    ch.append(dict(kind="std", src="w_o", c0=512, w=512, nk=8, nm="wo1"))
    for l in range(2):
        if l == 1:
            for i in range(4):
                ch.append(dict(kind="pw1", i=i, nk=8, w=512, nm="pw1_%d" % i))
            for c in range(8):
                ch.append(dict(kind="diag", c=c, nk=31, w=128, nm="dg%d" % c))
            ch.append(dict(kind="std", src="w_pw2", c0=0, w=512, nk=8, nm="pw2_0"))
            ch.append(dict(kind="std", src="w_pw2", c0=512, w=512, nk=8, nm="pw2_1"))
        for j in range(6):
            w = 512 if j < 5 else 256
            ch.append(dict(kind="std", src="gate%d" % l, c0=j * 512, w=w, nk=8, nm="g%d_%d" % (l, j)))
            ch.append(dict(kind="std", src="up%d" % l, c0=j * 512, w=w, nk=8, nm="u%d_%d" % (l, j)))
        for m in range(8):
            ch.append(dict(kind="std", src="down%d" % l, c0=m * 128, w=128, nk=22, nm="d%d_%d" % (l, m)))
    assert len(ch) == NCH, len(ch)
    return ch


def t5_bucket_np(d):
    d = np.asarray(d, dtype=np.int64)
    dm = np.maximum(d, 1).astype(np.float32)
    large = 16 + (np.log(dm / np.float32(16)) / np.float32(math.log(128 / 16)) * np.float32(16)).astype(np.int32)
    large = np.minimum(large, 31)
    return np.where(d < 16, d, large)


def host_consts():
    c = {}
    c["ident"] = np.eye(128, dtype=np.float32)
    j = np.arange(384)
    dist = np.maximum(255 - j, 0)
    b = t5_bucket_np(dist)
    oh = np.zeros((32, 384), np.float32)
    oh[b, j] = 1.0
    c["ohr"] = oh
    q = np.arange(128)[:, None]
    k = np.arange(128)[None, :]
    c["cbias"] = np.where(k <= q, 0.0, NEG).astype(np.float32)
    c["padb"] = np.broadcast_to(np.where(k >= 16, NEG, 0.0), (128, 128)).astype(np.float32).copy()
    c["pow2"] = np.broadcast_to((2.0 ** (1.0 - np.arange(NIT + 1)))[None, :], (128, NIT + 1)).astype(np.float32).copy()
    return c


def build_program(n_seq=2, debug=False, stop_after=None, info=None):
    nc = bass.Bass("TRN2", target_bir_lowering=False)
    es = ExitStack()
    with es:
        es.enter_context(nc.allow_non_contiguous_dma(reason="small param/layout loads"))
        es.enter_context(nc.allow_low_precision(reason="bf16 matmul operands by design"))

        def din(name, shape, dt=F32):
            return nc.dram_tensor(name, list(shape), dt, kind="ExternalInput").ap()

        x = din("x", [n_seq, SEQ, D])
        meta = din("meta_tokens", [16, D])
        relb = din("rel_bias", [32, 16])
        w_in = din("w_in_attn", [1, D, 3656])
        w_o = din("w_o_attn", [1, D, D])
        w_pw1 = din("w_pw1", [1, D, 2 * D])
        b_pw1 = din("b_pw1", [1, 2 * D])
        w_dw = din("w_dw", [1, 31, D])
        b_dw = din("b_dw", [1, D])
        cln_g = din("conv_ln_g", [1, D])
        cln_b = din("conv_ln_b", [1, D])
        w_pw2 = din("w_pw2", [1, D, D])
        b_pw2 = din("b_pw2", [1, D])
        ln1_g = din("ln1_g", [2, D])
        ln1_b = din("ln1_b", [2, D])
        w_gate = din("ffn_w_gate", [2, D, DFF])
        w_up = din("ffn_w_up", [2, D, DFF])
        w_down = din("ffn_w_down", [2, DFF, D])
        ln2_g = din("ln2_g", [2, D])
        ln2_b = din("ln2_b", [2, D])
        c_ident = din("ident", [128, 128])
        c_ohr = din("ohr", [32, 384])
        c_cbias = din("cbias", [128, 128])
        c_padb = din("padb", [128, 128])
        c_pow2 = din("pow2", [128, NIT + 1])
        out = nc.dram_tensor("out", [n_seq, SEQ, D], F32, kind="ExternalOutput").ap()
        wscr = nc.dram_tensor("wscr", [NCH, 128, WSZ], BF16, kind="Internal").ap()
        tbd = nc.dram_tensor("tbd", [16, 384], F32, kind="Internal").ap()
        ZH = 128 * 385
        ztd = nc.dram_tensor("ztd", [16, ZH], F32, kind="Internal").ap()
        dbg = None
        if debug:
            dbg = nc.dram_tensor("dbg", [8, 128, 2176], F32, kind="ExternalOutput").ap()

        wsrc = {"w_in": w_in[0], "w_o": w_o[0], "w_pw2": w_pw2[0],
                "gate0": w_gate[0], "gate1": w_gate[1], "up0": w_up[0], "up1": w_up[1],
                "down0": w_down[0], "down1": w_down[1]}
        chunks = chunk_table()
        cid = {c["nm"]: i for i, c in enumerate(chunks)}

        P = Prog(nc, es)
        V_, A_, S_, T_, Q_ = nc.vector, nc.scalar, nc.sync, nc.tensor, nc.gpsimd

        def sb(name, shape, dt):
            return es.enter_context(nc.sbuf_tensor("s_" + name, list(shape), dt))

        ident32 = sb("ident32", [128, 128], F32)
        identb = sb("identb", [128, 128], BF16)
        P.dma("sp", w=[("ident32",)], out=ident32[:], in_=c_ident[:, :])
        P.op("dve", V_.tensor_copy, r=[("ident32",)], w=[("identb",)], out=identb[:], in_=ident32[:])

        with ExitStack() as es2:
            NPB = 4
            stg = [es2.enter_context(nc.sbuf_tensor("stg%d" % i, [128, WSZ], F32)) for i in range(NPB)]
            cvt = [es2.enter_context(nc.sbuf_tensor("cvt%d" % i, [128, WSZ], BF16)) for i in range(NPB)]
            wT = es2.enter_context(nc.sbuf_tensor("wdwT", [128, 8, 31], F32))
            tbs = es2.enter_context(nc.sbuf_tensor("tbs_p", [16, 384], F32))
            rb_sb = es2.enter_context(nc.sbuf_tensor("rb_p", [32, 16], F32))
            oh_sb = es2.enter_context(nc.sbuf_tensor("oh_p", [32, 384], F32))
            pst = es2.enter_context(nc.psum_tensor("pst_p", [128, 512], F32))
            P.dma("act", w=[("rb_sb",)], out=rb_sb[:], in_=relb[:, :])
            P.dma("act", w=[("oh_sb",)], out=oh_sb[:], in_=c_ohr[:, :])
            P.op("pe", T_.matmul, r=[("rb_sb",), ("oh_sb",)], w=[("ps", 99)],
                 out=pst[0:16, 0:384], lhsT=rb_sb[:, :], rhs=oh_sb[:, :], start=True, stop=True)
            P.op("dve", V_.tensor_copy, r=[("ps", 99)], w=[("tbs0",)], out=tbs[:, :], in_=pst[0:16, 0:384])
            P.op("dve", V_.tensor_scalar, r=[("tbs0",)], w=[("tbs",)], out=tbs[:, :], in0=tbs[:, :],
                 scalar1=tbs[:, 0:1], scalar2=1.0 / ATT_SCALE, op0=ALU.subtract, op1=ALU.mult)
            P.dma("act", r=[("tbs",)], w=[("tbd",)], out=tbd[:, :], in_=tbs[:, :])
            for h in range(NH):
                P.dma("act", r=[("tbd",)], w=[("ztd", h)], out=bass.AP(ztd.tensor, h * ZH, [[385, 128], [1, 384]]),
                      in_=bass.AP(tbd.tensor, h * 384, [[0, 128], [1, 384]]))
            import os
            for c in range(8):
                if os.environ.get("K_NOWT"):
                    break
                src = bass.AP(w_dw.tensor, c * 128, [[1, 128], [D, 31]])
                P.dma("sp", w=[("wdwT", c)], out=wT[:, c, :], in_=src)
            for i, c in enumerate(chunks):
                if os.environ.get("K_PRE") and c["nm"] not in os.environ["K_PRE"].split(","):
                    continue
                b = i % NPB
                free = c["nk"] * c["w"]
                sv = stg[b][:, :free].rearrange("p (k n) -> p k n", n=c["w"])
                cv = cvt[b][:, :free]
                if c["kind"] == "std":
                    W = wsrc[c["src"]]
                    src = W.rearrange("(k p) n -> p k n", p=128)[:, :, c["c0"]:c["c0"] + c["w"]]
                    P.dma("sp", w=[("stg", b)], out=sv, in_=src)
                elif c["kind"] == "kiwi":
                    W = w_in[0].rearrange("(k p) n -> p k n", p=128)
                    P.dma("sp", w=[("stg", b)], out=sv[:, :, 0:64], in_=W[:, :, 3584:3648])
                    P.dma("sp", w=[("stg", b)], out=sv[:, :, 64:128], in_=W[:, :, 3584:3648])
                    P.dma("sp", w=[("stg", b)], out=sv[:, :, 128:136], in_=W[:, :, 3648:3656])
                elif c["kind"] == "pw1":
                    W = w_pw1[0].rearrange("(k p) n -> p k n", p=128)
                    ii = c["i"]
                    P.dma("sp", w=[("stg", b)], out=sv[:, :, 0:256], in_=W[:, :, ii * 256:(ii + 1) * 256])
                    P.dma("sp", w=[("stg", b)], out=sv[:, :, 256:512], in_=W[:, :, D + ii * 256:D + (ii + 1) * 256])
                if c["kind"] == "diag":
                    cc = c["c"]
                    cv3 = cv.rearrange("p (k n) -> p k n", n=128)
                    P.op("dve", V_.tensor_tensor, r=[("wdwT", cc), ("ident32",)], w=[("cvt", b, j) for j in range(31)],
                         out=cv3, in0=ident32[:, :].unsqueeze(1).to_broadcast([128, 31, 128]),
                         in1=wT[:, cc, :].unsqueeze(2).to_broadcast([128, 31, 128]), op=ALU.mult)
                    wkeys = [("cvt", b, j) for j in range(31)]
                    P.dma("act", r=wkeys, w=[("wscr", i)], out=wscr[i][:, :free], in_=cv)
                else:
                    wkeys = [("cvt", b, j) for j in range(31)]
                    if (i // 2) % 2 == 0:
                        P.op("dve", V_.tensor_copy, r=[("stg", b)], w=wkeys, out=cv, in_=stg[b][:, :free])
                    else:
                        P.op("act", A_.copy, r=[("stg", b)], w=wkeys, out=cv, in_=stg[b][:, :free])
                    P.dma("act", r=wkeys, w=[("wscr", i)], out=wscr[i][:, :free], in_=cv)
            P.flush()
        if stop_after == "prepass":
            P.final_wait()
            return nc

        KT = sb("KT", [128, KC, NSLOT], BF16)
        Vt = sb("Vt", [128, NKT, NH, 65], BF16)
        kiT2 = sb("kiT2", [128, NSLOT], BF16)
        hT32 = sb("hT32", [128, KC, G], F32)
        hT16 = sb("hT16", [128, KC, G], BF16)
        xnb = [sb("xn%d" % i, [128, D], F32) for i in range(2)]
        qz = [sb("qz%d" % i, [128, G], BF16) for i in range(4)]
        qiz = [sb("qiz%d" % i, [128, 128], BF16) for i in range(8)]
        wi = sb("wi", [128, NT, 8], F32)
        mixT = sb("mixT", [128, KC, G], BF16)
        wslot = [sb("ws%d" % i, [128, WSZ], BF16) for i in range(NS)]
        ARENA_BYTES = 58 * 1024
        arena = sb("arena", [128, ARENA_BYTES // 2], BF16)
        off = [0]

        def carve(nelem, dt, shape=None, base=None):
            nb = nelem * (4 if dt == F32 else 2)
            o = off[0] if base is None else base
            assert o % 4 == 0
            if base is None:
                off[0] += (nb + 3) // 4 * 4
            assert o + nb <= ARENA_BYTES, (o, nb)
            v = arena[:, o // 2:(o + nb) // 2]
            if dt == F32:
                v = v.bitcast(F32)
            return v

        score_f = carve(NT * NSLOT, F32)
        score = score_f.rearrange("p (t s) -> p t s", s=NSLOT)
        maskq_f = carve(NT * NSLOT, BF16)
        maskq = maskq_f.rearrange("p (t s) -> p t s", s=NSLOT)
        maskT = carve(NKT * G, BF16).rearrange("p (j g) -> p j g", g=G)
        qT = carve(KC * G, BF16).rearrange("p (k g) -> p k g", g=G)
        qiT = carve(4 * G, BF16).rearrange("p (k g) -> p k g", g=G)
        NRB = 5
        off_rb = off[0]
        Rb = [carve(512, BF16) for _ in range(NRB)]
        dgw = carve(NT * NIH * 128, BF16).rearrange("p (t h k) -> p t h k", h=NIH, k=128)
        NPT = 5
        Pt = [carve(512, BF16) for _ in range(NPT)]
        attn = carve(NT * D, BF16).rearrange("p (t d) -> p t d", d=D)
        endA = off[0]
        xin = carve(NT * D, F32, base=off_rb).rearrange("p (t d) -> p t d", d=D)
        off[0] = 0
        actT = carve(FC * G, BF16).rearrange("p (k g) -> p k g", g=G)
        sgt = [carve(G, F32) for _ in range(2)]
        uT = carve(KC * (30 + G), BF16).rearrange("p (k g) -> p k g", g=30 + G)
        yT32 = carve(KC * G, F32).rearrange("p (k g) -> p k g", g=G)
        tmpe = [carve(G, F32) for _ in range(2)]
        endB = off[0]
        A_NAMES = ["score", "maskq", "maskT", "qT", "qiT", "R", "Pt", "attn", "dgw"]
        B_NAMES = ["actT", "sgt", "uT", "yT32", "tmpe"]

        ucarry = sb("ucarry", [128, KC, 30], BF16)
        ucarry_meta = sb("ucarry_meta", [128, KC, 30], BF16)
        Asm = sb("Asm", [128, NT], F32)
        Amx = sb("Amx", [128, NT, 2], F32)
        wtab = sb("wtab", [128, NT, NIT + 1], F32)
        test = sb("test", [128, NT], F32)
        cnt = sb("cnt", [128, NT], F32)
        sg = sb("sg", [128, NT], F32)
        tmpb = sb("tmpb", [128, NT], F32)
        lo = sb("lo", [128, NT], F32)
        thr = sb("thr", [128, NT], F32)
        mhalf = sb("mhalf", [128, NT], F32)
        hb1g = sb("hb1g", [128, 8], F32)
        rec = [sb("rec%d" % i, [128, NT], F32) for i in range(2)]
        stats = sb("stats", [128, NT, 12], F32)
        mv = sb("mv", [128, NT, 2], F32)
        vpe = sb("vpe", [128, NT], F32)
        rstd = sb("rstd", [128, NT], F32)
        nmr = sb("nmr", [128, NT], F32)
        lnp = sb("lnp", [128, 8, 8], F32)
        cpar = sb("cpar", [128, 6, 8], F32)
        Bq = sb("Bq", [128, NH, 2, 128], BF16)
        Bqm = sb("Bqm", [128, NH, 128], BF16)
        cfar = sb("cfar", [128, NH], F32)
        cbias = sb("cbias", [128, 128], F32)
        padb = sb("padb", [128, 128], F32)
        pow2 = sb("pow2", [128, NIT + 1], F32)
        zerob = sb("zerob", [128, 256], BF16)
        psf = [es.enter_context(nc.psum_tensor("psf%d" % i, [128, 512], F32)) for i in range(6)]
        psb = [es.enter_context(nc.psum_tensor("psb%d" % i, [128, 1024], BF16)) for i in range(2)]
        bctr = [0, 0]

        def bank():
            i = bctr[0] % 4
            bctr[0] += 1
            return i

        b2ctr = [0]

        def bank2():
            i = 4 + b2ctr[0] % 2
            b2ctr[0] += 1
            return i

        def bankb():
            i = bctr[1] % 2
            bctr[1] += 1
            return i

        def ld_pp(dst, src_vec, key):
            P.dma("sp", w=[key], out=dst, in_=src_vec.rearrange("(c p) -> p c", p=128))

        for l in range(2):
            ld_pp(lnp[:, 4 * l + 0, :], ln1_g[l], ("lnp", 4 * l + 0))
            ld_pp(lnp[:, 4 * l + 1, :], ln1_b[l], ("lnp", 4 * l + 1))
            ld_pp(lnp[:, 4 * l + 2, :], ln2_g[l], ("lnp", 4 * l + 2))
            ld_pp(lnp[:, 4 * l + 3, :], ln2_b[l], ("lnp", 4 * l + 3))
        ld_pp(cpar[:, 0, :], cln_g[0], ("cpar", 0))
        ld_pp(cpar[:, 1, :], cln_b[0], ("cpar", 1))
        ld_pp(cpar[:, 2, :], b_dw[0], ("cpar", 2))
        ld_pp(cpar[:, 3, :], b_pw2[0], ("cpar", 3))
        ld_pp(cpar[:, 4, :], b_pw1[0][0:D], ("cpar", 4))
        ld_pp(cpar[:, 5, :], b_pw1[0][D:2 * D], ("cpar", 5))
        P.dma("sp", w=[("cbias",)], out=cbias[:], in_=c_cbias[:, :])
        P.dma("sp", w=[("padb",)], out=padb[:], in_=c_padb[:, :])
        P.dma("sp", w=[("pow2",)], out=pow2[:], in_=c_pow2[:, :])
        P.dma("sp", w=[("cfar",)], out=cfar[:], in_=bass.AP(relb.tensor, 31 * 16, [[0, 128], [1, 16]]))
        P.op("pool", Q_.memset, w=[("zerob",)], ap=zerob[:], constant=0.0)
        P.op("dve", V_.tensor_scalar, r=[("cpar", 5)], w=[("hb1g",)], out=hb1g[:, :], in0=cpar[:, 5, :], scalar1=0.5, scalar2=None,
             op0=ALU.mult)
        P.op("pool", Q_.memset, w=[("mhalf",)], ap=mhalf[:], constant=-0.5)
        for i in range(4):
            P.op("pool", Q_.memset, w=[("qz", i)], ap=qz[i][:], constant=0.0)
        for i in range(8):
            P.op("pool", Q_.memset, w=[("qiz", i)], ap=qiz[i][:], constant=0.0)
        P.op("pool", Q_.memset, w=[("Vones",)], ap=Vt[:, :, :, 64:65], constant=1.0)
        stgs = [xnb[0][:, :].rearrange("p (h k) -> p h k", k=128), xnb[1][:, :].rearrange("p (h k) -> p h k", k=128),
                hT32[:, :, 0:128], hT32[:, :, 128:256],
                score_f[:, 0:1024].rearrange("p (h k) -> p h k", k=128), score_f[:, 1024:2048].rearrange("p (h k) -> p h k", k=128)]
        for dlt in range(3):
            for hh in range(2):
                si_ = dlt * 2 + hh
                bst = stgs[si_]
                for h8 in range(8):
                    h = hh * 8 + h8
                    base = {0: 255, 1: 127, 2: 239}[dlt]
                    src = bass.AP(ztd.tensor, h * ZH + base, [[384, 128], [1, 128]])
                    P.dma("sp" if h8 % 2 == 0 else "act", w=[("stg6", si_)], out=bst[:, h8, :], in_=src)
                if dlt < 2:
                    P.op("dve", V_.tensor_copy, r=[("stg6", si_)], w=[("Bq",)], out=Bq[:, hh * 8:(hh + 1) * 8, dlt, :], in_=bst)
                else:
                    P.op("dve", V_.tensor_copy, r=[("stg6", si_)], w=[("Bqm",)], out=Bqm[:, hh * 8:(hh + 1) * 8, :], in_=bst)
        P.flush()
        if stop_after == "setup":
            P.final_wait()
            return nc

        passes = [("meta", 0, 0)] + [("x", b, gi) for b in range(n_seq) for gi in range(NGRP)]
        n_meta_chunks = cid["pw1_3"] + 1
        order = []
        for ps_ in passes:
            n = n_meta_chunks if ps_[0] == "meta" else NCH
            order.extend(range(n))
        wst = dict(issued=0, pos=0)

        def wissue(i):
            ch = order[i]
            c = chunks[ch]
            free = c["nk"] * c["w"]
            s = i % NS
            P.dma("sp", r=[("wscr", ch)], w=[("ws", s)], out=wslot[s][:, :free], in_=wscr[ch][:, :free])

        def wnext(expect, issue=True):
            while issue and wst["issued"] < min(len(order), wst["pos"] + NS):
                wissue(wst["issued"])
                wst["issued"] += 1
            i = wst["pos"]
            ch = order[i]
            assert chunks[ch]["nm"] == expect, (chunks[ch]["nm"], expect)
            c = chunks[ch]
            s = i % NS
            wst["pos"] += 1
            return s, wslot[s][:, :c["nk"] * c["w"]].rearrange("p (k n) -> p k n", n=c["w"])

        def linear_fm(wv, s, nm_, rhs_fn, nk, Gg, evac, rkeys):
            for mi in range(nm_):
                b = bank()
                for k in range(nk):
                    P.op("pe", T_.matmul, r=[("ws", s)] + rkeys, w=[("ps", b)],
                         out=psf[b][:, 0:Gg], lhsT=wv[:, k, mi * 128:(mi + 1) * 128], rhs=rhs_fn(k),
                         start=(k == 0), stop=(k == nk - 1))
                evac(mi, b)

        def layer_norm(src, srcname, gi_, bi_, gtab, mode, Gg, NTg, out_rows=None):
            banks = []
            for tt in range(NTg):
                bA, bB = bank(), bank()
                banks.append((bA, bB))
                for kc in range(KC):
                    b = bA if kc < 4 else bB
                    P.op("pe", T_.transpose, r=[(srcname, kc), ("ident32",)], w=[("ps", b)],
                         out=psf[b][:, (kc % 4) * 128:(kc % 4 + 1) * 128], in_=src[:, kc, tt * 128:(tt + 1) * 128],
                         identity=ident32[:])
                P.op("dve", V_.bn_stats, r=[("ps", bA)], w=[("stats", tt, 0)], out=stats[:, tt, 0:6], in_=psf[bA][:, :])
                P.op("dve", V_.bn_stats, r=[("ps", bB)], w=[("stats", tt, 1)], out=stats[:, tt, 6:12], in_=psf[bB][:, :])
                P.op("dve", V_.bn_aggr, r=[("stats", tt, 0), ("stats", tt, 1)], w=[("mv", tt)], out=mv[:, tt, :], in_=stats[:, tt, :])
            mvk = [("mv", tt) for tt in range(NTg)]
            P.op("dve", V_.tensor_scalar, r=mvk, w=[("vpe",)], out=vpe[:, 0:NTg], in0=mv[:, 0:NTg, 1],
                 scalar1=LN_EPS, scalar2=None, op0=ALU.add)
            if USE_POOL_POW:
                P.op("pool", Q_.tensor_tensor, r=[("vpe",), ("mhalf",)], w=[("rstd",)], out=rstd[:, 0:NTg], in0=vpe[:, 0:NTg],
                     in1=mhalf[:, 0:NTg], op=ALU.pow)
            else:
                P.op("act", A_.activation, r=[("vpe",)], w=[("vpe2",)], out=vpe[:, 0:NTg], in_=vpe[:, 0:NTg], func=AF.Sqrt)
                P.op("dve", V_.reciprocal, r=[("vpe2",)], w=[("rstd",)], out=rstd[:, 0:NTg], in_=vpe[:, 0:NTg])
            P.op("dve", V_.scalar_tensor_tensor, r=mvk + [("rstd",)], w=[("nmr",)], out=nmr[:, 0:NTg], in0=mv[:, 0:NTg, 0],
                 scalar=-1.0, in1=rstd[:, 0:NTg], op0=ALU.mult, op1=ALU.mult)
            for tt in range(NTg):
                bA, bB = banks[tt]
                xn = xnb[tt % 2]
                xi = tt % 2
                P.op("act", A_.activation, r=[("ps", bA), ("rstd",), ("nmr",)], w=[("xn", xi, 0)],
                     out=xn[:, 0:512], in_=psf[bA][:, :], func=AF.Identity, scale=rstd[:, tt:tt + 1], bias=nmr[:, tt:tt + 1])
                P.op("dve", V_.tensor_scalar, r=[("ps", bB), ("rstd",), ("nmr",)], w=[("xn", xi, 1)],
                     out=xn[:, 512:1024], in0=psf[bB][:, :], scalar1=rstd[:, tt:tt + 1], scalar2=nmr[:, tt:tt + 1],
                     op0=ALU.mult, op1=ALU.add)
            cbanks = []
            for tt in range(NTg):
                xn = xnb[tt % 2]
                xi = tt % 2
                cA, cB = (bank2(), bank2()) if tt % 2 == 0 else (bank(), bank())
                cbanks.append((cA, cB))
                for kc in range(KC):
                    b = cA if kc < 4 else cB
                    P.op("pe", T_.transpose, r=[("xn", xi, kc // 4), ("ident32",)], w=[("ps", b)],
                         out=psf[b][:, (kc % 4) * 128:(kc % 4 + 1) * 128], in_=xn[:, kc * 128:(kc + 1) * 128],
                         identity=ident32[:])
            if mode == "conv":
                for tt in range(NTg):
                    cA, cB = cbanks[tt]
                    for kc in range(KC):
                        b = cA if kc < 4 else cB
                        pv = psf[b][:, (kc % 4) * 128:(kc % 4 + 1) * 128]
                        P.op("act", A_.activation, r=[("ps", b), ("cpar", 0), ("cpar", 1)], w=[("mixT", kc)],
                             out=mixT[:, kc, tt * 128:(tt + 1) * 128], in_=pv, func=AF.Silu,
                             scale=cpar[:, 0, kc:kc + 1], bias=cpar[:, 1, kc:kc + 1])
            else:
                for (dst, dkey) in (((hT16, "h16"), (hT32, "h32")) if mode == "main" else ((hT32, "h32"),)):
                    for tt in range(NTg):
                        cA, cB = cbanks[tt]
                        for kc in range(KC):
                            b = cA if kc < 4 else cB
                            pv = psf[b][:, (kc % 4) * 128:(kc % 4 + 1) * 128]
                            if kc < 4:
                                P.op("dve", V_.tensor_scalar, r=[("ps", b), ("lnp", gi_), ("lnp", bi_)], w=[(dkey, kc)],
                                     out=dst[:, kc, tt * 128:(tt + 1) * 128], in0=pv, scalar1=gtab[:, gi_, kc:kc + 1],
                                     scalar2=gtab[:, bi_, kc:kc + 1], op0=ALU.mult, op1=ALU.add)
                            else:
                                P.op("act", A_.activation, r=[("ps", b), ("lnp", gi_), ("lnp", bi_)], w=[(dkey, kc)],
                                     out=dst[:, kc, tt * 128:(tt + 1) * 128], in_=pv, func=AF.Identity, scale=gtab[:, gi_, kc:kc + 1],
                                     bias=gtab[:, bi_, kc:kc + 1])
            if mode == "final":
                for tt in range(NTg):
                    xn = xnb[tt % 2]
                    xi = tt % 2
                    dA, dB = bank(), bank()
                    for kc in range(KC):
                        b = dA if kc < 4 else dB
                        P.op("pe", T_.transpose, r=[("h32", kc), ("ident32",)], w=[("ps", b)],
                             out=psf[b][:, (kc % 4) * 128:(kc % 4 + 1) * 128], in_=hT32[:, kc, tt * 128:(tt + 1) * 128],
                             identity=ident32[:])
                    P.op("act", A_.copy, r=[("ps", dA)], w=[("xn", xi, 0)], out=xn[:, 0:512], in_=psf[dA][:, :])
                    P.op("dve", V_.tensor_copy, r=[("ps", dB)], w=[("xn", xi, 1)], out=xn[:, 512:1024], in_=psf[dB][:, :])
                    P.dma("sp", r=[("xn", xi, 0), ("xn", xi, 1)], w=[("out",)], out=out_rows(tt), in_=xn[:, :])

        def ffn(l, Gg, NTg):
            P.fence(A_NAMES, B_NAMES)
            for j in range(6):
                wcols = 512 if j < 5 else 256
                sgi, wg = wnext("g%d_%d" % (l, j))
                sui, wu = wnext("u%d_%d" % (l, j), issue=False)
                for mi in range(wcols // 128):
                    m = j * 4 + mi
                    bg, bu = bank(), bank()
                    for k in range(KC):
                        P.op("pe", T_.matmul, r=[("ws", sgi), ("h16", k)], w=[("ps", bg)], out=psf[bg][:, 0:Gg],
                             lhsT=wg[:, k, mi * 128:(mi + 1) * 128], rhs=hT16[:, k, 0:Gg], start=(k == 0), stop=(k == KC - 1))
                    for k in range(KC):
                        P.op("pe", T_.matmul, r=[("ws", sui), ("h16", k)], w=[("ps", bu)], out=psf[bu][:, 0:Gg],
                             lhsT=wu[:, k, mi * 128:(mi + 1) * 128], rhs=hT16[:, k, 0:Gg], start=(k == 0), stop=(k == KC - 1))
                    si = m % 2
                    P.op("act", A_.activation, r=[("ps", bg)], w=[("sgt", si)], out=sgt[si][:, 0:Gg], in_=psf[bg][:, 0:Gg], func=AF.Silu)
                    P.op("dve", V_.tensor_tensor, r=[("sgt", si), ("ps", bu)], w=[("actT", m)], out=actT[:, m, 0:Gg],
                         in0=sgt[si][:, 0:Gg], in1=psf[bu][:, 0:Gg], op=ALU.mult)
            for m in range(8):
                s, wd = wnext("d%d_%d" % (l, m))
                b = bank()
                for k in range(FC):
                    P.op("pe", T_.matmul, r=[("ws", s), ("actT", k)], w=[("ps", b)], out=psf[b][:, 0:Gg],
                         lhsT=wd[:, k, :], rhs=actT[:, k, 0:Gg], start=(k == 0), stop=(k == FC - 1))
                P.op("dve", V_.scalar_tensor_tensor, r=[("ps", b), ("h32", m)], w=[("h32", m)], out=hT32[:, m, 0:Gg],
                     in0=hT32[:, m, 0:Gg], scalar=ALPHA, in1=psf[b][:, 0:Gg], op0=ALU.mult, op1=ALU.add)

        def prefetch_x(pidx):
            if pidx >= len(passes):
                return
            kind_, b_, g_ = passes[pidx]
            if kind_ == "meta":
                return
            if isinstance(stop_after, int) and pidx >= stop_after:
                return
            P.fence(["R", "dgw", "Pt"], ["xin"])
            for tt in range(NT):
                r0 = g_ * G + tt * 128
                P.dma("sp", w=[("xin", tt)], out=xin[:, tt, :], in_=x[b_, r0:r0 + 128, :])

        def emit_group(kind, bsel, gi, pidx=0):
            is_meta = kind == "meta"
            NTg = 1 if is_meta else NT
            Gg = NTg * 128
            T0 = -1 if is_meta else gi * NT
            slot0 = (T0 + 1) * 128
            P.fence(B_NAMES, A_NAMES)
            for tt in range(NTg):
                xi = tt % 2
                xn = xnb[xi]
                if is_meta:
                    P.op("pool", Q_.memset, w=[("xn", xi, 0), ("xn", xi, 1)], ap=xn[:, :], constant=0.0)
                    P.dma("sp", w=[("xn", xi, 0), ("xn", xi, 1)], out=xn[0:16, :], in_=meta[:, :])
                if os.environ.get("K_STOPPH") == "A0":
                    return
                bA, bB = bank(), bank()
                for kc in range(KC):
                    b = bA if kc < 4 else bB
                    if is_meta:
                        P.op("pe", T_.transpose, r=[("xn", xi, kc // 4), ("ident32",)], w=[("ps", b)],
                             out=psf[b][:, (kc % 4) * 128:(kc % 4 + 1) * 128], in_=xn[:, kc * 128:(kc + 1) * 128], identity=ident32[:])
                    else:
                        P.op("pe", T_.transpose, r=[("xin", tt), ("ident32",)], w=[("ps", b)],
                             out=psf[b][:, (kc % 4) * 128:(kc % 4 + 1) * 128], in_=xin[:, tt, kc * 128:(kc + 1) * 128], identity=ident32[:])
                if os.environ.get("K_STOPPH") == "A1":
                    return
                for hf, b in ((0, bA), (1, bB)):
                    pv = psf[b][:, :].rearrange("p (k n) -> p k n", n=128)
                    hk = [("h32", kc) for kc in range(hf * 4, hf * 4 + 4)]
                    hk16 = [("h16", kc) for kc in range(hf * 4, hf * 4 + 4)]
                    if os.environ.get("K_X") != "noact":
                        P.op("act", A_.copy, r=[("ps", b)], w=hk, out=hT32[:, hf * 4:hf * 4 + 4, tt * 128:(tt + 1) * 128], in_=pv)
                    if os.environ.get("K_X") != "nodve":
                        P.op("dve", V_.tensor_copy, r=[("ps", b)] + (hk if os.environ.get("K_X") == "ser" else []), w=hk16, out=hT16[:, hf * 4:hf * 4 + 4, tt * 128:(tt + 1) * 128], in_=pv)
            if os.environ.get("K_STOPPH") == "A":
                return
            if not is_meta:
                P.fence(["xin"], ["R", "dgw", "Pt"])
            hkeys = [("h16", k) for k in range(KC)]
            rh = lambda k: hT16[:, k, 0:Gg]
            s, wv = wnext("qi")
            def ev(mi, b):
                P.op("act", A_.copy, r=[("ps", b)], w=[("qiT", mi)], out=qiT[:, mi, 0:Gg], in_=psf[b][:, 0:Gg])
            linear_fm(wv, s, 4, rh, KC, Gg, ev, hkeys)
            s, wv = wnext("kiwi")
            def ev(mi, b):
                P.op("dve", V_.tensor_copy, r=[("ps", b)], w=[("kiT2", T0 + 1 + t_) for t_ in range(NTg)],
                     out=kiT2[:, slot0:slot0 + Gg], in_=psf[b][:, 0:Gg])
            linear_fm(wv, s, 1, rh, KC, Gg, ev, hkeys)
            for tt in range(NTg):
                b = bank()
                for k in range(KC):
                    P.op("pe", T_.matmul, r=[("ws", s), ("h16", k)], w=[("ps", b)], out=psf[b][:, 0:8],
                         lhsT=hT16[:, k, tt * 128:(tt + 1) * 128], rhs=wv[:, k, 128:136], start=(k == 0), stop=(k == KC - 1))
                P.op("dve", V_.tensor_scalar, r=[("ps", b)], w=[("wi", tt)], out=wi[:, tt, :], in0=psf[b][:, 0:8],
                     scalar1=WI_SCALE, scalar2=None, op0=ALU.mult)
            for qh in range(2):
                s, wv = wnext("q%d" % qh)
                def ev(mi, b, qh=qh):
                    m = qh * 4 + mi
                    P.op("act", A_.copy, r=[("ps", b)], w=[("qT", m)], out=qT[:, m, 0:Gg], in_=psf[b][:, 0:Gg])
                linear_fm(wv, s, 4, rh, KC, Gg, ev, hkeys)
            for kh in range(2):
                s, wv = wnext("k%d" % kh)
                def ev(mi, b, kh=kh):
                    m = kh * 4 + mi
                    P.op("act", A_.copy, r=[("ps", b)], w=[("KT", m, T0 + 1 + t_) for t_ in range(NTg)],
                         out=KT[:, m, slot0:slot0 + Gg], in_=psf[b][:, 0:Gg])
                linear_fm(wv, s, 4, rh, KC, Gg, ev, hkeys)
            for vh in range(2):
                s, wv = wnext("v%d" % vh)
                for tt in range(NTg):
                    b = bank()
                    for k in range(KC):
                        P.op("pe", T_.matmul, r=[("ws", s), ("h16", k)], w=[("ps", b)], out=psf[b][:, :],
                             lhsT=hT16[:, k, tt * 128:(tt + 1) * 128], rhs=wv[:, k, :], start=(k == 0), stop=(k == KC - 1))
                    jt = T0 + 1 + tt
                    pv = psf[b][:, :].rearrange("p (h d) -> p h d", d=64)
                    P.op("act", A_.copy, r=[("ps", b)], w=[("V", jt, vh)], out=Vt[:, jt, vh * 8:(vh + 1) * 8, 0:64], in_=pv)
            if os.environ.get("K_STOPPH") == "B":
                return
            Ss = [(T0 + tt + 2) * 128 for tt in range(NTg)]
            for tt in range(NTg):
                for hh in range(NIH):
                    half = hh % 2
                    P.op("dve", V_.tensor_copy, r=[("qiT", hh // 2)], w=[("qiz", hh)],
                         out=qiz[hh][half * 64:(half + 1) * 64, :], in_=qiT[half * 64:(half + 1) * 64, hh // 2, tt * 128:(tt + 1) * 128])
                    P.op("act", A_.activation, r=[("wi", tt), ("identb",)], w=[("dgw", tt, hh)], out=dgw[:, tt, hh, :], in_=identb[:, :],
                         func=AF.Copy, scale=wi[:, tt, hh:hh + 1])
                S = Ss[tt]
                nkb = (S + 511) // 512
                iitems = [(kb, hh) for kb in range(nkb) for hh in range(NIH)]
                accb = {}
                rinfo = {}

                def idx_l(i, tt=tt, S=S):
                    kb, hh = iitems[i]
                    c0, c1 = kb * 512, min(S, kb * 512 + 512)
                    b = bank()
                    kkeys = [("kiT2", j) for j in range(c0 // 128, c1 // 128)]
                    P.op("pe", T_.matmul, r=[("qiz", hh)] + kkeys, w=[("ps", b)], out=psf[b][:, 0:c1 - c0],
                         lhsT=qiz[hh][:, :], rhs=kiT2[:, c0:c1], start=True, stop=True)
                    ri = i % NRB
                    if i % 2 == 0:
                        P.op("act", A_.activation, r=[("ps", b)], w=[("R", ri)], out=Rb[ri][:, 0:c1 - c0],
                             in_=psf[b][:, 0:c1 - c0], func=AF.Relu)
                    else:
                        P.op("dve", V_.tensor_scalar, r=[("ps", b)], w=[("R", ri)], out=Rb[ri][:, 0:c1 - c0],
                             in0=psf[b][:, 0:c1 - c0], scalar1=0.0, scalar2=None, op0=ALU.max)
                    rinfo[i] = ri

                def idx_a(i, tt=tt, S=S):
                    kb, hh = iitems[i]
                    c0, c1 = kb * 512, min(S, kb * 512 + 512)
                    if hh == 0:
                        accb[kb] = bank2()
                    ab = accb[kb]
                    ri = rinfo[i]
                    P.op("pe", T_.matmul, r=[("R", ri), ("dgw", tt, hh)], w=[("ps", ab)], out=psf[ab][:, 0:c1 - c0],
                         lhsT=dgw[:, tt, hh, :], rhs=Rb[ri][:, 0:c1 - c0], start=(hh == 0), stop=(hh == NIH - 1))
                    if hh == NIH - 1:
                        P.op("act", A_.copy, r=[("ps", ab)], w=[("score", tt, kb)], out=score[:, tt, c0:c1], in_=psf[ab][:, 0:c1 - c0])

                ISK = 3
                for i in range(len(iitems) + ISK):
                    if i < len(iitems):
                        idx_l(i)
                    if i - ISK >= 0:
                        idx_a(i - ISK)
                skeys = [("score", tt, kb) for kb in range(nkb)]
                P.op("dve", V_.tensor_scalar, r=skeys, w=[("maskq", tt), ("Amx", tt)], out=maskq[:, tt, 0:S], in0=score[:, tt, 0:S],
                     scalar1=1.0, scalar2=None, op0=ALU.mult, op1=ALU.max, accum_out=Amx[:, tt, 0:1])
                P.op("dve", V_.tensor_scalar, r=skeys, w=[("maskq", tt), ("Amn", tt)], out=maskq[:, tt, 0:S], in0=score[:, tt, 0:S],
                     scalar1=-1.0, scalar2=None, op0=ALU.mult, op1=ALU.max, accum_out=Amx[:, tt, 1:2])
                P.op("dve", V_.tensor_tensor, r=[("Amx", tt), ("Amn", tt)], w=[("Asm", tt)], out=Asm[:, tt:tt + 1], in0=Amx[:, tt, 0:1],
                     in1=Amx[:, tt, 1:2], op=ALU.max)
                P.op("pool", Q_.tensor_tensor, r=[("score", tt, 0), ("padb",)], w=[("score", tt, 0)], out=score[:, tt, 0:128],
                     in0=score[:, tt, 0:128], in1=padb[:, :], op=ALU.add)
                kbd = (S - 128) // 512
                P.op("pool", Q_.tensor_tensor, r=[("score", tt, kbd), ("cbias",)], w=[("score", tt, kbd)], out=score[:, tt, S - 128:S],
                     in0=score[:, tt, S - 128:S], in1=cbias[:, :], op=ALU.add)
                P.op("dve", V_.tensor_scalar, r=[("Asm", tt), ("pow2",)], w=[("wtab", tt)], out=wtab[:, tt, :], in0=pow2[:, :],
                     scalar1=Asm[:, tt:tt + 1], scalar2=None, op0=ALU.mult)
            allsk = [[("score", tt, kb) for kb in range((Ss[tt] + 511) // 512)] for tt in range(NTg)]
            split = (NTg == 2)
            P.op("dve", V_.memset, w=[("test",)], ap=test[:, :], constant=0.0)
            P.op("dve", V_.memset, w=[("thr",)], ap=thr[:, 0:1], constant=255.5)
            if split:
                P.op("dve", V_.memset, w=[("thr",)], ap=thr[:, 1:2], constant=float(511 - Ss[1]))
                P.op("dve", V_.tensor_scalar, r=[("wtab", 1)], w=[("wtab", 1)], out=wtab[:, 1, :], in0=wtab[:, 1, :], scalar1=-1.0,
                     scalar2=None, op0=ALU.mult)
            wk = [("wtab", tt) for tt in range(NTg)]
            for it in range(1, NIT + 1):
                for tt in range(NTg):
                    S = Ss[tt]
                    if split and tt == 1:
                        P.op("act", A_.activation, r=allsk[tt] + [("test",)], w=[("maskq", tt), ("cnt", tt)], out=maskq[:, tt, 0:S],
                             in_=score[:, tt, 0:S], func=AF.Sign, bias=test[:, 1:2], scale=1.0, accum_out=cnt[:, 1:2])
                    else:
                        P.op("dve", V_.tensor_scalar, r=allsk[tt] + [("test",)], w=[("maskq", tt), ("cnt", tt)], out=maskq[:, tt, 0:S],
                             in0=score[:, tt, 0:S], scalar1=test[:, tt:tt + 1], scalar2=None, op0=ALU.is_ge, op1=ALU.add,
                             accum_out=cnt[:, tt:tt + 1])
                ck = [("cnt", tt) for tt in range(NTg)]
                sub = 0.5 if it < NIT else 1.0
                dst, dkey = (test, "test") if it < NIT else (lo, "lo")
                P.op("dve", V_.tensor_tensor, r=ck + [("thr",)], w=[("sg",)], out=sg[:, 0:NTg], in0=cnt[:, 0:NTg], in1=thr[:, 0:NTg],
                     op=ALU.is_ge)
                P.op("dve", V_.scalar_tensor_tensor, r=[("sg",)] + wk, w=[("tmpb",)], out=tmpb[:, 0:NTg], in0=sg[:, 0:NTg], scalar=sub,
                     in1=wtab[:, 0:NTg, it], op0=ALU.subtract, op1=ALU.mult)
                P.op("dve", V_.tensor_tensor, r=[("tmpb",), ("test",)], w=[(dkey,)], out=dst[:, 0:NTg], in0=test[:, 0:NTg],
                     in1=tmpb[:, 0:NTg], op=ALU.add)
            if split:
                P.op("dve", V_.tensor_scalar, r=[("lo",)], w=[("lo",)], out=lo[:, 1:2], in0=lo[:, 1:2], scalar1=-1.0, scalar2=None,
                     op0=ALU.mult)
            for tt in range(NTg):
                S = Ss[tt]
                P.op("dve", V_.tensor_scalar, r=allsk[tt] + [("lo",)], w=[("maskq", tt)], out=maskq[:, tt, 0:S], in0=score[:, tt, 0:S],
                     scalar1=lo[:, tt:tt + 1], scalar2=None, op0=ALU.is_ge)
            if os.environ.get("K_STOPPH") == "C1":
                return
            P.op("act", A_.preload_act_table, func=AF.Exp)
            njt = T0 + NTg + 1
            for j in range(njt):
                tmin = max(0, j - 1 - T0)
                bb = bankb()
                for tt in range(tmin, NTg):
                    P.op("pe", T_.transpose, r=[("maskq", tt), ("identb",)], w=[("psb", bb)], out=psb[bb][:, tt * 128:(tt + 1) * 128],
                         in_=maskq[:, tt, j * 128:(j + 1) * 128], identity=identb[:])
                P.op("dve", V_.tensor_copy, r=[("psb", bb)], w=[("maskT", j)], out=maskT[:, j, tmin * 128:Gg], in_=psb[bb][:, tmin * 128:Gg])
            if debug and (not is_meta) and bsel == 0 and gi == int(os.environ.get("K_DBG_GI", "0")):
                P.dma("sp", r=allsk[0], w=[("dbg", 0)], out=dbg[0][:, 0:Ss[0]], in_=score[:, 0, 0:Ss[0]])
                P.dma("sp", r=[("lo",)], w=[("dbg", 1)], out=dbg[1][:, 0:NTg], in_=lo[:, 0:NTg])
                P.dma("sp", r=[("cnt", 0)], w=[("dbg", 1, 1)], out=dbg[1][:, 8:8 + NTg], in_=cnt[:, 0:NTg])
            if os.environ.get("K_STOPPH") == "C":
                return
            nfull = T0 + 2
            units = []
            j = 0
            while j < nfull:
                if j + 1 < nfull and Gg == G and 2 * Gg <= 512:
                    units.append([j, j + 1])
                    j += 2
                else:
                    units.append([j])
                    j += 1
            for j in range(nfull, njt):
                units.append([j])
            items = [(h, ui) for h in range(NH) for ui in range(len(units))]
            st_info = {}
            head_bo = {}

            def emit_st(i):
                h, ui = items[i]
                unit = units[ui]
                c = h // 2
                half = h % 2
                zi = half * 2 + (c % 2)
                if ui == 0:
                    for hn in ([0, 1] if h == 0 else [h + 1]):
                        if hn < NH:
                            cn, hfn = hn // 2, hn % 2
                            zn = hfn * 2 + (cn % 2)
                            P.op("dve", V_.tensor_copy, r=[("qT", cn)], w=[("qz", zn)], out=qz[zn][hfn * 64:(hfn + 1) * 64, 0:Gg],
                                 in_=qT[hfn * 64:(hfn + 1) * 64, cn, 0:Gg])
                bs = bank()
                pi = i % NPT
                lo_col, hi_col = None, None
                for k_, j in enumerate(unit):
                    tmin = max(0, j - 1 - T0)
                    c0 = tmin * 128
                    base = k_ * Gg
                    nb = []
                    for tt in range(tmin, NTg):
                        T = T0 + tt
                        if j == 0:
                            if T == -1:
                                nb.append((tt, Bq[:, h, 0, :]))
                            elif T == 0:
                                nb.append((tt, Bqm[:, h, :]))
                        else:
                            dl = T + 1 - j
                            if dl in (0, 1):
                                nb.append((tt, Bq[:, h, dl, :]))
                    P.op("pe", T_.matmul, r=[("KT", c, j), ("qz", zi)], w=[("ps", bs)], out=psf[bs][:, base + c0:base + Gg],
                         lhsT=KT[:, c, j * 128:(j + 1) * 128], rhs=qz[zi][:, c0:Gg], start=True, stop=(len(nb) == 0))
                    for ii, (tt, bq) in enumerate(nb):
                        P.op("pe", T_.matmul, r=[("Bq",), ("Bqm",), ("identb",)], w=[("ps", bs)],
                             out=psf[bs][:, base + tt * 128:base + (tt + 1) * 128],
                             lhsT=bq, rhs=identb[:, :], start=False, stop=(ii == len(nb) - 1))
                    if lo_col is None:
                        lo_col = base + c0
                    hi_col = base + Gg
                P.op("act", A_.activation, r=[("ps", bs), ("cfar",)], w=[("Pt", pi)], out=Pt[pi][:, lo_col:hi_col], in_=psf[bs][:, lo_col:hi_col],
                     func=AF.Exp, scale=ATT_SCALE, bias=cfar[:, h:h + 1])
                if len(unit) == 2:
                    mview = maskT[:, unit[0]:unit[0] + 2, :].rearrange("p j g -> p (j g)")
                else:
                    mview = maskT[:, unit[0], lo_col:hi_col]
                me = "pool" if i % 3 == 0 else "dve"
                ME = Q_ if me == "pool" else V_
                P.op(me, ME.tensor_tensor, r=[("Pt", pi)] + [("maskT", j) for j in unit], w=[("Pt", pi)], out=Pt[pi][:, lo_col:hi_col],
                     in0=Pt[pi][:, lo_col:hi_col], in1=mview, op=ALU.mult)
                st_info[i] = pi

            def emit_pv(i):
                h, ui = items[i]
                unit = units[ui]
                pi = st_info[i]
                if ui == 0:
                    bo = bank2()
                    head_bo[h] = bo
                    P.op("pe", T_.matmul, r=[("zerob",)], w=[("ps", bo)], out=psf[bo][:, 0:NTg * 65], lhsT=zerob[:, 0:128],
                         rhs=zerob[:, 0:NTg * 65], start=True, stop=False)
                bo = head_bo[h]
                for k_, j in enumerate(unit):
                    tmin = max(0, j - 1 - T0)
                    base = k_ * Gg
                    for tt in range(tmin, NTg):
                        last = (ui == len(units) - 1 and k_ == len(unit) - 1 and tt == NTg - 1)
                        P.op("pe", T_.matmul, r=[("Pt", pi), ("V", j, h // 8), ("Vones",)], w=[("ps", bo)],
                             out=psf[bo][:, tt * 65:(tt + 1) * 65], lhsT=Pt[pi][:, base + tt * 128:base + (tt + 1) * 128], rhs=Vt[:, j, h, :],
                             start=False, stop=last)
                if ui == len(units) - 1:
                    ov = psf[bo][:, 0:NTg * 65].rearrange("p (t d) -> p t d", d=65)
                    ri = h % 2
                    P.op("dve", V_.reciprocal, r=[("ps", bo)], w=[("rec", ri)], out=rec[ri][:, 0:NTg], in_=ov[:, :, 64])
                    for tt in range(NTg):
                        P.op("dve", V_.tensor_scalar, r=[("ps", bo), ("rec", ri)], w=[("attn", tt, h // 2)],
                             out=attn[:, tt, h * 64:(h + 1) * 64], in0=ov[:, tt, 0:64], scalar1=rec[ri][:, tt:tt + 1], scalar2=None,
                             op0=ALU.mult)

            SK = 4
            for i in range(len(items) + SK):
                if i < len(items):
                    emit_st(i)
                if i - SK >= 0:
                    emit_pv(i - SK)
            if os.environ.get("K_STOPPH") == "D":
                return
            prefetch_x(pidx + 1)
            P.op("act", A_.preload_act_table, func=AF.Silu)
            for tt in range(NTg):
                bb = bankb()
                for kc in range(KC):
                    P.op("pe", T_.transpose, r=[("attn", tt, kc), ("identb",)], w=[("psb", bb)], out=psb[bb][:, kc * 128:(kc + 1) * 128],
                         in_=attn[:, tt, kc * 128:(kc + 1) * 128], identity=identb[:])
                P.op("act", A_.copy, r=[("psb", bb)], w=[("mixT", kc) for kc in range(KC)], out=mixT[:, :, tt * 128:(tt + 1) * 128],
                     in_=psb[bb][:, :].rearrange("p (k n) -> p k n", n=128))
            mkeys = [("mixT", k) for k in range(KC)]
            rm = lambda k: mixT[:, k, 0:Gg]
            for oh_ in range(2):
                s, wv = wnext("wo%d" % oh_)
                def ev(mi, b, oh_=oh_):
                    m = oh_ * 4 + mi
                    P.op("dve", V_.scalar_tensor_tensor, r=[("ps", b), ("h32", m)], w=[("h32", m)], out=hT32[:, m, 0:Gg],
                         in0=hT32[:, m, 0:Gg], scalar=ALPHA, in1=psf[b][:, 0:Gg], op0=ALU.mult, op1=ALU.add)
                linear_fm(wv, s, 4, rm, KC, Gg, ev, mkeys)
            if os.environ.get("K_STOPPH") == "E":
                return
            def dump(idx):
                if debug and (not is_meta) and bsel == 0 and gi == int(os.environ.get("K_DBG_GI", "0")):
                    P.dma("sp", r=[("h32", k) for k in range(KC)], w=[("dbg", idx)],
                          out=dbg[idx][:, 0:KC * Gg].rearrange("p (k g) -> p k g", g=Gg), in_=hT32[:, :, 0:Gg])
            dump(2)
            layer_norm(hT32, "h32", 0, 1, lnp, "main", Gg, NTg)
            dump(3)
            ffn(0, Gg, NTg)
            dump(6)
            layer_norm(hT32, "h32", 2, 3, lnp, "main", Gg, NTg)
            dump(4)
            if is_meta:
                P.op("pool", Q_.memset, w=[("uT", k) for k in range(KC)], ap=uT[:, :, 0:30], constant=0.0)
            else:
                if gi == 0:
                    P.op("pool", Q_.tensor_copy, r=[("ucarry_meta",)], w=[("uT", k) for k in range(KC)], out=uT[:, :, 0:30], in_=ucarry_meta[:, :, :])
                else:
                    P.op("pool", Q_.tensor_copy, r=[("ucarry",)], w=[("uT", k) for k in range(KC)], out=uT[:, :, 0:30], in_=ucarry[:, :, :])
            for i in range(4):
                s, wv = wnext("pw1_%d" % i)
                for mi in range(2):
                    cg = i * 2 + mi
                    bv, bg = bank(), bank()
                    for k in range(KC):
                        P.op("pe", T_.matmul, r=[("ws", s), ("h16", k)], w=[("ps", bv)], out=psf[bv][:, 0:Gg],
                             lhsT=wv[:, k, mi * 128:(mi + 1) * 128], rhs=hT16[:, k, 0:Gg], start=(k == 0), stop=(k == KC - 1))
                    for k in range(KC):
                        P.op("pe", T_.matmul, r=[("ws", s), ("h16", k)], w=[("ps", bg)], out=psf[bg][:, 0:Gg],
                             lhsT=wv[:, k, 256 + mi * 128:256 + (mi + 1) * 128], rhs=hT16[:, k, 0:Gg], start=(k == 0), stop=(k == KC - 1))
                    si = cg % 2
                    P.op("act", A_.activation, r=[("ps", bg), ("cpar", 5)], w=[("sgt", si)], out=sgt[si][:, 0:Gg], in_=psf[bg][:, 0:Gg],
                         func=AF.Tanh, bias=hb1g[:, cg:cg + 1], scale=0.5)
                    P.op("pool", Q_.tensor_scalar, r=[("sgt", si)], w=[("sgt", si)], out=sgt[si][:, 0:Gg], in0=sgt[si][:, 0:Gg],
                         scalar1=0.5, scalar2=0.5, op0=ALU.mult, op1=ALU.add)
                    P.op("dve", V_.scalar_tensor_tensor, r=[("ps", bv), ("sgt", si), ("cpar", 4)], w=[("uT", cg)],
                         out=uT[:, cg, 30:30 + Gg], in0=psf[bv][:, 0:Gg], scalar=cpar[:, 4, cg:cg + 1], in1=sgt[si][:, 0:Gg],
                         op0=ALU.add, op1=ALU.mult)
            ukeys = [("uT", k) for k in range(KC)]
            if is_meta:
                P.op("pool", Q_.tensor_copy, r=ukeys, w=[("ucarry_meta",)], out=ucarry_meta[:, :, :], in_=uT[:, :, 16:46])
                return
            P.op("pool", Q_.tensor_copy, r=ukeys, w=[("ucarry",)], out=ucarry[:, :, :], in_=uT[:, :, Gg:Gg + 30])
            for c in range(8):
                s, wv = wnext("dg%d" % c)
                b = bank()
                for j in range(31):
                    P.op("pe", T_.matmul, r=[("ws", s), ("uT", c)], w=[("ps", b)], out=psf[b][:, 0:Gg], lhsT=wv[:, j, :],
                         rhs=uT[:, c, j:j + Gg], start=(j == 0), stop=(j == 30))
                P.op("act", A_.activation, r=[("ps", b), ("cpar", 2)], w=[("yT32", c)], out=yT32[:, c, 0:Gg], in_=psf[b][:, 0:Gg],
                     func=AF.Identity, bias=cpar[:, 2, c:c + 1], scale=1.0)
            layer_norm(yT32, "yT32", 0, 1, cpar, "conv", Gg, NTg)
            for oh_ in range(2):
                s, wv = wnext("pw2_%d" % oh_)
                def ev(mi, b, oh_=oh_):
                    m = oh_ * 4 + mi
                    ti = m % 2
                    P.op("act", A_.activation, r=[("ps", b), ("cpar", 3)], w=[("tmpe", ti)], out=tmpe[ti][:, 0:Gg], in_=psf[b][:, 0:Gg],
                         func=AF.Identity, bias=cpar[:, 3, m:m + 1], scale=1.0)
                    P.op("dve", V_.scalar_tensor_tensor, r=[("tmpe", ti), ("h32", m)], w=[("h32", m)], out=hT32[:, m, 0:Gg],
                         in0=hT32[:, m, 0:Gg], scalar=ALPHA, in1=tmpe[ti][:, 0:Gg], op0=ALU.mult, op1=ALU.add)
                linear_fm(wv, s, 4, rm, KC, Gg, ev, mkeys)
            dump(5)
            layer_norm(hT32, "h32", 4, 5, lnp, "main", Gg, NTg)
            dump(7)
            ffn(1, Gg, NTg)
            orow = lambda tt: out[bsel, gi * G + tt * 128: gi * G + (tt + 1) * 128, :]
            layer_norm(hT32, "h32", 6, 7, lnp, "final", Gg, NTg, out_rows=orow)

        for pi_, (kind, bsel, gi) in enumerate(passes):
            if isinstance(stop_after, int) and pi_ >= stop_after:
                break
            emit_group(kind, bsel, gi, pi_)
        P.flush()
        P.final_wait()
        if info is not None:
            info["ms"] = dict(P.ms_count)
            info["dmax"] = max(16 * c for c in P.dcount)
    return nc


_CACHE = {}


def kernel(**inputs):
    n_cores = 8
    x = np.ascontiguousarray(inputs["x"], dtype=np.float32)
    B = x.shape[0]
    per = B // n_cores
    consts = host_consts()
    if "nc" not in _CACHE:
        _CACHE["nc"] = build_program(n_seq=per)
    nc = _CACHE["nc"]
    shared = {k: np.ascontiguousarray(v, dtype=np.float32) for k, v in inputs.items() if k != "x"}
    in_maps = []
    for c in range(n_cores):
        m = dict(shared)
        m.update(consts)
        m["x"] = np.ascontiguousarray(x[c * per:(c + 1) * per])
        in_maps.append(m)
    res = run_bass_kernel_spmd(nc, in_maps, core_ids=list(range(n_cores)))
    outs = [np.asarray(r["out"], dtype=np.float32) for r in res.results]
    return np.concatenate(outs, axis=0)
```

```python
import math
import os
from functools import partial
from contextlib import ExitStack
import numpy as np
import concourse.bass as bass
import concourse.mybir as mybir
from concourse.bass_utils import run_bass_kernel_spmd

F32 = mybir.dt.float32
BF16 = mybir.dt.bfloat16
AF = mybir.ActivationFunctionType
ALU = mybir.AluOpType

D = 1024
KC = 8
NH = 16
NIH = 8
DFF = 2816
FC = 22
SEQ = 2048
NSLOT = 2176
NKT = 17
NT = 2
G = NT * 128
NGRP = SEQ // G
NS = 4
WSZ = 4096
NIT = 12
ALPHA = 4.0 ** 0.25
WI_SCALE = (NIH ** -0.5) * (64 ** -0.5)
ATT_SCALE = 0.125
LN_EPS = 1e-5
NEG = -1.0e30
SAME_ENG_SYNC = True
USE_POOL_POW = True
NCH = 64


class Op:
    __slots__ = ("eng", "fn", "is_dma", "deps", "flag", "ms", "dsem", "dval", "idx")


class Prog:
    def __init__(self, nc, es, n_dma_sems=24):
        self.nc = nc
        self.eng = {"pe": nc.tensor, "act": nc.scalar, "dve": nc.vector, "pool": nc.gpsimd, "sp": nc.sync}
        self.sem = {e: es.enter_context(nc.semaphore("ms_" + e)) for e in self.eng}
        self.dsems = [es.enter_context(nc.semaphore("dq%d" % i)) for i in range(n_dma_sems)]
        self.dcount = [0] * n_dma_sems
        self.dlast = [None] * n_dma_sems
        self.dnext = 0
        self.ops = []
        self.ms_count = {e: 0 for e in self.eng}
        self.waited = {}
        self.lw = {}
        self.rd = {}
        self.rd_dma = {}
        self.names = {}
        self.nfloor = {}
        self.gfloor = []
        self.floor_done = set()
        self.last_on_eng = {}
        self.nops = 0

    def _add(self, eng, fn, r, w, is_dma):
        o = Op()
        o.eng = eng
        o.fn = fn
        o.is_dma = is_dma
        o.flag = False
        o.ms = None
        o.dsem = None
        o.dval = 0
        o.idx = self.nops
        self.nops += 1
        deps = []
        for k in r:
            if k[0] in ("ps", "psb"):
                rr = self.rd.get(k)
                if rr:
                    deps.extend(o2 for e2, o2 in rr.items() if e2 != eng)
        if eng not in self.floor_done:
            deps.extend(self.gfloor)
            self.floor_done.add(eng)
        for k in r:
            d = self.lw.get(k)
            if d is not None:
                deps.append(d)
        for k in w:
            d = self.lw.get(k)
            if d is not None:
                deps.append(d)
            rr = self.rd.get(k)
            if rr:
                deps.extend(rr.values())
            rl = self.rd_dma.get(k)
            if rl:
                deps.extend(rl)
            nf = self.nfloor.get(k[0])
            if nf is not None and eng not in nf[1]:
                deps.extend(nf[0])
                nf[1].add(eng)
        if is_dma:
            s = self.dnext % len(self.dsems)
            self.dnext += 1
            if self.dlast[s] is not None:
                deps.append(self.dlast[s])
            self.dcount[s] += 1
            o.dsem = self.dsems[s]
            o.dval = 16 * self.dcount[s]
            self.dlast[s] = o
        fd = []
        seen = set()
        for d in deps:
            if d is o or id(d) in seen:
                continue
            seen.add(id(d))
            if not d.is_dma and d.eng == eng:
                if eng == "pe" or eng == "sp" or not SAME_ENG_SYNC:
                    continue
            if not d.is_dma:
                d.flag = True
            fd.append(d)
        o.deps = fd
        for k in w:
            self.lw[k] = o
            self.rd[k] = {}
            self.rd_dma[k] = []
            self.names.setdefault(k[0], set()).add(k)
        for k in r:
            if k in w:
                continue
            self.names.setdefault(k[0], set()).add(k)
            if is_dma:
                self.rd_dma.setdefault(k, []).append(o)
            else:
                self.rd.setdefault(k, {})[eng] = o
        self.ops.append(o)
        if not is_dma:
            self.last_on_eng[eng] = o
        return o

    def op(self, eng, _f, r=(), w=(), **kw):
        return self._add(eng, partial(_f, **kw), tuple(r), tuple(w), False)

    def dma(self, q, r=(), w=(), **kw):
        return self._add(q, partial(self.eng[q].dma_start, **kw), tuple(r), tuple(w), True)

    def fence(self, from_names, to_names):
        acc = []
        for n in from_names:
            for k in self.names.get(n, ()):
                d = self.lw.get(k)
                if d is not None:
                    acc.append(d)
                rr = self.rd.get(k)
                if rr:
                    acc.extend(rr.values())
                rl = self.rd_dma.get(k)
                if rl:
                    acc.extend(rl)
        best = {}
        dm = []
        for d in acc:
            if d.is_dma:
                dm.append(d)
            else:
                b = best.get(d.eng)
                if b is None or d.idx > b.idx:
                    best[d.eng] = d
        lst = list(best.values()) + dm
        for n in to_names:
            prev = self.nfloor.get(n)
            self.nfloor[n] = ((list(prev[0]) if prev else []) + lst, set())

    def flush(self):
        for e, o in self.last_on_eng.items():
            o.flag = True
        for o in self.ops:
            E = self.eng[o.eng]
            for d in o.deps:
                if d.is_dma:
                    sem, val = d.dsem, d.dval
                else:
                    assert d.ms is not None, "dep on unflagged op"
                    sem, val = self.sem[d.eng], d.ms
                key = (o.eng, id(sem))
                if self.waited.get(key, 0) < val:
                    E.wait_ge(sem, val)
                    self.waited[key] = val
            ins = o.fn()
            if o.is_dma:
                ins.then_inc(o.dsem, 16)
            elif o.flag:
                self.ms_count[o.eng] += 1
                o.ms = self.ms_count[o.eng]
                ins.then_inc(self.sem[o.eng], 1)
            o.fn = None
        self.gfloor = [o for o in self.last_on_eng.values()] + [d for d in self.dlast if d is not None]
        self.floor_done = set()
        self.ops = []
        self.lw = {}
        self.rd = {}
        self.rd_dma = {}
        self.names = {}
        self.nfloor = {}

    def final_wait(self):
        E = self.eng["sp"]
        for s, d in enumerate(self.dlast):
            if d is not None:
                E.wait_ge(d.dsem, d.dval)
        for e, o in self.last_on_eng.items():
            if o.ms is not None and e != "sp":
                E.wait_ge(self.sem[e], o.ms)


def chunk_table():
    ch = []
    ch.append(dict(kind="std", src="w_in", c0=3072, w=512, nk=8, nm="qi"))
    ch.append(dict(kind="kiwi", nk=8, w=136, nm="kiwi"))
    for nm, c0 in (("q0", 0), ("q1", 512), ("k0", 1024), ("k1", 1536)):
        ch.append(dict(kind="std", src="w_in", c0=c0, w=512, nk=8, nm=nm))
    ch.append(dict(kind="std", src="w_in", c0=2048, w=512, nk=8, nm="v0"))
    ch.append(dict(kind="std", src="w_in", c0=2560, w=512, nk=8, nm="v1"))
    ch.append(dict(kind="std", src="w_o", c0=0, w=512, nk=8, nm="wo0"))
    ch.append(dict(kind="std", src="w_o", c0=512, w=512, nk=8, nm="wo1"))
    for l in range(2):
        if l == 1:
            for i in range(4):
                ch.append(dict(kind="pw1", i=i, nk=8, w=512, nm="pw1_%d" % i))
            for c in range(8):
                ch.append(dict(kind="diag", c=c, nk=31, w=128, nm="dg%d" % c))
            ch.append(dict(kind="std", src="w_pw2", c0=0, w=512, nk=8, nm="pw2_0"))
            ch.append(dict(kind="std", src="w_pw2", c0=512, w=512, nk=8, nm="pw2_1"))
        for j in range(6):
            w = 512 if j < 5 else 256
            ch.append(dict(kind="std", src="gate%d" % l, c0=j * 512, w=w, nk=8, nm="g%d_%d" % (l, j)))
            ch.append(dict(kind="std", src="up%d" % l, c0=j * 512, w=w, nk=8, nm="u%d_%d" % (l, j)))
        for m in range(8):
            ch.append(dict(kind="std", src="down%d" % l, c0=m * 128, w=128, nk=22, nm="d%d_%d" % (l, m)))
    assert len(ch) == NCH, len(ch)
    return ch


def t5_bucket_np(d):
    d = np.asarray(d, dtype=np.int64)
    dm = np.maximum(d, 1).astype(np.float32)
    large = 16 + (np.log(dm / np.float32(16)) / np.float32(math.log(128 / 16)) * np.float32(16)).astype(np.int32)
    large = np.minimum(large, 31)
    return np.where(d < 16, d, large)


def host_consts():
    c = {}
    c["ident"] = np.eye(128, dtype=np.float32)
    j = np.arange(384)
    dist = np.maximum(255 - j, 0)
    b = t5_bucket_np(dist)
    oh = np.zeros((32, 384), np.float32)
    oh[b, j] = 1.0
    c["ohr"] = oh
    q = np.arange(128)[:, None]
    k = np.arange(128)[None, :]
    c["cbias"] = np.where(k <= q, 0.0, NEG).astype(np.float32)
    c["padb"] = np.broadcast_to(np.where(k >= 16, NEG, 0.0), (128, 128)).astype(np.float32).copy()
    c["pow2"] = np.broadcast_to((2.0 ** (1.0 - np.arange(NIT + 1)))[None, :], (128, NIT + 1)).astype(np.float32).copy()
    return c


def build_program(n_seq=2, debug=False, stop_after=None, info=None):
    nc = bass.Bass("TRN2", target_bir_lowering=False)
    es = ExitStack()
    with es:
        es.enter_context(nc.allow_non_contiguous_dma(reason="small param/layout loads"))
        es.enter_context(nc.allow_low_precision(reason="bf16 matmul operands by design"))

        def din(name, shape, dt=F32):
            return nc.dram_tensor(name, list(shape), dt, kind="ExternalInput").ap()

        x = din("x", [n_seq, SEQ, D])
        meta = din("meta_tokens", [16, D])
        relb = din("rel_bias", [32, 16])
        w_in = din("w_in_attn", [1, D, 3656])
        w_o = din("w_o_attn", [1, D, D])
        w_pw1 = din("w_pw1", [1, D, 2 * D])
        b_pw1 = din("b_pw1", [1, 2 * D])
        w_dw = din("w_dw", [1, 31, D])
        b_dw = din("b_dw", [1, D])
        cln_g = din("conv_ln_g", [1, D])
        cln_b = din("conv_ln_b", [1, D])
        w_pw2 = din("w_pw2", [1, D, D])
        b_pw2 = din("b_pw2", [1, D])
        ln1_g = din("ln1_g", [2, D])
        ln1_b = din("ln1_b", [2, D])
        w_gate = din("ffn_w_gate", [2, D, DFF])
        w_up = din("ffn_w_up", [2, D, DFF])
        w_down = din("ffn_w_down", [2, DFF, D])
        ln2_g = din("ln2_g", [2, D])
        ln2_b = din("ln2_b", [2, D])
        c_ident = din("ident", [128, 128])
        c_ohr = din("ohr", [32, 384])
        c_cbias = din("cbias", [128, 128])
        c_padb = din("padb", [128, 128])
        c_pow2 = din("pow2", [128, NIT + 1])
        out = nc.dram_tensor("out", [n_seq, SEQ, D], F32, kind="ExternalOutput").ap()
        wscr = nc.dram_tensor("wscr", [NCH, 128, WSZ], BF16, kind="Internal").ap()
        tbd = nc.dram_tensor("tbd", [16, 384], F32, kind="Internal").ap()
        ZH = 128 * 385
        ztd = nc.dram_tensor("ztd", [16, ZH], F32, kind="Internal").ap()
        dbg = None
        if debug:
            dbg = nc.dram_tensor("dbg", [8, 128, 2176], F32, kind="ExternalOutput").ap()

        wsrc = {"w_in": w_in[0], "w_o": w_o[0], "w_pw2": w_pw2[0],
                "gate0": w_gate[0], "gate1": w_gate[1], "up0": w_up[0], "up1": w_up[1],
                "down0": w_down[0], "down1": w_down[1]}
        chunks = chunk_table()
        cid = {c["nm"]: i for i, c in enumerate(chunks)}

        P = Prog(nc, es)
        V_, A_, S_, T_, Q_ = nc.vector, nc.scalar, nc.sync, nc.tensor, nc.gpsimd

        def sb(name, shape, dt):
            return es.enter_context(nc.sbuf_tensor("s_" + name, list(shape), dt))

        ident32 = sb("ident32", [128, 128], F32)
        identb = sb("identb", [128, 128], BF16)
        P.dma("sp", w=[("ident32",)], out=ident32[:], in_=c_ident[:, :])
        P.op("dve", V_.tensor_copy, r=[("ident32",)], w=[("identb",)], out=identb[:], in_=ident32[:])

        with ExitStack() as es2:
            NPB = 4
            stg = [es2.enter_context(nc.sbuf_tensor("stg%d" % i, [128, WSZ], F32)) for i in range(NPB)]
            cvt = [es2.enter_context(nc.sbuf_tensor("cvt%d" % i, [128, WSZ], BF16)) for i in range(NPB)]
            wT = es2.enter_context(nc.sbuf_tensor("wdwT", [128, 8, 31], F32))
            tbs = es2.enter_context(nc.sbuf_tensor("tbs_p", [16, 384], F32))
            rb_sb = es2.enter_context(nc.sbuf_tensor("rb_p", [32, 16], F32))
            oh_sb = es2.enter_context(nc.sbuf_tensor("oh_p", [32, 384], F32))
            pst = es2.enter_context(nc.psum_tensor("pst_p", [128, 512], F32))
            P.dma("act", w=[("rb_sb",)], out=rb_sb[:], in_=relb[:, :])
            P.dma("act", w=[("oh_sb",)], out=oh_sb[:], in_=c_ohr[:, :])
            P.op("pe", T_.matmul, r=[("rb_sb",), ("oh_sb",)], w=[("ps", 99)],
                 out=pst[0:16, 0:384], lhsT=rb_sb[:, :], rhs=oh_sb[:, :], start=True, stop=True)
            P.op("dve", V_.tensor_copy, r=[("ps", 99)], w=[("tbs0",)], out=tbs[:, :], in_=pst[0:16, 0:384])
            P.op("dve", V_.tensor_scalar, r=[("tbs0",)], w=[("tbs",)], out=tbs[:, :], in0=tbs[:, :],
                 scalar1=tbs[:, 0:1], scalar2=1.0 / ATT_SCALE, op0=ALU.subtract, op1=ALU.mult)
            P.dma("act", r=[("tbs",)], w=[("tbd",)], out=tbd[:, :], in_=tbs[:, :])
            for h in range(NH):
                P.dma("act", r=[("tbd",)], w=[("ztd", h)], out=bass.AP(ztd.tensor, h * ZH, [[385, 128], [1, 384]]),
                      in_=bass.AP(tbd.tensor, h * 384, [[0, 128], [1, 384]]))
            import os
            for c in range(8):
                if os.environ.get("K_NOWT"):
                    break
                src = bass.AP(w_dw.tensor, c * 128, [[1, 128], [D, 31]])
                P.dma("sp", w=[("wdwT", c)], out=wT[:, c, :], in_=src)
            for i, c in enumerate(chunks):
                if os.environ.get("K_PRE") and c["nm"] not in os.environ["K_PRE"].split(","):
                    continue
                b = i % NPB
                free = c["nk"] * c["w"]
                sv = stg[b][:, :free].rearrange("p (k n) -> p k n", n=c["w"])
                cv = cvt[b][:, :free]
                if c["kind"] == "std":
                    W = wsrc[c["src"]]
                    src = W.rearrange("(k p) n -> p k n", p=128)[:, :, c["c0"]:c["c0"] + c["w"]]
                    P.dma("sp", w=[("stg", b)], out=sv, in_=src)
                elif c["kind"] == "kiwi":
                    W = w_in[0].rearrange("(k p) n -> p k n", p=128)
                    P.dma("sp", w=[("stg", b)], out=sv[:, :, 0:64], in_=W[:, :, 3584:3648])
                    P.dma("sp", w=[("stg", b)], out=sv[:, :, 64:128], in_=W[:, :, 3584:3648])
                    P.dma("sp", w=[("stg", b)], out=sv[:, :, 128:136], in_=W[:, :, 3648:3656])
                elif c["kind"] == "pw1":
                    W = w_pw1[0].rearrange("(k p) n -> p k n", p=128)
                    ii = c["i"]
                    P.dma("sp", w=[("stg", b)], out=sv[:, :, 0:256], in_=W[:, :, ii * 256:(ii + 1) * 256])
                    P.dma("sp", w=[("stg", b)], out=sv[:, :, 256:512], in_=W[:, :, D + ii * 256:D + (ii + 1) * 256])
                if c["kind"] == "diag":
                    cc = c["c"]
                    cv3 = cv.rearrange("p (k n) -> p k n", n=128)
                    P.op("dve", V_.tensor_tensor, r=[("wdwT", cc), ("ident32",)], w=[("cvt", b, j) for j in range(31)],
                         out=cv3, in0=ident32[:, :].unsqueeze(1).to_broadcast([128, 31, 128]),
                         in1=wT[:, cc, :].unsqueeze(2).to_broadcast([128, 31, 128]), op=ALU.mult)
                    wkeys = [("cvt", b, j) for j in range(31)]
                    P.dma("act", r=wkeys, w=[("wscr", i)], out=wscr[i][:, :free], in_=cv)
                else:
                    wkeys = [("cvt", b, j) for j in range(31)]
                    if (i // 2) % 2 == 0:
                        P.op("dve", V_.tensor_copy, r=[("stg", b)], w=wkeys, out=cv, in_=stg[b][:, :free])
                    else:
                        P.op("act", A_.copy, r=[("stg", b)], w=wkeys, out=cv, in_=stg[b][:, :free])
                    P.dma("act", r=wkeys, w=[("wscr", i)], out=wscr[i][:, :free], in_=cv)
            P.flush()
        if stop_after == "prepass":
            P.final_wait()
            return nc

        KT = sb("KT", [128, KC, NSLOT], BF16)
        Vt = sb("Vt", [128, NKT, NH, 65], BF16)
        kiT2 = sb("kiT2", [128, NSLOT], BF16)
        hT32 = sb("hT32", [128, KC, G], F32)
        hT16 = sb("hT16", [128, KC, G], BF16)
        xnb = [sb("xn%d" % i, [128, D], F32) for i in range(2)]
        qz = [sb("qz%d" % i, [128, G], BF16) for i in range(4)]
        qiz = [sb("qiz%d" % i, [128, 128], BF16) for i in range(8)]
        wi = sb("wi", [128, NT, 8], F32)
        mixT = sb("mixT", [128, KC, G], BF16)
        wslot = [sb("ws%d" % i, [128, WSZ], BF16) for i in range(NS)]
        ARENA_BYTES = 58 * 1024
        arena = sb("arena", [128, ARENA_BYTES // 2], BF16)
        off = [0]

        def carve(nelem, dt, shape=None, base=None):
            nb = nelem * (4 if dt == F32 else 2)
            o = off[0] if base is None else base
            assert o % 4 == 0
            if base is None:
                off[0] += (nb + 3) // 4 * 4
            assert o + nb <= ARENA_BYTES, (o, nb)
            v = arena[:, o // 2:(o + nb) // 2]
            if dt == F32:
                v = v.bitcast(F32)
            return v

        score_f = carve(NT * NSLOT, F32)
        score = score_f.rearrange("p (t s) -> p t s", s=NSLOT)
        maskq_f = carve(NT * NSLOT, BF16)
        maskq = maskq_f.rearrange("p (t s) -> p t s", s=NSLOT)
        maskT = carve(NKT * G, BF16).rearrange("p (j g) -> p j g", g=G)
        qT = carve(KC * G, BF16).rearrange("p (k g) -> p k g", g=G)
        qiT = carve(4 * G, BF16).rearrange("p (k g) -> p k g", g=G)
        NRB = 5
        off_rb = off[0]
        Rb = [carve(512, BF16) for _ in range(NRB)]
        dgw = carve(NT * NIH * 128, BF16).rearrange("p (t h k) -> p t h k", h=NIH, k=128)
        NPT = 5
        Pt = [carve(512, BF16) for _ in range(NPT)]
        attn = carve(NT * D, BF16).rearrange("p (t d) -> p t d", d=D)
        endA = off[0]
        xin = carve(NT * D, F32, base=off_rb).rearrange("p (t d) -> p t d", d=D)
        off[0] = 0
        actT = carve(FC * G, BF16).rearrange("p (k g) -> p k g", g=G)
        sgt = [carve(G, F32) for _ in range(2)]
        uT = carve(KC * (30 + G), BF16).rearrange("p (k g) -> p k g", g=30 + G)
        yT32 = carve(KC * G, F32).rearrange("p (k g) -> p k g", g=G)
        tmpe = [carve(G, F32) for _ in range(2)]
        endB = off[0]
        A_NAMES = ["score", "maskq", "maskT", "qT", "qiT", "R", "Pt", "attn", "dgw"]
        B_NAMES = ["actT", "sgt", "uT", "yT32", "tmpe"]

        ucarry = sb("ucarry", [128, KC, 30], BF16)
        ucarry_meta = sb("ucarry_meta", [128, KC, 30], BF16)
        Asm = sb("Asm", [128, NT], F32)
        Amx = sb("Amx", [128, NT, 2], F32)
        wtab = sb("wtab", [128, NT, NIT + 1], F32)
        test = sb("test", [128, NT], F32)
        cnt = sb("cnt", [128, NT], F32)
        sg = sb("sg", [128, NT], F32)
        tmpb = sb("tmpb", [128, NT], F32)
        lo = sb("lo", [128, NT], F32)
        thr = sb("thr", [128, NT], F32)
        mhalf = sb("mhalf", [128, NT], F32)
        hb1g = sb("hb1g", [128, 8], F32)
        rec = [sb("rec%d" % i, [128, NT], F32) for i in range(2)]
        stats = sb("stats", [128, NT, 12], F32)
        mv = sb("mv", [128, NT, 2], F32)
        vpe = sb("vpe", [128, NT], F32)
        rstd = sb("rstd", [128, NT], F32)
        nmr = sb("nmr", [128, NT], F32)
        lnp = sb("lnp", [128, 8, 8], F32)
        cpar = sb("cpar", [128, 6, 8], F32)
        Bq = sb("Bq", [128, NH, 2, 128], BF16)
        Bqm = sb("Bqm", [128, NH, 128], BF16)
        cfar = sb("cfar", [128, NH], F32)
        cbias = sb("cbias", [128, 128], F32)
        padb = sb("padb", [128, 128], F32)
        pow2 = sb("pow2", [128, NIT + 1], F32)
        zerob = sb("zerob", [128, 256], BF16)
        psf = [es.enter_context(nc.psum_tensor("psf%d" % i, [128, 512], F32)) for i in range(6)]
        psb = [es.enter_context(nc.psum_tensor("psb%d" % i, [128, 1024], BF16)) for i in range(2)]
        bctr = [0, 0]

        def bank():
            i = bctr[0] % 4
            bctr[0] += 1
            return i

        b2ctr = [0]

        def bank2():
            i = 4 + b2ctr[0] % 2
            b2ctr[0] += 1
            return i

        def bankb():
            i = bctr[1] % 2
            bctr[1] += 1
            return i

        def ld_pp(dst, src_vec, key):
            P.dma("sp", w=[key], out=dst, in_=src_vec.rearrange("(c p) -> p c", p=128))

        for l in range(2):
            ld_pp(lnp[:, 4 * l + 0, :], ln1_g[l], ("lnp", 4 * l + 0))
            ld_pp(lnp[:, 4 * l + 1, :], ln1_b[l], ("lnp", 4 * l + 1))
            ld_pp(lnp[:, 4 * l + 2, :], ln2_g[l], ("lnp", 4 * l + 2))
            ld_pp(lnp[:, 4 * l + 3, :], ln2_b[l], ("lnp", 4 * l + 3))
        ld_pp(cpar[:, 0, :], cln_g[0], ("cpar", 0))
        ld_pp(cpar[:, 1, :], cln_b[0], ("cpar", 1))
        ld_pp(cpar[:, 2, :], b_dw[0], ("cpar", 2))
        ld_pp(cpar[:, 3, :], b_pw2[0], ("cpar", 3))
        ld_pp(cpar[:, 4, :], b_pw1[0][0:D], ("cpar", 4))
        ld_pp(cpar[:, 5, :], b_pw1[0][D:2 * D], ("cpar", 5))
        P.dma("sp", w=[("cbias",)], out=cbias[:], in_=c_cbias[:, :])
        P.dma("sp", w=[("padb",)], out=padb[:], in_=c_padb[:, :])
        P.dma("sp", w=[("pow2",)], out=pow2[:], in_=c_pow2[:, :])
        P.dma("sp", w=[("cfar",)], out=cfar[:], in_=bass.AP(relb.tensor, 31 * 16, [[0, 128], [1, 16]]))
        P.op("pool", Q_.memset, w=[("zerob",)], ap=zerob[:], constant=0.0)
        P.op("dve", V_.tensor_scalar, r=[("cpar", 5)], w=[("hb1g",)], out=hb1g[:, :], in0=cpar[:, 5, :], scalar1=0.5, scalar2=None,
             op0=ALU.mult)
        P.op("pool", Q_.memset, w=[("mhalf",)], ap=mhalf[:], constant=-0.5)
        for i in range(4):
            P.op("pool", Q_.memset, w=[("qz", i)], ap=qz[i][:], constant=0.0)
        for i in range(8):
            P.op("pool", Q_.memset, w=[("qiz", i)], ap=qiz[i][:], constant=0.0)
        P.op("pool", Q_.memset, w=[("Vones",)], ap=Vt[:, :, :, 64:65], constant=1.0)
        stgs = [xnb[0][:, :].rearrange("p (h k) -> p h k", k=128), xnb[1][:, :].rearrange("p (h k) -> p h k", k=128),
                hT32[:, :, 0:128], hT32[:, :, 128:256],
                score_f[:, 0:1024].rearrange("p (h k) -> p h k", k=128), score_f[:, 1024:2048].rearrange("p (h k) -> p h k", k=128)]
        for dlt in range(3):
            for hh in range(2):
                si_ = dlt * 2 + hh
                bst = stgs[si_]
                for h8 in range(8):
                    h = hh * 8 + h8
                    base = {0: 255, 1: 127, 2: 239}[dlt]
                    src = bass.AP(ztd.tensor, h * ZH + base, [[384, 128], [1, 128]])
                    P.dma("sp" if h8 % 2 == 0 else "act", w=[("stg6", si_)], out=bst[:, h8, :], in_=src)
                if dlt < 2:
                    P.op("dve", V_.tensor_copy, r=[("stg6", si_)], w=[("Bq",)], out=Bq[:, hh * 8:(hh + 1) * 8, dlt, :], in_=bst)
                else:
                    P.op("dve", V_.tensor_copy, r=[("stg6", si_)], w=[("Bqm",)], out=Bqm[:, hh * 8:(hh + 1) * 8, :], in_=bst)
        P.flush()
        if stop_after == "setup":
            P.final_wait()
            return nc

        passes = [("meta", 0, 0)] + [("x", b, gi) for b in range(n_seq) for gi in range(NGRP)]
        n_meta_chunks = cid["pw1_3"] + 1
        order = []
        for ps_ in passes:
            n = n_meta_chunks if ps_[0] == "meta" else NCH
            order.extend(range(n))
        wst = dict(issued=0, pos=0)

        def wissue(i):
            ch = order[i]
            c = chunks[ch]
            free = c["nk"] * c["w"]
            s = i % NS
            P.dma("sp", r=[("wscr", ch)], w=[("ws", s)], out=wslot[s][:, :free], in_=wscr[ch][:, :free])

        def wnext(expect, issue=True):
            while issue and wst["issued"] < min(len(order), wst["pos"] + NS):
                wissue(wst["issued"])
                wst["issued"] += 1
            i = wst["pos"]
            ch = order[i]
            assert chunks[ch]["nm"] == expect, (chunks[ch]["nm"], expect)
            c = chunks[ch]
            s = i % NS
            wst["pos"] += 1
            return s, wslot[s][:, :c["nk"] * c["w"]].rearrange("p (k n) -> p k n", n=c["w"])

        def linear_fm(wv, s, nm_, rhs_fn, nk, Gg, evac, rkeys):
            for mi in range(nm_):
                b = bank()
                for k in range(nk):
                    P.op("pe", T_.matmul, r=[("ws", s)] + rkeys, w=[("ps", b)],
                         out=psf[b][:, 0:Gg], lhsT=wv[:, k, mi * 128:(mi + 1) * 128], rhs=rhs_fn(k),
                         start=(k == 0), stop=(k == nk - 1))
                evac(mi, b)

        def layer_norm(src, srcname, gi_, bi_, gtab, mode, Gg, NTg, out_rows=None):
            banks = []
            for tt in range(NTg):
                bA, bB = bank(), bank()
                banks.append((bA, bB))
                for kc in range(KC):
                    b = bA if kc < 4 else bB
                    P.op("pe", T_.transpose, r=[(srcname, kc), ("ident32",)], w=[("ps", b)],
                         out=psf[b][:, (kc % 4) * 128:(kc % 4 + 1) * 128], in_=src[:, kc, tt * 128:(tt + 1) * 128],
                         identity=ident32[:])
                P.op("dve", V_.bn_stats, r=[("ps", bA)], w=[("stats", tt, 0)], out=stats[:, tt, 0:6], in_=psf[bA][:, :])
                P.op("dve", V_.bn_stats, r=[("ps", bB)], w=[("stats", tt, 1)], out=stats[:, tt, 6:12], in_=psf[bB][:, :])
                P.op("dve", V_.bn_aggr, r=[("stats", tt, 0), ("stats", tt, 1)], w=[("mv", tt)], out=mv[:, tt, :], in_=stats[:, tt, :])
            mvk = [("mv", tt) for tt in range(NTg)]
            P.op("dve", V_.tensor_scalar, r=mvk, w=[("vpe",)], out=vpe[:, 0:NTg], in0=mv[:, 0:NTg, 1],
                 scalar1=LN_EPS, scalar2=None, op0=ALU.add)
            if USE_POOL_POW:
                P.op("pool", Q_.tensor_tensor, r=[("vpe",), ("mhalf",)], w=[("rstd",)], out=rstd[:, 0:NTg], in0=vpe[:, 0:NTg],
                     in1=mhalf[:, 0:NTg], op=ALU.pow)
            else:
                P.op("act", A_.activation, r=[("vpe",)], w=[("vpe2",)], out=vpe[:, 0:NTg], in_=vpe[:, 0:NTg], func=AF.Sqrt)
                P.op("dve", V_.reciprocal, r=[("vpe2",)], w=[("rstd",)], out=rstd[:, 0:NTg], in_=vpe[:, 0:NTg])
            P.op("dve", V_.scalar_tensor_tensor, r=mvk + [("rstd",)], w=[("nmr",)], out=nmr[:, 0:NTg], in0=mv[:, 0:NTg, 0],
                 scalar=-1.0, in1=rstd[:, 0:NTg], op0=ALU.mult, op1=ALU.mult)
            for tt in range(NTg):
                bA, bB = banks[tt]
                xn = xnb[tt % 2]
                xi = tt % 2
                P.op("act", A_.activation, r=[("ps", bA), ("rstd",), ("nmr",)], w=[("xn", xi, 0)],
                     out=xn[:, 0:512], in_=psf[bA][:, :], func=AF.Identity, scale=rstd[:, tt:tt + 1], bias=nmr[:, tt:tt + 1])
                P.op("dve", V_.tensor_scalar, r=[("ps", bB), ("rstd",), ("nmr",)], w=[("xn", xi, 1)],
                     out=xn[:, 512:1024], in0=psf[bB][:, :], scalar1=rstd[:, tt:tt + 1], scalar2=nmr[:, tt:tt + 1],
                     op0=ALU.mult, op1=ALU.add)
            cbanks = []
            for tt in range(NTg):
                xn = xnb[tt % 2]
                xi = tt % 2
                cA, cB = (bank2(), bank2()) if tt % 2 == 0 else (bank(), bank())
                cbanks.append((cA, cB))
                for kc in range(KC):
                    b = cA if kc < 4 else cB
                    P.op("pe", T_.transpose, r=[("xn", xi, kc // 4), ("ident32",)], w=[("ps", b)],
                         out=psf[b][:, (kc % 4) * 128:(kc % 4 + 1) * 128], in_=xn[:, kc * 128:(kc + 1) * 128],
                         identity=ident32[:])
            if mode == "conv":
                for tt in range(NTg):
                    cA, cB = cbanks[tt]
                    for kc in range(KC):
                        b = cA if kc < 4 else cB
                        pv = psf[b][:, (kc % 4) * 128:(kc % 4 + 1) * 128]
                        P.op("act", A_.activation, r=[("ps", b), ("cpar", 0), ("cpar", 1)], w=[("mixT", kc)],
                             out=mixT[:, kc, tt * 128:(tt + 1) * 128], in_=pv, func=AF.Silu,
                             scale=cpar[:, 0, kc:kc + 1], bias=cpar[:, 1, kc:kc + 1])
            else:
                for (dst, dkey) in (((hT16, "h16"), (hT32, "h32")) if mode == "main" else ((hT32, "h32"),)):
                    for tt in range(NTg):
                        cA, cB = cbanks[tt]
                        for kc in range(KC):
                            b = cA if kc < 4 else cB
                            pv = psf[b][:, (kc % 4) * 128:(kc % 4 + 1) * 128]
                            if kc < 4:
                                P.op("dve", V_.tensor_scalar, r=[("ps", b), ("lnp", gi_), ("lnp", bi_)], w=[(dkey, kc)],
                                     out=dst[:, kc, tt * 128:(tt + 1) * 128], in0=pv, scalar1=gtab[:, gi_, kc:kc + 1],
                                     scalar2=gtab[:, bi_, kc:kc + 1], op0=ALU.mult, op1=ALU.add)
                            else:
                                P.op("act", A_.activation, r=[("ps", b), ("lnp", gi_), ("lnp", bi_)], w=[(dkey, kc)],
                                     out=dst[:, kc, tt * 128:(tt + 1) * 128], in_=pv, func=AF.Identity, scale=gtab[:, gi_, kc:kc + 1],
                                     bias=gtab[:, bi_, kc:kc + 1])
            if mode == "final":
                for tt in range(NTg):
                    xn = xnb[tt % 2]
                    xi = tt % 2
                    dA, dB = bank(), bank()
                    for kc in range(KC):
                        b = dA if kc < 4 else dB
                        P.op("pe", T_.transpose, r=[("h32", kc), ("ident32",)], w=[("ps", b)],
                             out=psf[b][:, (kc % 4) * 128:(kc % 4 + 1) * 128], in_=hT32[:, kc, tt * 128:(tt + 1) * 128],
                             identity=ident32[:])
                    P.op("act", A_.copy, r=[("ps", dA)], w=[("xn", xi, 0)], out=xn[:, 0:512], in_=psf[dA][:, :])
                    P.op("dve", V_.tensor_copy, r=[("ps", dB)], w=[("xn", xi, 1)], out=xn[:, 512:1024], in_=psf[dB][:, :])
                    P.dma("sp", r=[("xn", xi, 0), ("xn", xi, 1)], w=[("out",)], out=out_rows(tt), in_=xn[:, :])

        def ffn(l, Gg, NTg):
            P.fence(A_NAMES, B_NAMES)
            for j in range(6):
                wcols = 512 if j < 5 else 256
                sgi, wg = wnext("g%d_%d" % (l, j))
                sui, wu = wnext("u%d_%d" % (l, j), issue=False)
                for mi in range(wcols // 128):
                    m = j * 4 + mi
                    bg, bu = bank(), bank()
                    for k in range(KC):
                        P.op("pe", T_.matmul, r=[("ws", sgi), ("h16", k)], w=[("ps", bg)], out=psf[bg][:, 0:Gg],
                             lhsT=wg[:, k, mi * 128:(mi + 1) * 128], rhs=hT16[:, k, 0:Gg], start=(k == 0), stop=(k == KC - 1))
                    for k in range(KC):
                        P.op("pe", T_.matmul, r=[("ws", sui), ("h16", k)], w=[("ps", bu)], out=psf[bu][:, 0:Gg],
                             lhsT=wu[:, k, mi * 128:(mi + 1) * 128], rhs=hT16[:, k, 0:Gg], start=(k == 0), stop=(k == KC - 1))
                    si = m % 2
                    P.op("act", A_.activation, r=[("ps", bg)], w=[("sgt", si)], out=sgt[si][:, 0:Gg], in_=psf[bg][:, 0:Gg], func=AF.Silu)
                    P.op("dve", V_.tensor_tensor, r=[("sgt", si), ("ps", bu)], w=[("actT", m)], out=actT[:, m, 0:Gg],
                         in0=sgt[si][:, 0:Gg], in1=psf[bu][:, 0:Gg], op=ALU.mult)
            for m in range(8):
                s, wd = wnext("d%d_%d" % (l, m))
                b = bank()
                for k in range(FC):
                    P.op("pe", T_.matmul, r=[("ws", s), ("actT", k)], w=[("ps", b)], out=psf[b][:, 0:Gg],
                         lhsT=wd[:, k, :], rhs=actT[:, k, 0:Gg], start=(k == 0), stop=(k == FC - 1))
                P.op("dve", V_.scalar_tensor_tensor, r=[("ps", b), ("h32", m)], w=[("h32", m)], out=hT32[:, m, 0:Gg],
                     in0=hT32[:, m, 0:Gg], scalar=ALPHA, in1=psf[b][:, 0:Gg], op0=ALU.mult, op1=ALU.add)

        def prefetch_x(pidx):
            if pidx >= len(passes):
                return
            kind_, b_, g_ = passes[pidx]
            if kind_ == "meta":
                return
            if isinstance(stop_after, int) and pidx >= stop_after:
                return
            P.fence(["R", "dgw", "Pt"], ["xin"])
            for tt in range(NT):
                r0 = g_ * G + tt * 128
                P.dma("sp", w=[("xin", tt)], out=xin[:, tt, :], in_=x[b_, r0:r0 + 128, :])

        def emit_group(kind, bsel, gi, pidx=0):
            is_meta = kind == "meta"
            NTg = 1 if is_meta else NT
            Gg = NTg * 128
            T0 = -1 if is_meta else gi * NT
            slot0 = (T0 + 1) * 128
            P.fence(B_NAMES, A_NAMES)
            for tt in range(NTg):
                xi = tt % 2
                xn = xnb[xi]
                if is_meta:
                    P.op("pool", Q_.memset, w=[("xn", xi, 0), ("xn", xi, 1)], ap=xn[:, :], constant=0.0)
                    P.dma("sp", w=[("xn", xi, 0), ("xn", xi, 1)], out=xn[0:16, :], in_=meta[:, :])
                if os.environ.get("K_STOPPH") == "A0":
                    return
                bA, bB = bank(), bank()
                for kc in range(KC):
                    b = bA if kc < 4 else bB
                    if is_meta:
                        P.op("pe", T_.transpose, r=[("xn", xi, kc // 4), ("ident32",)], w=[("ps", b)],
                             out=psf[b][:, (kc % 4) * 128:(kc % 4 + 1) * 128], in_=xn[:, kc * 128:(kc + 1) * 128], identity=ident32[:])
                    else:
                        P.op("pe", T_.transpose, r=[("xin", tt), ("ident32",)], w=[("ps", b)],
                             out=psf[b][:, (kc % 4) * 128:(kc % 4 + 1) * 128], in_=xin[:, tt, kc * 128:(kc + 1) * 128], identity=ident32[:])
                if os.environ.get("K_STOPPH") == "A1":
                    return
                for hf, b in ((0, bA), (1, bB)):
                    pv = psf[b][:, :].rearrange("p (k n) -> p k n", n=128)
                    hk = [("h32", kc) for kc in range(hf * 4, hf * 4 + 4)]
                    hk16 = [("h16", kc) for kc in range(hf * 4, hf * 4 + 4)]
                    if os.environ.get("K_X") != "noact":
                        P.op("act", A_.copy, r=[("ps", b)], w=hk, out=hT32[:, hf * 4:hf * 4 + 4, tt * 128:(tt + 1) * 128], in_=pv)
                    if os.environ.get("K_X") != "nodve":
                        P.op("dve", V_.tensor_copy, r=[("ps", b)] + (hk if os.environ.get("K_X") == "ser" else []), w=hk16, out=hT16[:, hf * 4:hf * 4 + 4, tt * 128:(tt + 1) * 128], in_=pv)
            if os.environ.get("K_STOPPH") == "A":
                return
            if not is_meta:
                P.fence(["xin"], ["R", "dgw", "Pt"])
            hkeys = [("h16", k) for k in range(KC)]
            rh = lambda k: hT16[:, k, 0:Gg]
            s, wv = wnext("qi")
            def ev(mi, b):
                P.op("act", A_.copy, r=[("ps", b)], w=[("qiT", mi)], out=qiT[:, mi, 0:Gg], in_=psf[b][:, 0:Gg])
            linear_fm(wv, s, 4, rh, KC, Gg, ev, hkeys)
            s, wv = wnext("kiwi")
            def ev(mi, b):
                P.op("dve", V_.tensor_copy, r=[("ps", b)], w=[("kiT2", T0 + 1 + t_) for t_ in range(NTg)],
                     out=kiT2[:, slot0:slot0 + Gg], in_=psf[b][:, 0:Gg])
            linear_fm(wv, s, 1, rh, KC, Gg, ev, hkeys)
            for tt in range(NTg):
                b = bank()
                for k in range(KC):
                    P.op("pe", T_.matmul, r=[("ws", s), ("h16", k)], w=[("ps", b)], out=psf[b][:, 0:8],
                         lhsT=hT16[:, k, tt * 128:(tt + 1) * 128], rhs=wv[:, k, 128:136], start=(k == 0), stop=(k == KC - 1))
                P.op("dve", V_.tensor_scalar, r=[("ps", b)], w=[("wi", tt)], out=wi[:, tt, :], in0=psf[b][:, 0:8],
                     scalar1=WI_SCALE, scalar2=None, op0=ALU.mult)
            if os.environ.get("K_STOPPH") == "B":
                return
            Ss = [(T0 + tt + 2) * 128 for tt in range(NTg)]
            for tt in range(NTg):
                for hh in range(NIH):
                    half = hh % 2
                    P.op("dve", V_.tensor_copy, r=[("qiT", hh // 2)], w=[("qiz", hh)],
                         out=qiz[hh][half * 64:(half + 1) * 64, :], in_=qiT[half * 64:(half + 1) * 64, hh // 2, tt * 128:(tt + 1) * 128])
                    P.op("act", A_.activation, r=[("wi", tt), ("identb",)], w=[("dgw", tt, hh)], out=dgw[:, tt, hh, :], in_=identb[:, :],
                         func=AF.Copy, scale=wi[:, tt, hh:hh + 1])
                S = Ss[tt]
                nkb = (S + 511) // 512
                iitems = [(kb, hh) for kb in range(nkb) for hh in range(NIH)]
                accb = {}
                rinfo = {}

                def idx_l(i, tt=tt, S=S):
                    kb, hh = iitems[i]
                    c0, c1 = kb * 512, min(S, kb * 512 + 512)
                    b = bank()
                    kkeys = [("kiT2", j) for j in range(c0 // 128, c1 // 128)]
                    P.op("pe", T_.matmul, r=[("qiz", hh)] + kkeys, w=[("ps", b)], out=psf[b][:, 0:c1 - c0],
                         lhsT=qiz[hh][:, :], rhs=kiT2[:, c0:c1], start=True, stop=True)
                    ri = i % NRB
                    if i % 2 == 0:
                        P.op("act", A_.activation, r=[("ps", b)], w=[("R", ri)], out=Rb[ri][:, 0:c1 - c0],
                             in_=psf[b][:, 0:c1 - c0], func=AF.Relu)
                    else:
                        P.op("dve", V_.tensor_scalar, r=[("ps", b)], w=[("R", ri)], out=Rb[ri][:, 0:c1 - c0],
                             in0=psf[b][:, 0:c1 - c0], scalar1=0.0, scalar2=None, op0=ALU.max)
                    rinfo[i] = ri

                def idx_a(i, tt=tt, S=S):
                    kb, hh = iitems[i]
                    c0, c1 = kb * 512, min(S, kb * 512 + 512)
                    if hh == 0:
                        accb[kb] = bank2()
                    ab = accb[kb]
                    ri = rinfo[i]
                    P.op("pe", T_.matmul, r=[("R", ri), ("dgw", tt, hh)], w=[("ps", ab)], out=psf[ab][:, 0:c1 - c0],
                         lhsT=dgw[:, tt, hh, :], rhs=Rb[ri][:, 0:c1 - c0], start=(hh == 0), stop=(hh == NIH - 1))
                    if hh == NIH - 1:
                        P.op("act", A_.copy, r=[("ps", ab)], w=[("score", tt, kb)], out=score[:, tt, c0:c1], in_=psf[ab][:, 0:c1 - c0])

                ISK = 3
                for i in range(len(iitems) + ISK):
                    if i < len(iitems):
                        idx_l(i)
                    if i - ISK >= 0:
                        idx_a(i - ISK)
                skeys = [("score", tt, kb) for kb in range(nkb)]
                P.op("dve", V_.tensor_scalar, r=skeys, w=[("maskq", tt), ("Amx", tt)], out=maskq[:, tt, 0:S], in0=score[:, tt, 0:S],
                     scalar1=1.0, scalar2=None, op0=ALU.mult, op1=ALU.max, accum_out=Amx[:, tt, 0:1])
                P.op("dve", V_.tensor_scalar, r=skeys, w=[("maskq", tt), ("Amn", tt)], out=maskq[:, tt, 0:S], in0=score[:, tt, 0:S],
                     scalar1=-1.0, scalar2=None, op0=ALU.mult, op1=ALU.max, accum_out=Amx[:, tt, 1:2])
                P.op("dve", V_.tensor_tensor, r=[("Amx", tt), ("Amn", tt)], w=[("Asm", tt)], out=Asm[:, tt:tt + 1], in0=Amx[:, tt, 0:1],
                     in1=Amx[:, tt, 1:2], op=ALU.max)
                P.op("pool", Q_.tensor_tensor, r=[("score", tt, 0), ("padb",)], w=[("score", tt, 0)], out=score[:, tt, 0:128],
                     in0=score[:, tt, 0:128], in1=padb[:, :], op=ALU.add)
                kbd = (S - 128) // 512
                P.op("pool", Q_.tensor_tensor, r=[("score", tt, kbd), ("cbias",)], w=[("score", tt, kbd)], out=score[:, tt, S - 128:S],
                     in0=score[:, tt, S - 128:S], in1=cbias[:, :], op=ALU.add)
                P.op("dve", V_.tensor_scalar, r=[("Asm", tt), ("pow2",)], w=[("wtab", tt)], out=wtab[:, tt, :], in0=pow2[:, :],
                     scalar1=Asm[:, tt:tt + 1], scalar2=None, op0=ALU.mult)
            for qh in range(2):
                s, wv = wnext("q%d" % qh)
                def ev(mi, b, qh=qh):
                    m = qh * 4 + mi
                    P.op("act", A_.copy, r=[("ps", b)], w=[("qT", m)], out=qT[:, m, 0:Gg], in_=psf[b][:, 0:Gg])
                linear_fm(wv, s, 4, rh, KC, Gg, ev, hkeys)
            for kh in range(2):
                s, wv = wnext("k%d" % kh)
                def ev(mi, b, kh=kh):
                    m = kh * 4 + mi
                    P.op("act", A_.copy, r=[("ps", b)], w=[("KT", m, T0 + 1 + t_) for t_ in range(NTg)],
                         out=KT[:, m, slot0:slot0 + Gg], in_=psf[b][:, 0:Gg])
                linear_fm(wv, s, 4, rh, KC, Gg, ev, hkeys)
            for vh in range(2):
                s, wv = wnext("v%d" % vh)
                for tt in range(NTg):
                    b = bank()
                    for k in range(KC):
                        P.op("pe", T_.matmul, r=[("ws", s), ("h16", k)], w=[("ps", b)], out=psf[b][:, :],
                             lhsT=hT16[:, k, tt * 128:(tt + 1) * 128], rhs=wv[:, k, :], start=(k == 0), stop=(k == KC - 1))
                    jt = T0 + 1 + tt
                    pv = psf[b][:, :].rearrange("p (h d) -> p h d", d=64)
                    P.op("act", A_.copy, r=[("ps", b)], w=[("V", jt, vh)], out=Vt[:, jt, vh * 8:(vh + 1) * 8, 0:64], in_=pv)
            allsk = [[("score", tt, kb) for kb in range((Ss[tt] + 511) // 512)] for tt in range(NTg)]
            split = (NTg == 2)
            P.op("dve", V_.memset, w=[("test",)], ap=test[:, :], constant=0.0)
            P.op("dve", V_.memset, w=[("thr",)], ap=thr[:, 0:1], constant=255.5)
            if split:
                P.op("dve", V_.memset, w=[("thr",)], ap=thr[:, 1:2], constant=float(511 - Ss[1]))
                P.op("dve", V_.tensor_scalar, r=[("wtab", 1)], w=[("wtab", 1)], out=wtab[:, 1, :], in0=wtab[:, 1, :], scalar1=-1.0,
                     scalar2=None, op0=ALU.mult)
            wk = [("wtab", tt) for tt in range(NTg)]
            for it in range(1, NIT + 1):
                for tt in range(NTg):
                    S = Ss[tt]
                    if split and tt == 1:
                        P.op("act", A_.activation, r=allsk[tt] + [("test",)], w=[("maskq", tt), ("cnt", tt)], out=maskq[:, tt, 0:S],
                             in_=score[:, tt, 0:S], func=AF.Sign, bias=test[:, 1:2], scale=1.0, accum_out=cnt[:, 1:2])
                    else:
                        P.op("dve", V_.tensor_scalar, r=allsk[tt] + [("test",)], w=[("maskq", tt), ("cnt", tt)], out=maskq[:, tt, 0:S],
                             in0=score[:, tt, 0:S], scalar1=test[:, tt:tt + 1], scalar2=None, op0=ALU.is_ge, op1=ALU.add,
                             accum_out=cnt[:, tt:tt + 1])
                ck = [("cnt", tt) for tt in range(NTg)]
                sub = 0.5 if it < NIT else 1.0
                dst, dkey = (test, "test") if it < NIT else (lo, "lo")
                P.op("dve", V_.tensor_tensor, r=ck + [("thr",)], w=[("sg",)], out=sg[:, 0:NTg], in0=cnt[:, 0:NTg], in1=thr[:, 0:NTg],
                     op=ALU.is_ge)
                P.op("dve", V_.scalar_tensor_tensor, r=[("sg",)] + wk, w=[("tmpb",)], out=tmpb[:, 0:NTg], in0=sg[:, 0:NTg], scalar=sub,
                     in1=wtab[:, 0:NTg, it], op0=ALU.subtract, op1=ALU.mult)
                P.op("dve", V_.tensor_tensor, r=[("tmpb",), ("test",)], w=[(dkey,)], out=dst[:, 0:NTg], in0=test[:, 0:NTg],
                     in1=tmpb[:, 0:NTg], op=ALU.add)
            if split:
                P.op("dve", V_.tensor_scalar, r=[("lo",)], w=[("lo",)], out=lo[:, 1:2], in0=lo[:, 1:2], scalar1=-1.0, scalar2=None,
                     op0=ALU.mult)
            for tt in range(NTg):
                S = Ss[tt]
                P.op("dve", V_.tensor_scalar, r=allsk[tt] + [("lo",)], w=[("maskq", tt)], out=maskq[:, tt, 0:S], in0=score[:, tt, 0:S],
                     scalar1=lo[:, tt:tt + 1], scalar2=None, op0=ALU.is_ge)
            if os.environ.get("K_STOPPH") == "C1":
                return
            P.op("act", A_.preload_act_table, func=AF.Exp)
            njt = T0 + NTg + 1
            for j in range(njt):
                tmin = max(0, j - 1 - T0)
                bb = bankb()
                for tt in range(tmin, NTg):
                    P.op("pe", T_.transpose, r=[("maskq", tt), ("identb",)], w=[("psb", bb)], out=psb[bb][:, tt * 128:(tt + 1) * 128],
                         in_=maskq[:, tt, j * 128:(j + 1) * 128], identity=identb[:])
                P.op("act", A_.copy, r=[("psb", bb)], w=[("maskT", j)], out=maskT[:, j, tmin * 128:Gg], in_=psb[bb][:, tmin * 128:Gg])
            if debug and (not is_meta) and bsel == 0 and gi == int(os.environ.get("K_DBG_GI", "0")):
                P.dma("sp", r=allsk[0], w=[("dbg", 0)], out=dbg[0][:, 0:Ss[0]], in_=score[:, 0, 0:Ss[0]])
                P.dma("sp", r=[("lo",)], w=[("dbg", 1)], out=dbg[1][:, 0:NTg], in_=lo[:, 0:NTg])
                P.dma("sp", r=[("cnt", 0)], w=[("dbg", 1, 1)], out=dbg[1][:, 8:8 + NTg], in_=cnt[:, 0:NTg])
            if os.environ.get("K_STOPPH") == "C":
                return
            nfull = T0 + 2
            units = []
            j = 0
            while j < nfull:
                if j + 1 < nfull and Gg == G and 2 * Gg <= 512:
                    units.append([j, j + 1])
                    j += 2
                else:
                    units.append([j])
                    j += 1
            for j in range(nfull, njt):
                units.append([j])
            items = [(h, ui) for h in range(NH) for ui in range(len(units))]
            st_info = {}
            head_bo = {}

            def emit_st(i):
                h, ui = items[i]
                unit = units[ui]
                c = h // 2
                half = h % 2
                zi = half * 2 + (c % 2)
                if ui == 0:
                    for hn in ([0, 1] if h == 0 else [h + 1]):
                        if hn < NH:
                            cn, hfn = hn // 2, hn % 2
                            zn = hfn * 2 + (cn % 2)
                            P.op("dve", V_.tensor_copy, r=[("qT", cn)], w=[("qz", zn)], out=qz[zn][hfn * 64:(hfn + 1) * 64, 0:Gg],
                                 in_=qT[hfn * 64:(hfn + 1) * 64, cn, 0:Gg])
                bs = bank()
                pi = i % NPT
                lo_col, hi_col = None, None
                for k_, j in enumerate(unit):
                    tmin = max(0, j - 1 - T0)
                    c0 = tmin * 128
                    base = k_ * Gg
                    nb = []
                    for tt in range(tmin, NTg):
                        T = T0 + tt
                        if j == 0:
                            if T == -1:
                                nb.append((tt, Bq[:, h, 0, :]))
                            elif T == 0:
                                nb.append((tt, Bqm[:, h, :]))
                        else:
                            dl = T + 1 - j
                            if dl in (0, 1):
                                nb.append((tt, Bq[:, h, dl, :]))
                    P.op("pe", T_.matmul, r=[("KT", c, j), ("qz", zi)], w=[("ps", bs)], out=psf[bs][:, base + c0:base + Gg],
                         lhsT=KT[:, c, j * 128:(j + 1) * 128], rhs=qz[zi][:, c0:Gg], start=True, stop=(len(nb) == 0))
                    for ii, (tt, bq) in enumerate(nb):
                        P.op("pe", T_.matmul, r=[("Bq",), ("Bqm",), ("identb",)], w=[("ps", bs)],
                             out=psf[bs][:, base + tt * 128:base + (tt + 1) * 128],
                             lhsT=bq, rhs=identb[:, :], start=False, stop=(ii == len(nb) - 1))
                    if lo_col is None:
                        lo_col = base + c0
                    hi_col = base + Gg
                P.op("act", A_.activation, r=[("ps", bs), ("cfar",)], w=[("Pt", pi)], out=Pt[pi][:, lo_col:hi_col], in_=psf[bs][:, lo_col:hi_col],
                     func=AF.Exp, scale=ATT_SCALE, bias=cfar[:, h:h + 1])
                if len(unit) == 2:
                    mview = maskT[:, unit[0]:unit[0] + 2, :].rearrange("p j g -> p (j g)")
                else:
                    mview = maskT[:, unit[0], lo_col:hi_col]
                me = "pool" if i % 3 == 0 else "dve"
                ME = Q_ if me == "pool" else V_
                P.op(me, ME.tensor_tensor, r=[("Pt", pi)] + [("maskT", j) for j in unit], w=[("Pt", pi)], out=Pt[pi][:, lo_col:hi_col],
                     in0=Pt[pi][:, lo_col:hi_col], in1=mview, op=ALU.mult)
                st_info[i] = pi

            def emit_pv(i):
                h, ui = items[i]
                unit = units[ui]
                pi = st_info[i]
                if ui == 0:
                    bo = bank2()
                    head_bo[h] = bo
                    P.op("pe", T_.matmul, r=[("zerob",)], w=[("ps", bo)], out=psf[bo][:, 0:NTg * 65], lhsT=zerob[:, 0:128],
                         rhs=zerob[:, 0:NTg * 65], start=True, stop=False)
                bo = head_bo[h]
                for k_, j in enumerate(unit):
                    tmin = max(0, j - 1 - T0)
                    base = k_ * Gg
                    for tt in range(tmin, NTg):
                        last = (ui == len(units) - 1 and k_ == len(unit) - 1 and tt == NTg - 1)
                        P.op("pe", T_.matmul, r=[("Pt", pi), ("V", j, h // 8), ("Vones",)], w=[("ps", bo)],
                             out=psf[bo][:, tt * 65:(tt + 1) * 65], lhsT=Pt[pi][:, base + tt * 128:base + (tt + 1) * 128], rhs=Vt[:, j, h, :],
                             start=False, stop=last)
                if ui == len(units) - 1:
                    ov = psf[bo][:, 0:NTg * 65].rearrange("p (t d) -> p t d", d=65)
                    ri = h % 2
                    P.op("dve", V_.reciprocal, r=[("ps", bo)], w=[("rec", ri)], out=rec[ri][:, 0:NTg], in_=ov[:, :, 64])
                    for tt in range(NTg):
                        P.op("dve", V_.tensor_scalar, r=[("ps", bo), ("rec", ri)], w=[("attn", tt, h // 2)],
                             out=attn[:, tt, h * 64:(h + 1) * 64], in0=ov[:, tt, 0:64], scalar1=rec[ri][:, tt:tt + 1], scalar2=None,
                             op0=ALU.mult)

            SK = 4
            for i in range(len(items) + SK):
                if i < len(items):
                    emit_st(i)
                if i - SK >= 0:
                    emit_pv(i - SK)
            if os.environ.get("K_STOPPH") == "D":
                return
            prefetch_x(pidx + 1)
            P.op("act", A_.preload_act_table, func=AF.Silu)
            for tt in range(NTg):
                bb = bankb()
                for kc in range(KC):
                    P.op("pe", T_.transpose, r=[("attn", tt, kc), ("identb",)], w=[("psb", bb)], out=psb[bb][:, kc * 128:(kc + 1) * 128],
                         in_=attn[:, tt, kc * 128:(kc + 1) * 128], identity=identb[:])
                P.op("act", A_.copy, r=[("psb", bb)], w=[("mixT", kc) for kc in range(KC)], out=mixT[:, :, tt * 128:(tt + 1) * 128],
                     in_=psb[bb][:, :].rearrange("p (k n) -> p k n", n=128))
            mkeys = [("mixT", k) for k in range(KC)]
            rm = lambda k: mixT[:, k, 0:Gg]
            for oh_ in range(2):
                s, wv = wnext("wo%d" % oh_)
                def ev(mi, b, oh_=oh_):
                    m = oh_ * 4 + mi
                    P.op("dve", V_.scalar_tensor_tensor, r=[("ps", b), ("h32", m)], w=[("h32", m)], out=hT32[:, m, 0:Gg],
                         in0=hT32[:, m, 0:Gg], scalar=ALPHA, in1=psf[b][:, 0:Gg], op0=ALU.mult, op1=ALU.add)
                linear_fm(wv, s, 4, rm, KC, Gg, ev, mkeys)
            if os.environ.get("K_STOPPH") == "E":
                return
            def dump(idx):
                if debug and (not is_meta) and bsel == 0 and gi == int(os.environ.get("K_DBG_GI", "0")):
                    P.dma("sp", r=[("h32", k) for k in range(KC)], w=[("dbg", idx)],
                          out=dbg[idx][:, 0:KC * Gg].rearrange("p (k g) -> p k g", g=Gg), in_=hT32[:, :, 0:Gg])
            dump(2)
            layer_norm(hT32, "h32", 0, 1, lnp, "main", Gg, NTg)
            dump(3)
            ffn(0, Gg, NTg)
            dump(6)
            layer_norm(hT32, "h32", 2, 3, lnp, "main", Gg, NTg)
            dump(4)
            if is_meta:
                P.op("pool", Q_.memset, w=[("uT", k) for k in range(KC)], ap=uT[:, :, 0:30], constant=0.0)
            else:
                if gi == 0:
                    P.op("pool", Q_.tensor_copy, r=[("ucarry_meta",)], w=[("uT", k) for k in range(KC)], out=uT[:, :, 0:30], in_=ucarry_meta[:, :, :])
                else:
                    P.op("pool", Q_.tensor_copy, r=[("ucarry",)], w=[("uT", k) for k in range(KC)], out=uT[:, :, 0:30], in_=ucarry[:, :, :])
            for i in range(4):
                s, wv = wnext("pw1_%d" % i)
                for mi in range(2):
                    cg = i * 2 + mi
                    bv, bg = bank(), bank()
                    for k in range(KC):
                        P.op("pe", T_.matmul, r=[("ws", s), ("h16", k)], w=[("ps", bv)], out=psf[bv][:, 0:Gg],
                             lhsT=wv[:, k, mi * 128:(mi + 1) * 128], rhs=hT16[:, k, 0:Gg], start=(k == 0), stop=(k == KC - 1))
                    for k in range(KC):
                        P.op("pe", T_.matmul, r=[("ws", s), ("h16", k)], w=[("ps", bg)], out=psf[bg][:, 0:Gg],
                             lhsT=wv[:, k, 256 + mi * 128:256 + (mi + 1) * 128], rhs=hT16[:, k, 0:Gg], start=(k == 0), stop=(k == KC - 1))
                    si = cg % 2
                    P.op("act", A_.activation, r=[("ps", bg), ("cpar", 5)], w=[("sgt", si)], out=sgt[si][:, 0:Gg], in_=psf[bg][:, 0:Gg],
                         func=AF.Tanh, bias=hb1g[:, cg:cg + 1], scale=0.5)
                    P.op("pool", Q_.tensor_scalar, r=[("sgt", si)], w=[("sgt", si)], out=sgt[si][:, 0:Gg], in0=sgt[si][:, 0:Gg],
                         scalar1=0.5, scalar2=0.5, op0=ALU.mult, op1=ALU.add)
                    P.op("dve", V_.scalar_tensor_tensor, r=[("ps", bv), ("sgt", si), ("cpar", 4)], w=[("uT", cg)],
                         out=uT[:, cg, 30:30 + Gg], in0=psf[bv][:, 0:Gg], scalar=cpar[:, 4, cg:cg + 1], in1=sgt[si][:, 0:Gg],
                         op0=ALU.add, op1=ALU.mult)
            ukeys = [("uT", k) for k in range(KC)]
            if is_meta:
                P.op("pool", Q_.tensor_copy, r=ukeys, w=[("ucarry_meta",)], out=ucarry_meta[:, :, :], in_=uT[:, :, 16:46])
                return
            P.op("pool", Q_.tensor_copy, r=ukeys, w=[("ucarry",)], out=ucarry[:, :, :], in_=uT[:, :, Gg:Gg + 30])
            for c in range(8):
                s, wv = wnext("dg%d" % c)
                b = bank()
                for j in range(31):
                    P.op("pe", T_.matmul, r=[("ws", s), ("uT", c)], w=[("ps", b)], out=psf[b][:, 0:Gg], lhsT=wv[:, j, :],
                         rhs=uT[:, c, j:j + Gg], start=(j == 0), stop=(j == 30))
                P.op("act", A_.activation, r=[("ps", b), ("cpar", 2)], w=[("yT32", c)], out=yT32[:, c, 0:Gg], in_=psf[b][:, 0:Gg],
                     func=AF.Identity, bias=cpar[:, 2, c:c + 1], scale=1.0)
            layer_norm(yT32, "yT32", 0, 1, cpar, "conv", Gg, NTg)
            for oh_ in range(2):
                s, wv = wnext("pw2_%d" % oh_)
                def ev(mi, b, oh_=oh_):
                    m = oh_ * 4 + mi
                    ti = m % 2
                    P.op("act", A_.activation, r=[("ps", b), ("cpar", 3)], w=[("tmpe", ti)], out=tmpe[ti][:, 0:Gg], in_=psf[b][:, 0:Gg],
                         func=AF.Identity, bias=cpar[:, 3, m:m + 1], scale=1.0)
                    P.op("dve", V_.scalar_tensor_tensor, r=[("tmpe", ti), ("h32", m)], w=[("h32", m)], out=hT32[:, m, 0:Gg],
                         in0=hT32[:, m, 0:Gg], scalar=ALPHA, in1=tmpe[ti][:, 0:Gg], op0=ALU.mult, op1=ALU.add)
                linear_fm(wv, s, 4, rm, KC, Gg, ev, mkeys)
            dump(5)
            layer_norm(hT32, "h32", 4, 5, lnp, "main", Gg, NTg)
            dump(7)
            ffn(1, Gg, NTg)
            orow = lambda tt: out[bsel, gi * G + tt * 128: gi * G + (tt + 1) * 128, :]
            layer_norm(hT32, "h32", 6, 7, lnp, "final", Gg, NTg, out_rows=orow)

        for pi_, (kind, bsel, gi) in enumerate(passes):
            if isinstance(stop_after, int) and pi_ >= stop_after:
                break
            emit_group(kind, bsel, gi, pi_)
        P.flush()
        P.final_wait()
        if info is not None:
            info["ms"] = dict(P.ms_count)
            info["dmax"] = max(16 * c for c in P.dcount)
    return nc


_CACHE = {}


def kernel(**inputs):
    n_cores = 8
    x = np.ascontiguousarray(inputs["x"], dtype=np.float32)
    B = x.shape[0]
    per = B // n_cores
    consts = host_consts()
    if "nc" not in _CACHE:
        _CACHE["nc"] = build_program(n_seq=per)
    nc = _CACHE["nc"]
    shared = {k: np.ascontiguousarray(v, dtype=np.float32) for k, v in inputs.items() if k != "x"}
    in_maps = []
    for c in range(n_cores):
        m = dict(shared)
        m.update(consts)
        m["x"] = np.ascontiguousarray(x[c * per:(c + 1) * per])
        in_maps.append(m)
    res = run_bass_kernel_spmd(nc, in_maps, core_ids=list(range(n_cores)))
    outs = [np.asarray(r["out"], dtype=np.float32) for r in res.results]
    return np.concatenate(outs, axis=0)
```

```python
import math
import os
from functools import partial
from contextlib import ExitStack
import numpy as np
import concourse.bass as bass
import concourse.mybir as mybir
from concourse.bass_utils import run_bass_kernel_spmd

F32 = mybir.dt.float32
BF16 = mybir.dt.bfloat16
AF = mybir.ActivationFunctionType
ALU = mybir.AluOpType

D = 1024
KC = 8
NH = 16
NIH = 8
DFF = 2816
FC = 22
SEQ = 2048
NSLOT = 2176
NKT = 17
NT = 2
G = NT * 128
NGRP = SEQ // G
NS = 4
WSZ = 4096
NIT = 12
ALPHA = 4.0 ** 0.25
WI_SCALE = (NIH ** -0.5) * (64 ** -0.5)
ATT_SCALE = 0.125
LN_EPS = 1e-5
NEG = -1.0e30
SAME_ENG_SYNC = True
USE_POOL_POW = True
NCH = 64


class Op:
    __slots__ = ("eng", "fn", "is_dma", "deps", "flag", "ms", "dsem", "dval", "idx")


class Prog:
    def __init__(self, nc, es, n_dma_sems=24):
        self.nc = nc
        self.eng = {"pe": nc.tensor, "act": nc.scalar, "dve": nc.vector, "pool": nc.gpsimd, "sp": nc.sync}
        self.sem = {e: es.enter_context(nc.semaphore("ms_" + e)) for e in self.eng}
        self.dsems = [es.enter_context(nc.semaphore("dq%d" % i)) for i in range(n_dma_sems)]
        self.dcount = [0] * n_dma_sems
        self.dlast = [None] * n_dma_sems
        self.dnext = 0
        self.ops = []
        self.ms_count = {e: 0 for e in self.eng}
        self.waited = {}
        self.lw = {}
        self.rd = {}
        self.rd_dma = {}
        self.names = {}
        self.nfloor = {}
        self.gfloor = []
        self.floor_done = set()
        self.last_on_eng = {}
        self.nops = 0

    def _add(self, eng, fn, r, w, is_dma):
        o = Op()
        o.eng = eng
        o.fn = fn
        o.is_dma = is_dma
        o.flag = False
        o.ms = None
        o.dsem = None
        o.dval = 0
        o.idx = self.nops
        self.nops += 1
        deps = []
        for k in r:
            if k[0] in ("ps", "psb"):
                rr = self.rd.get(k)
                if rr:
                    deps.extend(o2 for e2, o2 in rr.items() if e2 != eng)
        if eng not in self.floor_done:
            deps.extend(self.gfloor)
            self.floor_done.add(eng)
        for k in r:
            d = self.lw.get(k)
            if d is not None:
                deps.append(d)
        for k in w:
            d = self.lw.get(k)
            if d is not None:
                deps.append(d)
            rr = self.rd.get(k)
            if rr:
                deps.extend(rr.values())
            rl = self.rd_dma.get(k)
            if rl:
                deps.extend(rl)
            nf = self.nfloor.get(k[0])
            if nf is not None and eng not in nf[1]:
                deps.extend(nf[0])
                nf[1].add(eng)
        if is_dma:
            s = self.dnext % len(self.dsems)
            self.dnext += 1
            if self.dlast[s] is not None:
                deps.append(self.dlast[s])
            self.dcount[s] += 1
            o.dsem = self.dsems[s]
            o.dval = 16 * self.dcount[s]
            self.dlast[s] = o
        fd = []
        seen = set()
        for d in deps:
            if d is o or id(d) in seen:
                continue
            seen.add(id(d))
            if not d.is_dma and d.eng == eng:
                if eng == "pe" or eng == "sp" or not SAME_ENG_SYNC:
                    continue
            if not d.is_dma:
                d.flag = True
            fd.append(d)
        o.deps = fd
        for k in w:
            self.lw[k] = o
            self.rd[k] = {}
            self.rd_dma[k] = []
            self.names.setdefault(k[0], set()).add(k)
        for k in r:
            if k in w:
                continue
            self.names.setdefault(k[0], set()).add(k)
            if is_dma:
                self.rd_dma.setdefault(k, []).append(o)
            else:
                self.rd.setdefault(k, {})[eng] = o
        self.ops.append(o)
        if not is_dma:
            self.last_on_eng[eng] = o
        return o

    def op(self, eng, _f, r=(), w=(), **kw):
        return self._add(eng, partial(_f, **kw), tuple(r), tuple(w), False)

    def dma(self, q, r=(), w=(), **kw):
        return self._add(q, partial(self.eng[q].dma_start, **kw), tuple(r), tuple(w), True)

    def fence(self, from_names, to_names):
        acc = []
        for n in from_names:
            for k in self.names.get(n, ()):
                d = self.lw.get(k)
                if d is not None:
                    acc.append(d)
                rr = self.rd.get(k)
                if rr:
                    acc.extend(rr.values())
                rl = self.rd_dma.get(k)
                if rl:
                    acc.extend(rl)
        best = {}
        dm = []
        for d in acc:
            if d.is_dma:
                dm.append(d)
            else:
                b = best.get(d.eng)
                if b is None or d.idx > b.idx:
                    best[d.eng] = d
        lst = list(best.values()) + dm
        for n in to_names:
            prev = self.nfloor.get(n)
            self.nfloor[n] = ((list(prev[0]) if prev else []) + lst, set())

    def flush(self):
        for e, o in self.last_on_eng.items():
            o.flag = True
        for o in self.ops:
            E = self.eng[o.eng]
            for d in o.deps:
                if d.is_dma:
                    sem, val = d.dsem, d.dval
                else:
                    assert d.ms is not None, "dep on unflagged op"
                    sem, val = self.sem[d.eng], d.ms
                key = (o.eng, id(sem))
                if self.waited.get(key, 0) < val:
                    E.wait_ge(sem, val)
                    self.waited[key] = val
            ins = o.fn()
            if o.is_dma:
                ins.then_inc(o.dsem, 16)
            elif o.flag:
                self.ms_count[o.eng] += 1
                o.ms = self.ms_count[o.eng]
                ins.then_inc(self.sem[o.eng], 1)
            o.fn = None
        self.gfloor = [o for o in self.last_on_eng.values()] + [d for d in self.dlast if d is not None]
        self.floor_done = set()
        self.ops = []
        self.lw = {}
        self.rd = {}
        self.rd_dma = {}
        self.names = {}
        self.nfloor = {}

    def final_wait(self):
        E = self.eng["sp"]
        for s, d in enumerate(self.dlast):
            if d is not None:
                E.wait_ge(d.dsem, d.dval)
        for e, o in self.last_on_eng.items():
            if o.ms is not None and e != "sp":
                E.wait_ge(self.sem[e], o.ms)


def chunk_table():
    ch = []
    ch.append(dict(kind="std", src="w_in", c0=3072, w=512, nk=8, nm="qi"))
    ch.append(dict(kind="kiwi", nk=8, w=136, nm="kiwi"))
    for nm, c0 in (("q0", 0), ("q1", 512), ("k0", 1024), ("k1", 1536)):
        ch.append(dict(kind="std", src="w_in", c0=c0, w=512, nk=8, nm=nm))
    ch.append(dict(kind="std", src="w_in", c0=2048, w=512, nk=8, nm="v0"))
    ch.append(dict(kind="std", src="w_in", c0=2560, w=512, nk=8, nm="v1"))
    ch.append(dict(kind="std", src="w_o", c0=0, w=512, nk=8, nm="wo0"))
    ch.append(dict(kind="std", src="w_o", c0=512, w=512, nk=8, nm="wo1"))
    for l in range(2):
        if l == 1:
            for i in range(4):
                ch.append(dict(kind="pw1", i=i, nk=8, w=512, nm="pw1_%d" % i))
            for c in range(8):
                ch.append(dict(kind="diag", c=c, nk=31, w=128, nm="dg%d" % c))
            ch.append(dict(kind="std", src="w_pw2", c0=0, w=512, nk=8, nm="pw2_0"))
            ch.append(dict(kind="std", src="w_pw2", c0=512, w=512, nk=8, nm="pw2_1"))
        for j in range(6):
            w = 512 if j < 5 else 256
            ch.append(dict(kind="std", src="gate%d" % l, c0=j * 512, w=w, nk=8, nm="g%d_%d" % (l, j)))
            ch.append(dict(kind="std", src="up%d" % l, c0=j * 512, w=w, nk=8, nm="u%d_%d" % (l, j)))
        for m in range(8):
            ch.append(dict(kind="std", src="down%d" % l, c0=m * 128, w=128, nk=22, nm="d%d_%d" % (l, m)))
    assert len(ch) == NCH, len(ch)
    return ch


def t5_bucket_np(d):
    d = np.asarray(d, dtype=np.int64)
    dm = np.maximum(d, 1).astype(np.float32)
    large = 16 + (np.log(dm / np.float32(16)) / np.float32(math.log(128 / 16)) * np.float32(16)).astype(np.int32)
    large = np.minimum(large, 31)
    return np.where(d < 16, d, large)


def host_consts():
    c = {}
    c["ident"] = np.eye(128, dtype=np.float32)
    j = np.arange(384)
    dist = np.maximum(255 - j, 0)
    b = t5_bucket_np(dist)
    oh = np.zeros((32, 384), np.float32)
    oh[b, j] = 1.0
    c["ohr"] = oh
    q = np.arange(128)[:, None]
    k = np.arange(128)[None, :]
    c["cbias"] = np.where(k <= q, 0.0, NEG).astype(np.float32)
    c["padb"] = np.broadcast_to(np.where(k >= 16, NEG, 0.0), (128, 128)).astype(np.float32).copy()
    c["pow2"] = np.broadcast_to((2.0 ** (1.0 - np.arange(NIT + 1)))[None, :], (128, NIT + 1)).astype(np.float32).copy()
    return c


def build_program(n_seq=2, debug=False, stop_after=None, info=None):
    nc = bass.Bass("TRN2", target_bir_lowering=False)
    es = ExitStack()
    with es:
        es.enter_context(nc.allow_non_contiguous_dma(reason="small param/layout loads"))
        es.enter_context(nc.allow_low_precision(reason="bf16 matmul operands by design"))

        def din(name, shape, dt=F32):
            return nc.dram_tensor(name, list(shape), dt, kind="ExternalInput").ap()

        x = din("x", [n_seq, SEQ, D])
        meta = din("meta_tokens", [16, D])
        relb = din("rel_bias", [32, 16])
        w_in = din("w_in_attn", [1, D, 3656])
        w_o = din("w_o_attn", [1, D, D])
        w_pw1 = din("w_pw1", [1, D, 2 * D])
        b_pw1 = din("b_pw1", [1, 2 * D])
        w_dw = din("w_dw", [1, 31, D])
        b_dw = din("b_dw", [1, D])
        cln_g = din("conv_ln_g", [1, D])
        cln_b = din("conv_ln_b", [1, D])
        w_pw2 = din("w_pw2", [1, D, D])
        b_pw2 = din("b_pw2", [1, D])
        ln1_g = din("ln1_g", [2, D])
        ln1_b = din("ln1_b", [2, D])
        w_gate = din("ffn_w_gate", [2, D, DFF])
        w_up = din("ffn_w_up", [2, D, DFF])
        w_down = din("ffn_w_down", [2, DFF, D])
        ln2_g = din("ln2_g", [2, D])
        ln2_b = din("ln2_b", [2, D])
        c_ident = din("ident", [128, 128])
        c_ohr = din("ohr", [32, 384])
        c_cbias = din("cbias", [128, 128])
        c_padb = din("padb", [128, 128])
        c_pow2 = din("pow2", [128, NIT + 1])
        out = nc.dram_tensor("out", [n_seq, SEQ, D], F32, kind="ExternalOutput").ap()
        wscr = nc.dram_tensor("wscr", [NCH, 128, WSZ], BF16, kind="Internal").ap()
        tbd = nc.dram_tensor("tbd", [16, 384], F32, kind="Internal").ap()
        ZH = 128 * 385
        ztd = nc.dram_tensor("ztd", [16, ZH], F32, kind="Internal").ap()
        dbg = None
        if debug:
            dbg = nc.dram_tensor("dbg", [8, 128, 2176], F32, kind="ExternalOutput").ap()

        wsrc = {"w_in": w_in[0], "w_o": w_o[0], "w_pw2": w_pw2[0],
                "gate0": w_gate[0], "gate1": w_gate[1], "up0": w_up[0], "up1": w_up[1],
                "down0": w_down[0], "down1": w_down[1]}
        chunks = chunk_table()
        cid = {c["nm"]: i for i, c in enumerate(chunks)}

        P = Prog(nc, es)
        V_, A_, S_, T_, Q_ = nc.vector, nc.scalar, nc.sync, nc.tensor, nc.gpsimd

        def sb(name, shape, dt):
            return es.enter_context(nc.sbuf_tensor("s_" + name, list(shape), dt))

        ident32 = sb("ident32", [128, 128], F32)
        identb = sb("identb", [128, 128], BF16)
        P.dma("sp", w=[("ident32",)], out=ident32[:], in_=c_ident[:, :])
        P.op("dve", V_.tensor_copy, r=[("ident32",)], w=[("identb",)], out=identb[:], in_=ident32[:])

        with ExitStack() as es2:
            NPB = 4
            stg = [es2.enter_context(nc.sbuf_tensor("stg%d" % i, [128, WSZ], F32)) for i in range(NPB)]
            cvt = [es2.enter_context(nc.sbuf_tensor("cvt%d" % i, [128, WSZ], BF16)) for i in range(NPB)]
            wT = es2.enter_context(nc.sbuf_tensor("wdwT", [128, 8, 31], F32))
            tbs = es2.enter_context(nc.sbuf_tensor("tbs_p", [16, 384], F32))
            rb_sb = es2.enter_context(nc.sbuf_tensor("rb_p", [32, 16], F32))
            oh_sb = es2.enter_context(nc.sbuf_tensor("oh_p", [32, 384], F32))
            pst = es2.enter_context(nc.psum_tensor("pst_p", [128, 512], F32))
            P.dma("act", w=[("rb_sb",)], out=rb_sb[:], in_=relb[:, :])
            P.dma("act", w=[("oh_sb",)], out=oh_sb[:], in_=c_ohr[:, :])
            P.op("pe", T_.matmul, r=[("rb_sb",), ("oh_sb",)], w=[("ps", 99)],
                 out=pst[0:16, 0:384], lhsT=rb_sb[:, :], rhs=oh_sb[:, :], start=True, stop=True)
            P.op("dve", V_.tensor_copy, r=[("ps", 99)], w=[("tbs0",)], out=tbs[:, :], in_=pst[0:16, 0:384])
            P.op("dve", V_.tensor_scalar, r=[("tbs0",)], w=[("tbs",)], out=tbs[:, :], in0=tbs[:, :],
                 scalar1=tbs[:, 0:1], scalar2=1.0 / ATT_SCALE, op0=ALU.subtract, op1=ALU.mult)
            P.dma("act", r=[("tbs",)], w=[("tbd",)], out=tbd[:, :], in_=tbs[:, :])
            for h in range(NH):
                P.dma("act", r=[("tbd",)], w=[("ztd", h)], out=bass.AP(ztd.tensor, h * ZH, [[385, 128], [1, 384]]),
                      in_=bass.AP(tbd.tensor, h * 384, [[0, 128], [1, 384]]))
            import os
            for c in range(8):
                if os.environ.get("K_NOWT"):
                    break
                src = bass.AP(w_dw.tensor, c * 128, [[1, 128], [D, 31]])
                P.dma("sp", w=[("wdwT", c)], out=wT[:, c, :], in_=src)
            for i, c in enumerate(chunks):
                if os.environ.get("K_PRE") and c["nm"] not in os.environ["K_PRE"].split(","):
                    continue
                b = i % NPB
                free = c["nk"] * c["w"]
                sv = stg[b][:, :free].rearrange("p (k n) -> p k n", n=c["w"])
                cv = cvt[b][:, :free]
                if c["kind"] == "std":
                    W = wsrc[c["src"]]
                    src = W.rearrange("(k p) n -> p k n", p=128)[:, :, c["c0"]:c["c0"] + c["w"]]
                    P.dma("sp", w=[("stg", b)], out=sv, in_=src)
                elif c["kind"] == "kiwi":
                    W = w_in[0].rearrange("(k p) n -> p k n", p=128)
                    P.dma("sp", w=[("stg", b)], out=sv[:, :, 0:64], in_=W[:, :, 3584:3648])
                    P.dma("sp", w=[("stg", b)], out=sv[:, :, 64:128], in_=W[:, :, 3584:3648])
                    P.dma("sp", w=[("stg", b)], out=sv[:, :, 128:136], in_=W[:, :, 3648:3656])
                elif c["kind"] == "pw1":
                    W = w_pw1[0].rearrange("(k p) n -> p k n", p=128)
                    ii = c["i"]
                    P.dma("sp", w=[("stg", b)], out=sv[:, :, 0:256], in_=W[:, :, ii * 256:(ii + 1) * 256])
                    P.dma("sp", w=[("stg", b)], out=sv[:, :, 256:512], in_=W[:, :, D + ii * 256:D + (ii + 1) * 256])
                if c["kind"] == "diag":
                    cc = c["c"]
                    cv3 = cv.rearrange("p (k n) -> p k n", n=128)
                    P.op("dve", V_.tensor_tensor, r=[("wdwT", cc), ("ident32",)], w=[("cvt", b, j) for j in range(31)],
                         out=cv3, in0=ident32[:, :].unsqueeze(1).to_broadcast([128, 31, 128]),
                         in1=wT[:, cc, :].unsqueeze(2).to_broadcast([128, 31, 128]), op=ALU.mult)
                    wkeys = [("cvt", b, j) for j in range(31)]
                    P.dma("act", r=wkeys, w=[("wscr", i)], out=wscr[i][:, :free], in_=cv)
                else:
                    wkeys = [("cvt", b, j) for j in range(31)]
                    if (i // 2) % 2 == 0:
                        P.op("dve", V_.tensor_copy, r=[("stg", b)], w=wkeys, out=cv, in_=stg[b][:, :free])
                    else:
                        P.op("act", A_.copy, r=[("stg", b)], w=wkeys, out=cv, in_=stg[b][:, :free])
                    P.dma("act", r=wkeys, w=[("wscr", i)], out=wscr[i][:, :free], in_=cv)
            P.flush()
        if stop_after == "prepass":
            P.final_wait()
            return nc

        KT = sb("KT", [128, KC, NSLOT], BF16)
        Vt = sb("Vt", [128, NKT, NH, 65], BF16)
        kiT2 = sb("kiT2", [128, NSLOT], BF16)
        hT32 = sb("hT32", [128, KC, G], F32)
        hT16 = sb("hT16", [128, KC, G], BF16)
        xnb = [sb("xn%d" % i, [128, D], F32) for i in range(2)]
        qz = [sb("qz%d" % i, [128, G], BF16) for i in range(4)]
        qiz = [sb("qiz%d" % i, [128, 128], BF16) for i in range(8)]
        wi = sb("wi", [128, NT, 8], F32)
        mixT = sb("mixT", [128, KC, G], BF16)
        wslot = [sb("ws%d" % i, [128, WSZ], BF16) for i in range(NS)]
        ARENA_BYTES = 58 * 1024
        arena = sb("arena", [128, ARENA_BYTES // 2], BF16)
        off = [0]

        def carve(nelem, dt, shape=None, base=None):
            nb = nelem * (4 if dt == F32 else 2)
            o = off[0] if base is None else base
            assert o % 4 == 0
            if base is None:
                off[0] += (nb + 3) // 4 * 4
            assert o + nb <= ARENA_BYTES, (o, nb)
            v = arena[:, o // 2:(o + nb) // 2]
            if dt == F32:
                v = v.bitcast(F32)
            return v

        score_f = carve(NT * NSLOT, F32)
        score = score_f.rearrange("p (t s) -> p t s", s=NSLOT)
        maskq_f = carve(NT * NSLOT, BF16)
        maskq = maskq_f.rearrange("p (t s) -> p t s", s=NSLOT)
        maskT = carve(NKT * G, BF16).rearrange("p (j g) -> p j g", g=G)
        qT = carve(KC * G, BF16).rearrange("p (k g) -> p k g", g=G)
        qiT = carve(4 * G, BF16).rearrange("p (k g) -> p k g", g=G)
        NRB = 5
        off_rb = off[0]
        Rb = [carve(512, BF16) for _ in range(NRB)]
        dgw = carve(NT * NIH * 128, BF16).rearrange("p (t h k) -> p t h k", h=NIH, k=128)
        NPT = 5
        Pt = [carve(512, BF16) for _ in range(NPT)]
        attn = carve(NT * D, BF16).rearrange("p (t d) -> p t d", d=D)
        endA = off[0]
        xin = carve(NT * D, F32, base=off_rb).rearrange("p (t d) -> p t d", d=D)
        off[0] = 0
        actT = carve(FC * G, BF16).rearrange("p (k g) -> p k g", g=G)
        sgt = [carve(G, F32) for _ in range(2)]
        uT = carve(KC * (30 + G), BF16).rearrange("p (k g) -> p k g", g=30 + G)
        yT32 = carve(KC * G, F32).rearrange("p (k g) -> p k g", g=G)
        tmpe = [carve(G, F32) for _ in range(2)]
        endB = off[0]
        A_NAMES = ["score", "maskq", "maskT", "qT", "qiT", "R", "Pt", "attn", "dgw"]
        B_NAMES = ["actT", "sgt", "uT", "yT32", "tmpe"]

        ucarry = sb("ucarry", [128, KC, 30], BF16)
        ucarry_meta = sb("ucarry_meta", [128, KC, 30], BF16)
        Asm = sb("Asm", [128, NT], F32)
        Amx = sb("Amx", [128, NT, 2], F32)
        wtab = sb("wtab", [128, NT, NIT + 1], F32)
        test = sb("test", [128, NT], F32)
        cnt = sb("cnt", [128, NT], F32)
        sg = sb("sg", [128, NT], F32)
        tmpb = sb("tmpb", [128, NT], F32)
        lo = sb("lo", [128, NT], F32)
        thr = sb("thr", [128, NT], F32)
        mhalf = sb("mhalf", [128, NT], F32)
        hb1g = sb("hb1g", [128, 8], F32)
        rec = [sb("rec%d" % i, [128, NT], F32) for i in range(2)]
        stats = sb("stats", [128, NT, 12], F32)
        mv = sb("mv", [128, NT, 2], F32)
        vpe = sb("vpe", [128, NT], F32)
        rstd = sb("rstd", [128, NT], F32)
        nmr = sb("nmr", [128, NT], F32)
        lnp = sb("lnp", [128, 8, 8], F32)
        cpar = sb("cpar", [128, 6, 8], F32)
        Bq = sb("Bq", [128, NH, 2, 128], BF16)
        Bqm = sb("Bqm", [128, NH, 128], BF16)
        cfar = sb("cfar", [128, NH], F32)
        cbias = sb("cbias", [128, 128], F32)
        padb = sb("padb", [128, 128], F32)
        pow2 = sb("pow2", [128, NIT + 1], F32)
        zerob = sb("zerob", [128, 256], BF16)
        psf = [es.enter_context(nc.psum_tensor("psf%d" % i, [128, 512], F32)) for i in range(6)]
        psb = [es.enter_context(nc.psum_tensor("psb%d" % i, [128, 1024], BF16)) for i in range(2)]
        bctr = [0, 0]

        def bank():
            i = bctr[0] % 4
            bctr[0] += 1
            return i

        b2ctr = [0]

        def bank2():
            i = 4 + b2ctr[0] % 2
            b2ctr[0] += 1
            return i

        def bankb():
            i = bctr[1] % 2
            bctr[1] += 1
            return i

        def ld_pp(dst, src_vec, key):
            P.dma("sp", w=[key], out=dst, in_=src_vec.rearrange("(c p) -> p c", p=128))

        for l in range(2):
            ld_pp(lnp[:, 4 * l + 0, :], ln1_g[l], ("lnp", 4 * l + 0))
            ld_pp(lnp[:, 4 * l + 1, :], ln1_b[l], ("lnp", 4 * l + 1))
            ld_pp(lnp[:, 4 * l + 2, :], ln2_g[l], ("lnp", 4 * l + 2))
            ld_pp(lnp[:, 4 * l + 3, :], ln2_b[l], ("lnp", 4 * l + 3))
        ld_pp(cpar[:, 0, :], cln_g[0], ("cpar", 0))
        ld_pp(cpar[:, 1, :], cln_b[0], ("cpar", 1))
        ld_pp(cpar[:, 2, :], b_dw[0], ("cpar", 2))
        ld_pp(cpar[:, 3, :], b_pw2[0], ("cpar", 3))
        ld_pp(cpar[:, 4, :], b_pw1[0][0:D], ("cpar", 4))
        ld_pp(cpar[:, 5, :], b_pw1[0][D:2 * D], ("cpar", 5))
        P.dma("sp", w=[("cbias",)], out=cbias[:], in_=c_cbias[:, :])
        P.dma("sp", w=[("padb",)], out=padb[:], in_=c_padb[:, :])
        P.dma("sp", w=[("pow2",)], out=pow2[:], in_=c_pow2[:, :])
        P.dma("sp", w=[("cfar",)], out=cfar[:], in_=bass.AP(relb.tensor, 31 * 16, [[0, 128], [1, 16]]))
        P.op("pool", Q_.memset, w=[("zerob",)], ap=zerob[:], constant=0.0)
        P.op("dve", V_.tensor_scalar, r=[("cpar", 5)], w=[("hb1g",)], out=hb1g[:, :], in0=cpar[:, 5, :], scalar1=0.5, scalar2=None,
             op0=ALU.mult)
        P.op("pool", Q_.memset, w=[("mhalf",)], ap=mhalf[:], constant=-0.5)
        for i in range(4):
            P.op("pool", Q_.memset, w=[("qz", i)], ap=qz[i][:], constant=0.0)
        for i in range(8):
            P.op("pool", Q_.memset, w=[("qiz", i)], ap=qiz[i][:], constant=0.0)
        P.op("pool", Q_.memset, w=[("Vones",)], ap=Vt[:, :, :, 64:65], constant=1.0)
        stgs = [xnb[0][:, :].rearrange("p (h k) -> p h k", k=128), xnb[1][:, :].rearrange("p (h k) -> p h k", k=128),
                hT32[:, :, 0:128], hT32[:, :, 128:256],
                score_f[:, 0:1024].rearrange("p (h k) -> p h k", k=128), score_f[:, 1024:2048].rearrange("p (h k) -> p h k", k=128)]
        for dlt in range(3):
            for hh in range(2):
                si_ = dlt * 2 + hh
                bst = stgs[si_]
                for h8 in range(8):
                    h = hh * 8 + h8
                    base = {0: 255, 1: 127, 2: 239}[dlt]
                    src = bass.AP(ztd.tensor, h * ZH + base, [[384, 128], [1, 128]])
                    P.dma("sp" if h8 % 2 == 0 else "act", w=[("stg6", si_)], out=bst[:, h8, :], in_=src)
                if dlt < 2:
                    P.op("dve", V_.tensor_copy, r=[("stg6", si_)], w=[("Bq",)], out=Bq[:, hh * 8:(hh + 1) * 8, dlt, :], in_=bst)
                else:
                    P.op("dve", V_.tensor_copy, r=[("stg6", si_)], w=[("Bqm",)], out=Bqm[:, hh * 8:(hh + 1) * 8, :], in_=bst)
        P.flush()
        if stop_after == "setup":
            P.final_wait()
            return nc

        passes = [("meta", 0, 0)] + [("x", b, gi) for b in range(n_seq) for gi in range(NGRP)]
        n_meta_chunks = cid["pw1_3"] + 1
        order = []
        for ps_ in passes:
            n = n_meta_chunks if ps_[0] == "meta" else NCH
            order.extend(range(n))
        wst = dict(issued=0, pos=0)

        def wissue(i):
            ch = order[i]
            c = chunks[ch]
            free = c["nk"] * c["w"]
            s = i % NS
            P.dma("sp", r=[("wscr", ch)], w=[("ws", s)], out=wslot[s][:, :free], in_=wscr[ch][:, :free])

        def wnext(expect, issue=True):
            while issue and wst["issued"] < min(len(order), wst["pos"] + NS):
                wissue(wst["issued"])
                wst["issued"] += 1
            i = wst["pos"]
            ch = order[i]
            assert chunks[ch]["nm"] == expect, (chunks[ch]["nm"], expect)
            c = chunks[ch]
            s = i % NS
            wst["pos"] += 1
            return s, wslot[s][:, :c["nk"] * c["w"]].rearrange("p (k n) -> p k n", n=c["w"])

        def linear_fm(wv, s, nm_, rhs_fn, nk, Gg, evac, rkeys):
            for mi in range(nm_):
                b = bank()
                for k in range(nk):
                    P.op("pe", T_.matmul, r=[("ws", s)] + rkeys, w=[("ps", b)],
                         out=psf[b][:, 0:Gg], lhsT=wv[:, k, mi * 128:(mi + 1) * 128], rhs=rhs_fn(k),
                         start=(k == 0), stop=(k == nk - 1))
                evac(mi, b)

        def layer_norm(src, srcname, gi_, bi_, gtab, mode, Gg, NTg, out_rows=None):
            banks = []
            for tt in range(NTg):
                bA, bB = bank(), bank()
                banks.append((bA, bB))
                for kc in range(KC):
                    b = bA if kc < 4 else bB
                    P.op("pe", T_.transpose, r=[(srcname, kc), ("ident32",)], w=[("ps", b)],
                         out=psf[b][:, (kc % 4) * 128:(kc % 4 + 1) * 128], in_=src[:, kc, tt * 128:(tt + 1) * 128],
                         identity=ident32[:])
                P.op("dve", V_.bn_stats, r=[("ps", bA)], w=[("stats", tt, 0)], out=stats[:, tt, 0:6], in_=psf[bA][:, :])
                P.op("dve", V_.bn_stats, r=[("ps", bB)], w=[("stats", tt, 1)], out=stats[:, tt, 6:12], in_=psf[bB][:, :])
                P.op("dve", V_.bn_aggr, r=[("stats", tt, 0), ("stats", tt, 1)], w=[("mv", tt)], out=mv[:, tt, :], in_=stats[:, tt, :])
            mvk = [("mv", tt) for tt in range(NTg)]
            P.op("dve", V_.tensor_scalar, r=mvk, w=[("vpe",)], out=vpe[:, 0:NTg], in0=mv[:, 0:NTg, 1],
                 scalar1=LN_EPS, scalar2=None, op0=ALU.add)
            if USE_POOL_POW:
                P.op("pool", Q_.tensor_tensor, r=[("vpe",), ("mhalf",)], w=[("rstd",)], out=rstd[:, 0:NTg], in0=vpe[:, 0:NTg],
                     in1=mhalf[:, 0:NTg], op=ALU.pow)
            else:
                P.op("act", A_.activation, r=[("vpe",)], w=[("vpe2",)], out=vpe[:, 0:NTg], in_=vpe[:, 0:NTg], func=AF.Sqrt)
                P.op("dve", V_.reciprocal, r=[("vpe2",)], w=[("rstd",)], out=rstd[:, 0:NTg], in_=vpe[:, 0:NTg])
            P.op("dve", V_.scalar_tensor_tensor, r=mvk + [("rstd",)], w=[("nmr",)], out=nmr[:, 0:NTg], in0=mv[:, 0:NTg, 0],
                 scalar=-1.0, in1=rstd[:, 0:NTg], op0=ALU.mult, op1=ALU.mult)
            for tt in range(NTg):
                bA, bB = banks[tt]
                xn = xnb[tt % 2]
                xi = tt % 2
                P.op("act", A_.activation, r=[("ps", bA), ("rstd",), ("nmr",)], w=[("xn", xi, 0)],
                     out=xn[:, 0:512], in_=psf[bA][:, :], func=AF.Identity, scale=rstd[:, tt:tt + 1], bias=nmr[:, tt:tt + 1])
                P.op("dve", V_.tensor_scalar, r=[("ps", bB), ("rstd",), ("nmr",)], w=[("xn", xi, 1)],
                     out=xn[:, 512:1024], in0=psf[bB][:, :], scalar1=rstd[:, tt:tt + 1], scalar2=nmr[:, tt:tt + 1],
                     op0=ALU.mult, op1=ALU.add)
            cbanks = []
            for tt in range(NTg):
                xn = xnb[tt % 2]
                xi = tt % 2
                cA, cB = (bank2(), bank2()) if tt % 2 == 0 else (bank(), bank())
                cbanks.append((cA, cB))
                for kc in range(KC):
                    b = cA if kc < 4 else cB
                    P.op("pe", T_.transpose, r=[("xn", xi, kc // 4), ("ident32",)], w=[("ps", b)],
                         out=psf[b][:, (kc % 4) * 128:(kc % 4 + 1) * 128], in_=xn[:, kc * 128:(kc + 1) * 128],
                         identity=ident32[:])
            if mode == "conv":
                for tt in range(NTg):
                    cA, cB = cbanks[tt]
                    for kc in range(KC):
                        b = cA if kc < 4 else cB
                        pv = psf[b][:, (kc % 4) * 128:(kc % 4 + 1) * 128]
                        P.op("act", A_.activation, r=[("ps", b), ("cpar", 0), ("cpar", 1)], w=[("mixT", kc)],
                             out=mixT[:, kc, tt * 128:(tt + 1) * 128], in_=pv, func=AF.Silu,
                             scale=cpar[:, 0, kc:kc + 1], bias=cpar[:, 1, kc:kc + 1])
            else:
                for (dst, dkey) in (((hT16, "h16"), (hT32, "h32")) if mode == "main" else ((hT32, "h32"),)):
                    for tt in range(NTg):
                        cA, cB = cbanks[tt]
                        for kc in range(KC):
                            b = cA if kc < 4 else cB
                            pv = psf[b][:, (kc % 4) * 128:(kc % 4 + 1) * 128]
                            if kc < 4:
                                P.op("dve", V_.tensor_scalar, r=[("ps", b), ("lnp", gi_), ("lnp", bi_)], w=[(dkey, kc)],
                                     out=dst[:, kc, tt * 128:(tt + 1) * 128], in0=pv, scalar1=gtab[:, gi_, kc:kc + 1],
                                     scalar2=gtab[:, bi_, kc:kc + 1], op0=ALU.mult, op1=ALU.add)
                            else:
                                P.op("act", A_.activation, r=[("ps", b), ("lnp", gi_), ("lnp", bi_)], w=[(dkey, kc)],
                                     out=dst[:, kc, tt * 128:(tt + 1) * 128], in_=pv, func=AF.Identity, scale=gtab[:, gi_, kc:kc + 1],
                                     bias=gtab[:, bi_, kc:kc + 1])
            if mode == "final":
                for tt in range(NTg):
                    xn = xnb[tt % 2]
                    xi = tt % 2
                    dA, dB = bank(), bank()
                    for kc in range(KC):
                        b = dA if kc < 4 else dB
                        P.op("pe", T_.transpose, r=[("h32", kc), ("ident32",)], w=[("ps", b)],
                             out=psf[b][:, (kc % 4) * 128:(kc % 4 + 1) * 128], in_=hT32[:, kc, tt * 128:(tt + 1) * 128],
                             identity=ident32[:])
                    P.op("act", A_.copy, r=[("ps", dA)], w=[("xn", xi, 0)], out=xn[:, 0:512], in_=psf[dA][:, :])
                    P.op("dve", V_.tensor_copy, r=[("ps", dB)], w=[("xn", xi, 1)], out=xn[:, 512:1024], in_=psf[dB][:, :])
                    P.dma("sp", r=[("xn", xi, 0), ("xn", xi, 1)], w=[("out",)], out=out_rows(tt), in_=xn[:, :])

        def ffn(l, Gg, NTg):
            P.fence(A_NAMES, B_NAMES)
            for j in range(6):
                wcols = 512 if j < 5 else 256
                sgi, wg = wnext("g%d_%d" % (l, j))
                sui, wu = wnext("u%d_%d" % (l, j), issue=False)
                for mi in range(wcols // 128):
                    m = j * 4 + mi
                    bg, bu = bank(), bank()
                    for k in range(KC):
                        P.op("pe", T_.matmul, r=[("ws", sgi), ("h16", k)], w=[("ps", bg)], out=psf[bg][:, 0:Gg],
                             lhsT=wg[:, k, mi * 128:(mi + 1) * 128], rhs=hT16[:, k, 0:Gg], start=(k == 0), stop=(k == KC - 1))
                    for k in range(KC):
                        P.op("pe", T_.matmul, r=[("ws", sui), ("h16", k)], w=[("ps", bu)], out=psf[bu][:, 0:Gg],
                             lhsT=wu[:, k, mi * 128:(mi + 1) * 128], rhs=hT16[:, k, 0:Gg], start=(k == 0), stop=(k == KC - 1))
                    si = m % 2
                    P.op("act", A_.activation, r=[("ps", bg)], w=[("sgt", si)], out=sgt[si][:, 0:Gg], in_=psf[bg][:, 0:Gg], func=AF.Silu)
                    P.op("dve", V_.tensor_tensor, r=[("sgt", si), ("ps", bu)], w=[("actT", m)], out=actT[:, m, 0:Gg],
                         in0=sgt[si][:, 0:Gg], in1=psf[bu][:, 0:Gg], op=ALU.mult)
            for m in range(8):
                s, wd = wnext("d%d_%d" % (l, m))
                b = bank()
                for k in range(FC):
                    P.op("pe", T_.matmul, r=[("ws", s), ("actT", k)], w=[("ps", b)], out=psf[b][:, 0:Gg],
                         lhsT=wd[:, k, :], rhs=actT[:, k, 0:Gg], start=(k == 0), stop=(k == FC - 1))
                P.op("dve", V_.scalar_tensor_tensor, r=[("ps", b), ("h32", m)], w=[("h32", m)], out=hT32[:, m, 0:Gg],
                     in0=hT32[:, m, 0:Gg], scalar=ALPHA, in1=psf[b][:, 0:Gg], op0=ALU.mult, op1=ALU.add)

        def prefetch_x(pidx):
            if pidx >= len(passes):
                return
            kind_, b_, g_ = passes[pidx]
            if kind_ == "meta":
                return
            if isinstance(stop_after, int) and pidx >= stop_after:
                return
            P.fence(["R", "dgw", "Pt"], ["xin"])
            for tt in range(NT):
                r0 = g_ * G + tt * 128
                P.dma("sp", w=[("xin", tt)], out=xin[:, tt, :], in_=x[b_, r0:r0 + 128, :])

        def emit_group(kind, bsel, gi, pidx=0):
            is_meta = kind == "meta"
            NTg = 1 if is_meta else NT
            Gg = NTg * 128
            T0 = -1 if is_meta else gi * NT
            slot0 = (T0 + 1) * 128
            P.fence(B_NAMES, A_NAMES)
            for tt in range(NTg):
                xi = tt % 2
                xn = xnb[xi]
                if is_meta:
                    P.op("pool", Q_.memset, w=[("xn", xi, 0), ("xn", xi, 1)], ap=xn[:, :], constant=0.0)
                    P.dma("sp", w=[("xn", xi, 0), ("xn", xi, 1)], out=xn[0:16, :], in_=meta[:, :])
                if os.environ.get("K_STOPPH") == "A0":
                    return
                bA, bB = bank(), bank()
                for kc in range(KC):
                    b = bA if kc < 4 else bB
                    if is_meta:
                        P.op("pe", T_.transpose, r=[("xn", xi, kc // 4), ("ident32",)], w=[("ps", b)],
                             out=psf[b][:, (kc % 4) * 128:(kc % 4 + 1) * 128], in_=xn[:, kc * 128:(kc + 1) * 128], identity=ident32[:])
                    else:
                        P.op("pe", T_.transpose, r=[("xin", tt), ("ident32",)], w=[("ps", b)],
                             out=psf[b][:, (kc % 4) * 128:(kc % 4 + 1) * 128], in_=xin[:, tt, kc * 128:(kc + 1) * 128], identity=ident32[:])
                if os.environ.get("K_STOPPH") == "A1":
                    return
                for hf, b in ((0, bA), (1, bB)):
                    pv = psf[b][:, :].rearrange("p (k n) -> p k n", n=128)
                    hk = [("h32", kc) for kc in range(hf * 4, hf * 4 + 4)]
                    hk16 = [("h16", kc) for kc in range(hf * 4, hf * 4 + 4)]
                    if os.environ.get("K_X") != "noact":
                        P.op("act", A_.copy, r=[("ps", b)], w=hk, out=hT32[:, hf * 4:hf * 4 + 4, tt * 128:(tt + 1) * 128], in_=pv)
                    if os.environ.get("K_X") != "nodve":
                        P.op("dve", V_.tensor_copy, r=[("ps", b)] + (hk if os.environ.get("K_X") == "ser" else []), w=hk16, out=hT16[:, hf * 4:hf * 4 + 4, tt * 128:(tt + 1) * 128], in_=pv)
            if os.environ.get("K_STOPPH") == "A":
                return
            if not is_meta:
                P.fence(["xin"], ["R", "dgw", "Pt"])
            hkeys = [("h16", k) for k in range(KC)]
            rh = lambda k: hT16[:, k, 0:Gg]
            s, wv = wnext("qi")
            def ev(mi, b):
                P.op("act", A_.copy, r=[("ps", b)], w=[("qiT", mi)], out=qiT[:, mi, 0:Gg], in_=psf[b][:, 0:Gg])
            linear_fm(wv, s, 4, rh, KC, Gg, ev, hkeys)
            s, wv = wnext("kiwi")
            def ev(mi, b):
                P.op("dve", V_.tensor_copy, r=[("ps", b)], w=[("kiT2", T0 + 1 + t_) for t_ in range(NTg)],
                     out=kiT2[:, slot0:slot0 + Gg], in_=psf[b][:, 0:Gg])
            linear_fm(wv, s, 1, rh, KC, Gg, ev, hkeys)
            for tt in range(NTg):
                b = bank()
                for k in range(KC):
                    P.op("pe", T_.matmul, r=[("ws", s), ("h16", k)], w=[("ps", b)], out=psf[b][:, 0:8],
                         lhsT=hT16[:, k, tt * 128:(tt + 1) * 128], rhs=wv[:, k, 128:136], start=(k == 0), stop=(k == KC - 1))
                P.op("dve", V_.tensor_scalar, r=[("ps", b)], w=[("wi", tt)], out=wi[:, tt, :], in0=psf[b][:, 0:8],
                     scalar1=WI_SCALE, scalar2=None, op0=ALU.mult)
            if os.environ.get("K_STOPPH") == "B":
                return
            Ss = [(T0 + tt + 2) * 128 for tt in range(NTg)]
            for tt in range(NTg):
                for hh in range(NIH):
                    half = hh % 2
                    P.op("dve", V_.tensor_copy, r=[("qiT", hh // 2)], w=[("qiz", hh)],
                         out=qiz[hh][half * 64:(half + 1) * 64, :], in_=qiT[half * 64:(half + 1) * 64, hh // 2, tt * 128:(tt + 1) * 128])
                    P.op("act", A_.activation, r=[("wi", tt), ("identb",)], w=[("dgw", tt, hh)], out=dgw[:, tt, hh, :], in_=identb[:, :],
                         func=AF.Copy, scale=wi[:, tt, hh:hh + 1])
                S = Ss[tt]
                nkb = (S + 511) // 512
                iitems = [(kb, hh) for kb in range(nkb) for hh in range(NIH)]
                accb = {}
                rinfo = {}

                def idx_l(i, tt=tt, S=S):
                    kb, hh = iitems[i]
                    c0, c1 = kb * 512, min(S, kb * 512 + 512)
                    b = bank()
                    kkeys = [("kiT2", j) for j in range(c0 // 128, c1 // 128)]
                    P.op("pe", T_.matmul, r=[("qiz", hh)] + kkeys, w=[("ps", b)], out=psf[b][:, 0:c1 - c0],
                         lhsT=qiz[hh][:, :], rhs=kiT2[:, c0:c1], start=True, stop=True)
                    ri = i % NRB
                    if i % 2 == 0:
                        P.op("act", A_.activation, r=[("ps", b)], w=[("R", ri)], out=Rb[ri][:, 0:c1 - c0],
                             in_=psf[b][:, 0:c1 - c0], func=AF.Relu)
                    else:
                        P.op("dve", V_.tensor_scalar, r=[("ps", b)], w=[("R", ri)], out=Rb[ri][:, 0:c1 - c0],
                             in0=psf[b][:, 0:c1 - c0], scalar1=0.0, scalar2=None, op0=ALU.max)
                    rinfo[i] = ri

                def idx_a(i, tt=tt, S=S):
                    kb, hh = iitems[i]
                    c0, c1 = kb * 512, min(S, kb * 512 + 512)
                    if hh == 0:
                        accb[kb] = bank2()
                    ab = accb[kb]
                    ri = rinfo[i]
                    P.op("pe", T_.matmul, r=[("R", ri), ("dgw", tt, hh)], w=[("ps", ab)], out=psf[ab][:, 0:c1 - c0],
                         lhsT=dgw[:, tt, hh, :], rhs=Rb[ri][:, 0:c1 - c0], start=(hh == 0), stop=(hh == NIH - 1))
                    if hh == NIH - 1:
                        P.op("act", A_.copy, r=[("ps", ab)], w=[("score", tt, kb)], out=score[:, tt, c0:c1], in_=psf[ab][:, 0:c1 - c0])

                ISK = 3
                for i in range(len(iitems) + ISK):
                    if i < len(iitems):
                        idx_l(i)
                    if i - ISK >= 0:
                        idx_a(i - ISK)
                skeys = [("score", tt, kb) for kb in range(nkb)]
                P.op("dve", V_.tensor_scalar, r=skeys, w=[("maskq", tt), ("Amx", tt)], out=maskq[:, tt, 0:S], in0=score[:, tt, 0:S],
                     scalar1=1.0, scalar2=None, op0=ALU.mult, op1=ALU.max, accum_out=Amx[:, tt, 0:1])
                P.op("dve", V_.tensor_scalar, r=skeys, w=[("maskq", tt), ("Amn", tt)], out=maskq[:, tt, 0:S], in0=score[:, tt, 0:S],
                     scalar1=-1.0, scalar2=None, op0=ALU.mult, op1=ALU.max, accum_out=Amx[:, tt, 1:2])
                P.op("dve", V_.tensor_tensor, r=[("Amx", tt), ("Amn", tt)], w=[("Asm", tt)], out=Asm[:, tt:tt + 1], in0=Amx[:, tt, 0:1],
                     in1=Amx[:, tt, 1:2], op=ALU.max)
                P.op("pool", Q_.tensor_tensor, r=[("score", tt, 0), ("padb",)], w=[("score", tt, 0)], out=score[:, tt, 0:128],
                     in0=score[:, tt, 0:128], in1=padb[:, :], op=ALU.add)
                kbd = (S - 128) // 512
                P.op("pool", Q_.tensor_tensor, r=[("score", tt, kbd), ("cbias",)], w=[("score", tt, kbd)], out=score[:, tt, S - 128:S],
                     in0=score[:, tt, S - 128:S], in1=cbias[:, :], op=ALU.add)
                P.op("dve", V_.tensor_scalar, r=[("Asm", tt), ("pow2",)], w=[("wtab", tt)], out=wtab[:, tt, :], in0=pow2[:, :],
                     scalar1=Asm[:, tt:tt + 1], scalar2=None, op0=ALU.mult)
            def proj_q(qh):
                s, wv = wnext("q%d" % qh)
                def ev(mi, b, qh=qh):
                    m = qh * 4 + mi
                    P.op("act", A_.copy, r=[("ps", b)], w=[("qT", m)], out=qT[:, m, 0:Gg], in_=psf[b][:, 0:Gg])
                linear_fm(wv, s, 4, rh, KC, Gg, ev, hkeys)

            def proj_k(kh):
                s, wv = wnext("k%d" % kh)
                def ev(mi, b, kh=kh):
                    m = kh * 4 + mi
                    P.op("act", A_.copy, r=[("ps", b)], w=[("KT", m, T0 + 1 + t_) for t_ in range(NTg)],
                         out=KT[:, m, slot0:slot0 + Gg], in_=psf[b][:, 0:Gg])
                linear_fm(wv, s, 4, rh, KC, Gg, ev, hkeys)

            def proj_v(vh):
                s, wv = wnext("v%d" % vh)
                for tt in range(NTg):
                    b = bank()
                    for k in range(KC):
                        P.op("pe", T_.matmul, r=[("ws", s), ("h16", k)], w=[("ps", b)], out=psf[b][:, :],
                             lhsT=hT16[:, k, tt * 128:(tt + 1) * 128], rhs=wv[:, k, :], start=(k == 0), stop=(k == KC - 1))
                    jt = T0 + 1 + tt
                    pv = psf[b][:, :].rearrange("p (h d) -> p h d", d=64)
                    P.op("act", A_.copy, r=[("ps", b)], w=[("V", jt, vh)], out=Vt[:, jt, vh * 8:(vh + 1) * 8, 0:64], in_=pv)

            proj_chunks = [lambda: proj_q(0), lambda: proj_q(1), lambda: proj_k(0), lambda: proj_k(1),
                           lambda: proj_v(0), lambda: proj_v(1)]
            allsk = [[("score", tt, kb) for kb in range((Ss[tt] + 511) // 512)] for tt in range(NTg)]
            split = (NTg == 2)
            P.op("dve", V_.memset, w=[("test",)], ap=test[:, :], constant=0.0)
            P.op("dve", V_.memset, w=[("thr",)], ap=thr[:, 0:1], constant=255.5)
            if split:
                P.op("dve", V_.memset, w=[("thr",)], ap=thr[:, 1:2], constant=float(511 - Ss[1]))
                P.op("dve", V_.tensor_scalar, r=[("wtab", 1)], w=[("wtab", 1)], out=wtab[:, 1, :], in0=wtab[:, 1, :], scalar1=-1.0,
                     scalar2=None, op0=ALU.mult)
            wk = [("wtab", tt) for tt in range(NTg)]
            for it in range(1, NIT + 1):
                for tt in range(NTg):
                    S = Ss[tt]
                    if split and tt == 1:
                        P.op("act", A_.activation, r=allsk[tt] + [("test",)], w=[("maskq", tt), ("cnt", tt)], out=maskq[:, tt, 0:S],
                             in_=score[:, tt, 0:S], func=AF.Sign, bias=test[:, 1:2], scale=1.0, accum_out=cnt[:, 1:2])
                    else:
                        P.op("dve", V_.tensor_scalar, r=allsk[tt] + [("test",)], w=[("maskq", tt), ("cnt", tt)], out=maskq[:, tt, 0:S],
                             in0=score[:, tt, 0:S], scalar1=test[:, tt:tt + 1], scalar2=None, op0=ALU.is_ge, op1=ALU.add,
                             accum_out=cnt[:, tt:tt + 1])
                ck = [("cnt", tt) for tt in range(NTg)]
                sub = 0.5 if it < NIT else 1.0
                dst, dkey = (test, "test") if it < NIT else (lo, "lo")
                P.op("dve", V_.tensor_tensor, r=ck + [("thr",)], w=[("sg",)], out=sg[:, 0:NTg], in0=cnt[:, 0:NTg], in1=thr[:, 0:NTg],
                     op=ALU.is_ge)
                P.op("dve", V_.scalar_tensor_tensor, r=[("sg",)] + wk, w=[("tmpb",)], out=tmpb[:, 0:NTg], in0=sg[:, 0:NTg], scalar=sub,
                     in1=wtab[:, 0:NTg, it], op0=ALU.subtract, op1=ALU.mult)
                P.op("dve", V_.tensor_tensor, r=[("tmpb",), ("test",)], w=[(dkey,)], out=dst[:, 0:NTg], in0=test[:, 0:NTg],
                     in1=tmpb[:, 0:NTg], op=ALU.add)
                if proj_chunks and it >= 2:
                    proj_chunks.pop(0)()
            while proj_chunks:
                proj_chunks.pop(0)()
            if split:
                P.op("dve", V_.tensor_scalar, r=[("lo",)], w=[("lo",)], out=lo[:, 1:2], in0=lo[:, 1:2], scalar1=-1.0, scalar2=None,
                     op0=ALU.mult)
            for tt in range(NTg):
                S = Ss[tt]
                P.op("dve", V_.tensor_scalar, r=allsk[tt] + [("lo",)], w=[("maskq", tt)], out=maskq[:, tt, 0:S], in0=score[:, tt, 0:S],
                     scalar1=lo[:, tt:tt + 1], scalar2=None, op0=ALU.is_ge)
            if os.environ.get("K_STOPPH") == "C1":
                return
            P.op("act", A_.preload_act_table, func=AF.Exp)
            njt = T0 + NTg + 1
            for j in range(njt):
                tmin = max(0, j - 1 - T0)
                bb = bankb()
                for tt in range(tmin, NTg):
                    P.op("pe", T_.transpose, r=[("maskq", tt), ("identb",)], w=[("psb", bb)], out=psb[bb][:, tt * 128:(tt + 1) * 128],
                         in_=maskq[:, tt, j * 128:(j + 1) * 128], identity=identb[:])
                P.op("act", A_.copy, r=[("psb", bb)], w=[("maskT", j)], out=maskT[:, j, tmin * 128:Gg], in_=psb[bb][:, tmin * 128:Gg])
            if debug and (not is_meta) and bsel == 0 and gi == int(os.environ.get("K_DBG_GI", "0")):
                P.dma("sp", r=allsk[0], w=[("dbg", 0)], out=dbg[0][:, 0:Ss[0]], in_=score[:, 0, 0:Ss[0]])
                P.dma("sp", r=[("lo",)], w=[("dbg", 1)], out=dbg[1][:, 0:NTg], in_=lo[:, 0:NTg])
                P.dma("sp", r=[("cnt", 0)], w=[("dbg", 1, 1)], out=dbg[1][:, 8:8 + NTg], in_=cnt[:, 0:NTg])
            if os.environ.get("K_STOPPH") == "C":
                return
            nfull = T0 + 2
            units = []
            j = 0
            while j < nfull:
                if j + 1 < nfull and Gg == G and 2 * Gg <= 512:
                    units.append([j, j + 1])
                    j += 2
                else:
                    units.append([j])
                    j += 1
            for j in range(nfull, njt):
                units.append([j])
            items = [(h, ui) for h in range(NH) for ui in range(len(units))]
            st_info = {}
            head_bo = {}

            def emit_st(i):
                h, ui = items[i]
                unit = units[ui]
                c = h // 2
                half = h % 2
                zi = half * 2 + (c % 2)
                if ui == 0:
                    for hn in ([0, 1] if h == 0 else [h + 1]):
                        if hn < NH:
                            cn, hfn = hn // 2, hn % 2
                            zn = hfn * 2 + (cn % 2)
                            P.op("dve", V_.tensor_copy, r=[("qT", cn)], w=[("qz", zn)], out=qz[zn][hfn * 64:(hfn + 1) * 64, 0:Gg],
                                 in_=qT[hfn * 64:(hfn + 1) * 64, cn, 0:Gg])
                bs = bank()
                pi = i % NPT
                lo_col, hi_col = None, None
                for k_, j in enumerate(unit):
                    tmin = max(0, j - 1 - T0)
                    c0 = tmin * 128
                    base = k_ * Gg
                    nb = []
                    for tt in range(tmin, NTg):
                        T = T0 + tt
                        if j == 0:
                            if T == -1:
                                nb.append((tt, Bq[:, h, 0, :]))
                            elif T == 0:
                                nb.append((tt, Bqm[:, h, :]))
                        else:
                            dl = T + 1 - j
                            if dl in (0, 1):
                                nb.append((tt, Bq[:, h, dl, :]))
                    P.op("pe", T_.matmul, r=[("KT", c, j), ("qz", zi)], w=[("ps", bs)], out=psf[bs][:, base + c0:base + Gg],
                         lhsT=KT[:, c, j * 128:(j + 1) * 128], rhs=qz[zi][:, c0:Gg], start=True, stop=(len(nb) == 0))
                    for ii, (tt, bq) in enumerate(nb):
                        P.op("pe", T_.matmul, r=[("Bq",), ("Bqm",), ("identb",)], w=[("ps", bs)],
                             out=psf[bs][:, base + tt * 128:base + (tt + 1) * 128],
                             lhsT=bq, rhs=identb[:, :], start=False, stop=(ii == len(nb) - 1))
                    if lo_col is None:
                        lo_col = base + c0
                    hi_col = base + Gg
                P.op("act", A_.activation, r=[("ps", bs), ("cfar",)], w=[("Pt", pi)], out=Pt[pi][:, lo_col:hi_col], in_=psf[bs][:, lo_col:hi_col],
                     func=AF.Exp, scale=ATT_SCALE, bias=cfar[:, h:h + 1])
                if len(unit) == 2:
                    mview = maskT[:, unit[0]:unit[0] + 2, :].rearrange("p j g -> p (j g)")
                else:
                    mview = maskT[:, unit[0], lo_col:hi_col]
                me = "pool" if i % 3 == 0 else "dve"
                ME = Q_ if me == "pool" else V_
                P.op(me, ME.tensor_tensor, r=[("Pt", pi)] + [("maskT", j) for j in unit], w=[("Pt", pi)], out=Pt[pi][:, lo_col:hi_col],
                     in0=Pt[pi][:, lo_col:hi_col], in1=mview, op=ALU.mult)
                st_info[i] = pi

            def emit_pv(i):
                h, ui = items[i]
                unit = units[ui]
                pi = st_info[i]
                if ui == 0:
                    bo = bank2()
                    head_bo[h] = bo
                    P.op("pe", T_.matmul, r=[("zerob",)], w=[("ps", bo)], out=psf[bo][:, 0:NTg * 65], lhsT=zerob[:, 0:128],
                         rhs=zerob[:, 0:NTg * 65], start=True, stop=False)
                bo = head_bo[h]
                for k_, j in enumerate(unit):
                    tmin = max(0, j - 1 - T0)
                    base = k_ * Gg
                    for tt in range(tmin, NTg):
                        last = (ui == len(units) - 1 and k_ == len(unit) - 1 and tt == NTg - 1)
                        P.op("pe", T_.matmul, r=[("Pt", pi), ("V", j, h // 8), ("Vones",)], w=[("ps", bo)],
                             out=psf[bo][:, tt * 65:(tt + 1) * 65], lhsT=Pt[pi][:, base + tt * 128:base + (tt + 1) * 128], rhs=Vt[:, j, h, :],
                             start=False, stop=last)
                if ui == len(units) - 1:
                    ov = psf[bo][:, 0:NTg * 65].rearrange("p (t d) -> p t d", d=65)
                    ri = h % 2
                    P.op("dve", V_.reciprocal, r=[("ps", bo)], w=[("rec", ri)], out=rec[ri][:, 0:NTg], in_=ov[:, :, 64])
                    for tt in range(NTg):
                        P.op("dve", V_.tensor_scalar, r=[("ps", bo), ("rec", ri)], w=[("attn", tt, h // 2)],
                             out=attn[:, tt, h * 64:(h + 1) * 64], in0=ov[:, tt, 0:64], scalar1=rec[ri][:, tt:tt + 1], scalar2=None,
                             op0=ALU.mult)

            SK = 4
            for i in range(len(items) + SK):
                if i < len(items):
                    emit_st(i)
                if i - SK >= 0:
                    emit_pv(i - SK)
            if os.environ.get("K_STOPPH") == "D":
                return
            prefetch_x(pidx + 1)
            P.op("act", A_.preload_act_table, func=AF.Silu)
            for tt in range(NTg):
                bb = bankb()
                for kc in range(KC):
                    P.op("pe", T_.transpose, r=[("attn", tt, kc), ("identb",)], w=[("psb", bb)], out=psb[bb][:, kc * 128:(kc + 1) * 128],
                         in_=attn[:, tt, kc * 128:(kc + 1) * 128], identity=identb[:])
                P.op("act", A_.copy, r=[("psb", bb)], w=[("mixT", kc) for kc in range(KC)], out=mixT[:, :, tt * 128:(tt + 1) * 128],
                     in_=psb[bb][:, :].rearrange("p (k n) -> p k n", n=128))
            mkeys = [("mixT", k) for k in range(KC)]
            rm = lambda k: mixT[:, k, 0:Gg]
            for oh_ in range(2):
                s, wv = wnext("wo%d" % oh_)
                def ev(mi, b, oh_=oh_):
                    m = oh_ * 4 + mi
                    P.op("dve", V_.scalar_tensor_tensor, r=[("ps", b), ("h32", m)], w=[("h32", m)], out=hT32[:, m, 0:Gg],
                         in0=hT32[:, m, 0:Gg], scalar=ALPHA, in1=psf[b][:, 0:Gg], op0=ALU.mult, op1=ALU.add)
                linear_fm(wv, s, 4, rm, KC, Gg, ev, mkeys)
            if os.environ.get("K_STOPPH") == "E":
                return
            def dump(idx):
                if debug and (not is_meta) and bsel == 0 and gi == int(os.environ.get("K_DBG_GI", "0")):
                    P.dma("sp", r=[("h32", k) for k in range(KC)], w=[("dbg", idx)],
                          out=dbg[idx][:, 0:KC * Gg].rearrange("p (k g) -> p k g", g=Gg), in_=hT32[:, :, 0:Gg])
            dump(2)
            layer_norm(hT32, "h32", 0, 1, lnp, "main", Gg, NTg)
            dump(3)
            ffn(0, Gg, NTg)
            dump(6)
            layer_norm(hT32, "h32", 2, 3, lnp, "main", Gg, NTg)
            dump(4)
            if is_meta:
                P.op("pool", Q_.memset, w=[("uT", k) for k in range(KC)], ap=uT[:, :, 0:30], constant=0.0)
            else:
                if gi == 0:
                    P.op("pool", Q_.tensor_copy, r=[("ucarry_meta",)], w=[("uT", k) for k in range(KC)], out=uT[:, :, 0:30], in_=ucarry_meta[:, :, :])
                else:
                    P.op("pool", Q_.tensor_copy, r=[("ucarry",)], w=[("uT", k) for k in range(KC)], out=uT[:, :, 0:30], in_=ucarry[:, :, :])
            for i in range(4):
                s, wv = wnext("pw1_%d" % i)
                for mi in range(2):
                    cg = i * 2 + mi
                    bv, bg = bank(), bank()
                    for k in range(KC):
                        P.op("pe", T_.matmul, r=[("ws", s), ("h16", k)], w=[("ps", bv)], out=psf[bv][:, 0:Gg],
                             lhsT=wv[:, k, mi * 128:(mi + 1) * 128], rhs=hT16[:, k, 0:Gg], start=(k == 0), stop=(k == KC - 1))
                    for k in range(KC):
                        P.op("pe", T_.matmul, r=[("ws", s), ("h16", k)], w=[("ps", bg)], out=psf[bg][:, 0:Gg],
                             lhsT=wv[:, k, 256 + mi * 128:256 + (mi + 1) * 128], rhs=hT16[:, k, 0:Gg], start=(k == 0), stop=(k == KC - 1))
                    si = cg % 2
                    P.op("act", A_.activation, r=[("ps", bg), ("cpar", 5)], w=[("sgt", si)], out=sgt[si][:, 0:Gg], in_=psf[bg][:, 0:Gg],
                         func=AF.Tanh, bias=hb1g[:, cg:cg + 1], scale=0.5)
                    P.op("pool", Q_.tensor_scalar, r=[("sgt", si)], w=[("sgt", si)], out=sgt[si][:, 0:Gg], in0=sgt[si][:, 0:Gg],
                         scalar1=0.5, scalar2=0.5, op0=ALU.mult, op1=ALU.add)
                    P.op("dve", V_.scalar_tensor_tensor, r=[("ps", bv), ("sgt", si), ("cpar", 4)], w=[("uT", cg)],
                         out=uT[:, cg, 30:30 + Gg], in0=psf[bv][:, 0:Gg], scalar=cpar[:, 4, cg:cg + 1], in1=sgt[si][:, 0:Gg],
                         op0=ALU.add, op1=ALU.mult)
            ukeys = [("uT", k) for k in range(KC)]
            if is_meta:
                P.op("pool", Q_.tensor_copy, r=ukeys, w=[("ucarry_meta",)], out=ucarry_meta[:, :, :], in_=uT[:, :, 16:46])
                return
            P.op("pool", Q_.tensor_copy, r=ukeys, w=[("ucarry",)], out=ucarry[:, :, :], in_=uT[:, :, Gg:Gg + 30])
            for c in range(8):
                s, wv = wnext("dg%d" % c)
                b = bank()
                for j in range(31):
                    P.op("pe", T_.matmul, r=[("ws", s), ("uT", c)], w=[("ps", b)], out=psf[b][:, 0:Gg], lhsT=wv[:, j, :],
                         rhs=uT[:, c, j:j + Gg], start=(j == 0), stop=(j == 30))
                P.op("act", A_.activation, r=[("ps", b), ("cpar", 2)], w=[("yT32", c)], out=yT32[:, c, 0:Gg], in_=psf[b][:, 0:Gg],
                     func=AF.Identity, bias=cpar[:, 2, c:c + 1], scale=1.0)
            layer_norm(yT32, "yT32", 0, 1, cpar, "conv", Gg, NTg)
            for oh_ in range(2):
                s, wv = wnext("pw2_%d" % oh_)
                def ev(mi, b, oh_=oh_):
                    m = oh_ * 4 + mi
                    ti = m % 2
                    P.op("act", A_.activation, r=[("ps", b), ("cpar", 3)], w=[("tmpe", ti)], out=tmpe[ti][:, 0:Gg], in_=psf[b][:, 0:Gg],
                         func=AF.Identity, bias=cpar[:, 3, m:m + 1], scale=1.0)
                    P.op("dve", V_.scalar_tensor_tensor, r=[("tmpe", ti), ("h32", m)], w=[("h32", m)], out=hT32[:, m, 0:Gg],
                         in0=hT32[:, m, 0:Gg], scalar=ALPHA, in1=tmpe[ti][:, 0:Gg], op0=ALU.mult, op1=ALU.add)
                linear_fm(wv, s, 4, rm, KC, Gg, ev, mkeys)
            dump(5)
            layer_norm(hT32, "h32", 4, 5, lnp, "main", Gg, NTg)
            dump(7)
            ffn(1, Gg, NTg)
            orow = lambda tt: out[bsel, gi * G + tt * 128: gi * G + (tt + 1) * 128, :]
            layer_norm(hT32, "h32", 6, 7, lnp, "final", Gg, NTg, out_rows=orow)

        for pi_, (kind, bsel, gi) in enumerate(passes):
            if isinstance(stop_after, int) and pi_ >= stop_after:
                break
            emit_group(kind, bsel, gi, pi_)
        P.flush()
        P.final_wait()
        if info is not None:
            info["ms"] = dict(P.ms_count)
            info["dmax"] = max(16 * c for c in P.dcount)
    return nc


_CACHE = {}


def kernel(**inputs):
    n_cores = 8
    x = np.ascontiguousarray(inputs["x"], dtype=np.float32)
    B = x.shape[0]
    per = B // n_cores
    consts = host_consts()
    if "nc" not in _CACHE:
        _CACHE["nc"] = build_program(n_seq=per)
    nc = _CACHE["nc"]
    shared = {k: np.ascontiguousarray(v, dtype=np.float32) for k, v in inputs.items() if k != "x"}
    in_maps = []
    for c in range(n_cores):
        m = dict(shared)
        m.update(consts)
        m["x"] = np.ascontiguousarray(x[c * per:(c + 1) * per])
        in_maps.append(m)
    res = run_bass_kernel_spmd(nc, in_maps, core_ids=list(range(n_cores)))
    outs = [np.asarray(r["out"], dtype=np.float32) for r in res.results]
    return np.concatenate(outs, axis=0)
```

```python
import math
import os
from functools import partial
from contextlib import ExitStack
import numpy as np
import concourse.bass as bass
import concourse.mybir as mybir
from concourse.bass_utils import run_bass_kernel_spmd

F32 = mybir.dt.float32
BF16 = mybir.dt.bfloat16
AF = mybir.ActivationFunctionType
ALU = mybir.AluOpType

D = 1024
KC = 8
NH = 16
NIH = 8
DFF = 2816
FC = 22
SEQ = 2048
NSLOT = 2176
NKT = 17
NT = 2
G = NT * 128
NGRP = SEQ // G
NS = 4
WSZ = 4096
NIT = 12
ALPHA = 4.0 ** 0.25
WI_SCALE = (NIH ** -0.5) * (64 ** -0.5)
ATT_SCALE = 0.125
LN_EPS = 1e-5
NEG = -1.0e30
SAME_ENG_SYNC = True
USE_POOL_POW = True
NCH = 64


class Op:
    __slots__ = ("eng", "fn", "is_dma", "deps", "flag", "ms", "dsem", "dval", "idx")


class Prog:
    def __init__(self, nc, es, n_dma_sems=24):
        self.nc = nc
        self.eng = {"pe": nc.tensor, "act": nc.scalar, "dve": nc.vector, "pool": nc.gpsimd, "sp": nc.sync}
        self.sem = {e: es.enter_context(nc.semaphore("ms_" + e)) for e in self.eng}
        self.dsems = [es.enter_context(nc.semaphore("dq%d" % i)) for i in range(n_dma_sems)]
        self.dcount = [0] * n_dma_sems
        self.dlast = [None] * n_dma_sems
        self.dnext = 0
        self.ops = []
        self.ms_count = {e: 0 for e in self.eng}
        self.waited = {}
        self.lw = {}
        self.rd = {}
        self.rd_dma = {}
        self.names = {}
        self.nfloor = {}
        self.gfloor = []
        self.floor_done = set()
        self.last_on_eng = {}
        self.nops = 0

    def _add(self, eng, fn, r, w, is_dma):
        o = Op()
        o.eng = eng
        o.fn = fn
        o.is_dma = is_dma
        o.flag = False
        o.ms = None
        o.dsem = None
        o.dval = 0
        o.idx = self.nops
        self.nops += 1
        deps = []
        for k in r:
            if k[0] in ("ps", "psb"):
                rr = self.rd.get(k)
                if rr:
                    deps.extend(o2 for e2, o2 in rr.items() if e2 != eng)
        if eng not in self.floor_done:
            deps.extend(self.gfloor)
            self.floor_done.add(eng)
        for k in r:
            d = self.lw.get(k)
            if d is not None:
                deps.append(d)
        for k in w:
            d = self.lw.get(k)
            if d is not None:
                deps.append(d)
            rr = self.rd.get(k)
            if rr:
                deps.extend(rr.values())
            rl = self.rd_dma.get(k)
            if rl:
                deps.extend(rl)
            nf = self.nfloor.get(k[0])
            if nf is not None and eng not in nf[1]:
                deps.extend(nf[0])
                nf[1].add(eng)
        if is_dma:
            s = self.dnext % len(self.dsems)
            self.dnext += 1
            if self.dlast[s] is not None:
                deps.append(self.dlast[s])
            self.dcount[s] += 1
            o.dsem = self.dsems[s]
            o.dval = 16 * self.dcount[s]
            self.dlast[s] = o
        fd = []
        seen = set()
        for d in deps:
            if d is o or id(d) in seen:
                continue
            seen.add(id(d))
            if not d.is_dma and d.eng == eng:
                if eng == "pe" or eng == "sp" or not SAME_ENG_SYNC:
                    continue
            if not d.is_dma:
                d.flag = True
            fd.append(d)
        o.deps = fd
        for k in w:
            self.lw[k] = o
            self.rd[k] = {}
            self.rd_dma[k] = []
            self.names.setdefault(k[0], set()).add(k)
        for k in r:
            if k in w:
                continue
            self.names.setdefault(k[0], set()).add(k)
            if is_dma:
                self.rd_dma.setdefault(k, []).append(o)
            else:
                self.rd.setdefault(k, {})[eng] = o
        self.ops.append(o)
        if not is_dma:
            self.last_on_eng[eng] = o
        return o

    def op(self, eng, _f, r=(), w=(), **kw):
        return self._add(eng, partial(_f, **kw), tuple(r), tuple(w), False)

    def dma(self, q, r=(), w=(), **kw):
        return self._add(q, partial(self.eng[q].dma_start, **kw), tuple(r), tuple(w), True)

    def fence(self, from_names, to_names):
        acc = []
        for n in from_names:
            for k in self.names.get(n, ()):
                d = self.lw.get(k)
                if d is not None:
                    acc.append(d)
                rr = self.rd.get(k)
                if rr:
                    acc.extend(rr.values())
                rl = self.rd_dma.get(k)
                if rl:
                    acc.extend(rl)
        best = {}
        dm = []
        for d in acc:
            if d.is_dma:
                dm.append(d)
            else:
                b = best.get(d.eng)
                if b is None or d.idx > b.idx:
                    best[d.eng] = d
        lst = list(best.values()) + dm
        for n in to_names:
            prev = self.nfloor.get(n)
            self.nfloor[n] = ((list(prev[0]) if prev else []) + lst, set())

    def flush(self):
        for e, o in self.last_on_eng.items():
            o.flag = True
        for o in self.ops:
            E = self.eng[o.eng]
            for d in o.deps:
                if d.is_dma:
                    sem, val = d.dsem, d.dval
                else:
                    assert d.ms is not None, "dep on unflagged op"
                    sem, val = self.sem[d.eng], d.ms
                key = (o.eng, id(sem))
                if self.waited.get(key, 0) < val:
                    E.wait_ge(sem, val)
                    self.waited[key] = val
            ins = o.fn()
            if o.is_dma:
                ins.then_inc(o.dsem, 16)
            elif o.flag:
                self.ms_count[o.eng] += 1
                o.ms = self.ms_count[o.eng]
                ins.then_inc(self.sem[o.eng], 1)
            o.fn = None
        self.gfloor = [o for o in self.last_on_eng.values()] + [d for d in self.dlast if d is not None]
        self.floor_done = set()
        self.ops = []
        self.lw = {}
        self.rd = {}
        self.rd_dma = {}
        self.names = {}
        self.nfloor = {}

    def final_wait(self):
        E = self.eng["sp"]
        for s, d in enumerate(self.dlast):
            if d is not None:
                E.wait_ge(d.dsem, d.dval)
        for e, o in self.last_on_eng.items():
            if o.ms is not None and e != "sp":
                E.wait_ge(self.sem[e], o.ms)


def chunk_table():
    ch = []
    ch.append(dict(kind="std", src="w_in", c0=3072, w=512, nk=8, nm="qi"))
    ch.append(dict(kind="kiwi", nk=8, w=136, nm="kiwi"))
    for nm, c0 in (("q0", 0), ("q1", 512), ("k0", 1024), ("k1", 1536)):
        ch.append(dict(kind="std", src="w_in", c0=c0, w=512, nk=8, nm=nm))
    ch.append(dict(kind="std", src="w_in", c0=2048, w=512, nk=8, nm="v0"))
    ch.append(dict(kind="std", src="w_in", c0=2560, w=512, nk=8, nm="v1"))
    ch.append(dict(kind="std", src="w_o", c0=0, w=512, nk=8, nm="wo0"))
    ch.append(dict(kind="std", src="w_o", c0=512, w=512, nk=8, nm="wo1"))
    for l in range(2):
        if l == 1:
            for i in range(4):
                ch.append(dict(kind="pw1", i=i, nk=8, w=512, nm="pw1_%d" % i))
            for c in range(8):
                ch.append(dict(kind="diag", c=c, nk=31, w=128, nm="dg%d" % c))
            ch.append(dict(kind="std", src="w_pw2", c0=0, w=512, nk=8, nm="pw2_0"))
            ch.append(dict(kind="std", src="w_pw2", c0=512, w=512, nk=8, nm="pw2_1"))
        for j in range(6):
            w = 512 if j < 5 else 256
            ch.append(dict(kind="std", src="gate%d" % l, c0=j * 512, w=w, nk=8, nm="g%d_%d" % (l, j)))
            ch.append(dict(kind="std", src="up%d" % l, c0=j * 512, w=w, nk=8, nm="u%d_%d" % (l, j)))
        for m in range(8):
            ch.append(dict(kind="std", src="down%d" % l, c0=m * 128, w=128, nk=22, nm="d%d_%d" % (l, m)))
    assert len(ch) == NCH, len(ch)
    return ch


def t5_bucket_np(d):
    d = np.asarray(d, dtype=np.int64)
    dm = np.maximum(d, 1).astype(np.float32)
    large = 16 + (np.log(dm / np.float32(16)) / np.float32(math.log(128 / 16)) * np.float32(16)).astype(np.int32)
    large = np.minimum(large, 31)
    return np.where(d < 16, d, large)


def host_consts():
    c = {}
    c["ident"] = np.eye(128, dtype=np.float32)
    j = np.arange(384)
    dist = np.maximum(255 - j, 0)
    b = t5_bucket_np(dist)
    oh = np.zeros((32, 384), np.float32)
    oh[b, j] = 1.0
    c["ohr"] = oh
    q = np.arange(128)[:, None]
    k = np.arange(128)[None, :]
    c["cbias"] = np.where(k <= q, 0.0, NEG).astype(np.float32)
    c["padb"] = np.broadcast_to(np.where(k >= 16, NEG, 0.0), (128, 128)).astype(np.float32).copy()
    c["pow2"] = np.broadcast_to((2.0 ** (1.0 - np.arange(NIT + 1)))[None, :], (128, NIT + 1)).astype(np.float32).copy()
    return c


def build_program(n_seq=2, debug=False, stop_after=None, info=None):
    nc = bass.Bass("TRN2", target_bir_lowering=False)
    es = ExitStack()
    with es:
        es.enter_context(nc.allow_non_contiguous_dma(reason="small param/layout loads"))
        es.enter_context(nc.allow_low_precision(reason="bf16 matmul operands by design"))

        def din(name, shape, dt=F32):
            return nc.dram_tensor(name, list(shape), dt, kind="ExternalInput").ap()

        x = din("x", [n_seq, SEQ, D])
        meta = din("meta_tokens", [16, D])
        relb = din("rel_bias", [32, 16])
        w_in = din("w_in_attn", [1, D, 3656])
        w_o = din("w_o_attn", [1, D, D])
        w_pw1 = din("w_pw1", [1, D, 2 * D])
        b_pw1 = din("b_pw1", [1, 2 * D])
        w_dw = din("w_dw", [1, 31, D])
        b_dw = din("b_dw", [1, D])
        cln_g = din("conv_ln_g", [1, D])
        cln_b = din("conv_ln_b", [1, D])
        w_pw2 = din("w_pw2", [1, D, D])
        b_pw2 = din("b_pw2", [1, D])
        ln1_g = din("ln1_g", [2, D])
        ln1_b = din("ln1_b", [2, D])
        w_gate = din("ffn_w_gate", [2, D, DFF])
        w_up = din("ffn_w_up", [2, D, DFF])
        w_down = din("ffn_w_down", [2, DFF, D])
        ln2_g = din("ln2_g", [2, D])
        ln2_b = din("ln2_b", [2, D])
        c_ident = din("ident", [128, 128])
        c_ohr = din("ohr", [32, 384])
        c_cbias = din("cbias", [128, 128])
        c_padb = din("padb", [128, 128])
        c_pow2 = din("pow2", [128, NIT + 1])
        out = nc.dram_tensor("out", [n_seq, SEQ, D], F32, kind="ExternalOutput").ap()
        wscr = nc.dram_tensor("wscr", [NCH, 128, WSZ], BF16, kind="Internal").ap()
        tbd = nc.dram_tensor("tbd", [16, 384], F32, kind="Internal").ap()
        ZH = 128 * 385
        ztd = nc.dram_tensor("ztd", [16, ZH], F32, kind="Internal").ap()
        dbg = None
        if debug:
            dbg = nc.dram_tensor("dbg", [8, 128, 2176], F32, kind="ExternalOutput").ap()

        wsrc = {"w_in": w_in[0], "w_o": w_o[0], "w_pw2": w_pw2[0],
                "gate0": w_gate[0], "gate1": w_gate[1], "up0": w_up[0], "up1": w_up[1],
                "down0": w_down[0], "down1": w_down[1]}
        chunks = chunk_table()
        cid = {c["nm"]: i for i, c in enumerate(chunks)}

        P = Prog(nc, es)
        V_, A_, S_, T_, Q_ = nc.vector, nc.scalar, nc.sync, nc.tensor, nc.gpsimd

        def sb(name, shape, dt):
            return es.enter_context(nc.sbuf_tensor("s_" + name, list(shape), dt))

        ident32 = sb("ident32", [128, 128], F32)
        identb = sb("identb", [128, 128], BF16)
        P.dma("sp", w=[("ident32",)], out=ident32[:], in_=c_ident[:, :])
        P.op("dve", V_.tensor_copy, r=[("ident32",)], w=[("identb",)], out=identb[:], in_=ident32[:])

        with ExitStack() as es2:
            NPB = 4
            stg = [es2.enter_context(nc.sbuf_tensor("stg%d" % i, [128, WSZ], F32)) for i in range(NPB)]
            cvt = [es2.enter_context(nc.sbuf_tensor("cvt%d" % i, [128, WSZ], BF16)) for i in range(NPB)]
            wT = es2.enter_context(nc.sbuf_tensor("wdwT", [128, 8, 31], F32))
            tbs = es2.enter_context(nc.sbuf_tensor("tbs_p", [16, 384], F32))
            rb_sb = es2.enter_context(nc.sbuf_tensor("rb_p", [32, 16], F32))
            oh_sb = es2.enter_context(nc.sbuf_tensor("oh_p", [32, 384], F32))
            pst = es2.enter_context(nc.psum_tensor("pst_p", [128, 512], F32))
            P.dma("act", w=[("rb_sb",)], out=rb_sb[:], in_=relb[:, :])
            P.dma("act", w=[("oh_sb",)], out=oh_sb[:], in_=c_ohr[:, :])
            P.op("pe", T_.matmul, r=[("rb_sb",), ("oh_sb",)], w=[("ps", 99)],
                 out=pst[0:16, 0:384], lhsT=rb_sb[:, :], rhs=oh_sb[:, :], start=True, stop=True)
            P.op("dve", V_.tensor_copy, r=[("ps", 99)], w=[("tbs0",)], out=tbs[:, :], in_=pst[0:16, 0:384])
            P.op("dve", V_.tensor_scalar, r=[("tbs0",)], w=[("tbs",)], out=tbs[:, :], in0=tbs[:, :],
                 scalar1=tbs[:, 0:1], scalar2=1.0 / ATT_SCALE, op0=ALU.subtract, op1=ALU.mult)
            P.dma("act", r=[("tbs",)], w=[("tbd",)], out=tbd[:, :], in_=tbs[:, :])
            for h in range(NH):
                P.dma("act", r=[("tbd",)], w=[("ztd", h)], out=bass.AP(ztd.tensor, h * ZH, [[385, 128], [1, 384]]),
                      in_=bass.AP(tbd.tensor, h * 384, [[0, 128], [1, 384]]))
            import os
            for c in range(8):
                if os.environ.get("K_NOWT"):
                    break
                src = bass.AP(w_dw.tensor, c * 128, [[1, 128], [D, 31]])
                P.dma("sp", w=[("wdwT", c)], out=wT[:, c, :], in_=src)
            for i, c in enumerate(chunks):
                if os.environ.get("K_PRE") and c["nm"] not in os.environ["K_PRE"].split(","):
                    continue
                b = i % NPB
                free = c["nk"] * c["w"]
                sv = stg[b][:, :free].rearrange("p (k n) -> p k n", n=c["w"])
                cv = cvt[b][:, :free]
                if c["kind"] == "std":
                    W = wsrc[c["src"]]
                    src = W.rearrange("(k p) n -> p k n", p=128)[:, :, c["c0"]:c["c0"] + c["w"]]
                    P.dma("act" if i % 4 == 3 else "sp", w=[("stg", b)], out=sv, in_=src)
                elif c["kind"] == "kiwi":
                    W = w_in[0].rearrange("(k p) n -> p k n", p=128)
                    P.dma("sp", w=[("stg", b)], out=sv[:, :, 0:64], in_=W[:, :, 3584:3648])
                    P.dma("sp", w=[("stg", b)], out=sv[:, :, 64:128], in_=W[:, :, 3584:3648])
                    P.dma("sp", w=[("stg", b)], out=sv[:, :, 128:136], in_=W[:, :, 3648:3656])
                elif c["kind"] == "pw1":
                    W = w_pw1[0].rearrange("(k p) n -> p k n", p=128)
                    ii = c["i"]
                    P.dma("sp", w=[("stg", b)], out=sv[:, :, 0:256], in_=W[:, :, ii * 256:(ii + 1) * 256])
                    P.dma("sp", w=[("stg", b)], out=sv[:, :, 256:512], in_=W[:, :, D + ii * 256:D + (ii + 1) * 256])
                if c["kind"] == "diag":
                    cc = c["c"]
                    cv3 = cv.rearrange("p (k n) -> p k n", n=128)
                    P.op("dve", V_.tensor_tensor, r=[("wdwT", cc), ("ident32",)], w=[("cvt", b, j) for j in range(31)],
                         out=cv3, in0=ident32[:, :].unsqueeze(1).to_broadcast([128, 31, 128]),
                         in1=wT[:, cc, :].unsqueeze(2).to_broadcast([128, 31, 128]), op=ALU.mult)
                    wkeys = [("cvt", b, j) for j in range(31)]
                    P.dma("act", r=wkeys, w=[("wscr", i)], out=wscr[i][:, :free], in_=cv)
                else:
                    wkeys = [("cvt", b, j) for j in range(31)]
                    if (i // 2) % 2 == 0:
                        P.op("dve", V_.tensor_copy, r=[("stg", b)], w=wkeys, out=cv, in_=stg[b][:, :free])
                    else:
                        P.op("act", A_.copy, r=[("stg", b)], w=wkeys, out=cv, in_=stg[b][:, :free])
                    P.dma("act", r=wkeys, w=[("wscr", i)], out=wscr[i][:, :free], in_=cv)
            P.flush()
        if stop_after == "prepass":
            P.final_wait()
            return nc

        KT = sb("KT", [128, KC, NSLOT], BF16)
        Vt = sb("Vt", [128, NKT, NH, 65], BF16)
        kiT2 = sb("kiT2", [128, NSLOT], BF16)
        hT32 = sb("hT32", [128, KC, G], F32)
        hT16 = sb("hT16", [128, KC, G], BF16)
        xnb = [sb("xn%d" % i, [128, D], F32) for i in range(2)]
        qz = [sb("qz%d" % i, [128, G], BF16) for i in range(4)]
        qiz = [sb("qiz%d" % i, [128, 128], BF16) for i in range(8)]
        wi = sb("wi", [128, NT, 8], F32)
        mixT = sb("mixT", [128, KC, G], BF16)
        wslot = [sb("ws%d" % i, [128, WSZ], BF16) for i in range(NS)]
        ARENA_BYTES = 58 * 1024
        arena = sb("arena", [128, ARENA_BYTES // 2], BF16)
        off = [0]

        def carve(nelem, dt, shape=None, base=None):
            nb = nelem * (4 if dt == F32 else 2)
            o = off[0] if base is None else base
            assert o % 4 == 0
            if base is None:
                off[0] += (nb + 3) // 4 * 4
            assert o + nb <= ARENA_BYTES, (o, nb)
            v = arena[:, o // 2:(o + nb) // 2]
            if dt == F32:
                v = v.bitcast(F32)
            return v

        score_f = carve(NT * NSLOT, F32)
        score = score_f.rearrange("p (t s) -> p t s", s=NSLOT)
        maskq_f = carve(NT * NSLOT, BF16)
        maskq = maskq_f.rearrange("p (t s) -> p t s", s=NSLOT)
        maskT = carve(NKT * G, BF16).rearrange("p (j g) -> p j g", g=G)
        qT = carve(KC * G, BF16).rearrange("p (k g) -> p k g", g=G)
        qiT = carve(4 * G, BF16).rearrange("p (k g) -> p k g", g=G)
        NRB = 5
        off_rb = off[0]
        Rb = [carve(512, BF16) for _ in range(NRB)]
        dgw = carve(NT * NIH * 128, BF16).rearrange("p (t h k) -> p t h k", h=NIH, k=128)
        NPT = 5
        Pt = [carve(512, BF16) for _ in range(NPT)]
        attn = carve(NT * D, BF16).rearrange("p (t d) -> p t d", d=D)
        endA = off[0]
        xin = carve(NT * D, F32, base=off_rb).rearrange("p (t d) -> p t d", d=D)
        off[0] = 0
        actT = carve(FC * G, BF16).rearrange("p (k g) -> p k g", g=G)
        sgt = [carve(G, F32) for _ in range(2)]
        uT = carve(KC * (30 + G), BF16).rearrange("p (k g) -> p k g", g=30 + G)
        yT32 = carve(KC * G, F32).rearrange("p (k g) -> p k g", g=G)
        tmpe = [carve(G, F32) for _ in range(2)]
        endB = off[0]
        A_NAMES = ["score", "maskq", "maskT", "qT", "qiT", "R", "Pt", "attn", "dgw"]
        B_NAMES = ["actT", "sgt", "uT", "yT32", "tmpe"]

        ucarry = sb("ucarry", [128, KC, 30], BF16)
        ucarry_meta = sb("ucarry_meta", [128, KC, 30], BF16)
        Asm = sb("Asm", [128, NT], F32)
        Amx = sb("Amx", [128, NT, 2], F32)
        wtab = sb("wtab", [128, NT, NIT + 1], F32)
        test = sb("test", [128, NT], F32)
        cnt = sb("cnt", [128, NT], F32)
        sg = sb("sg", [128, NT], F32)
        tmpb = sb("tmpb", [128, NT], F32)
        lo = sb("lo", [128, NT], F32)
        thr = sb("thr", [128, NT], F32)
        mhalf = sb("mhalf", [128, NT], F32)
        hb1g = sb("hb1g", [128, 8], F32)
        rec = [sb("rec%d" % i, [128, NT], F32) for i in range(2)]
        stats = sb("stats", [128, NT, 12], F32)
        mv = sb("mv", [128, NT, 2], F32)
        vpe = sb("vpe", [128, NT], F32)
        rstd = sb("rstd", [128, NT], F32)
        nmr = sb("nmr", [128, NT], F32)
        lnp = sb("lnp", [128, 8, 8], F32)
        cpar = sb("cpar", [128, 6, 8], F32)
        Bq = sb("Bq", [128, NH, 2, 128], BF16)
        Bqm = sb("Bqm", [128, NH, 128], BF16)
        cfar = sb("cfar", [128, NH], F32)
        cbias = sb("cbias", [128, 128], F32)
        padb = sb("padb", [128, 128], F32)
        pow2 = sb("pow2", [128, NIT + 1], F32)
        zerob = sb("zerob", [128, 256], BF16)
        psf = [es.enter_context(nc.psum_tensor("psf%d" % i, [128, 512], F32)) for i in range(6)]
        psb = [es.enter_context(nc.psum_tensor("psb%d" % i, [128, 1024], BF16)) for i in range(2)]
        bctr = [0, 0]

        def bank():
            i = bctr[0] % 4
            bctr[0] += 1
            return i

        b2ctr = [0]

        def bank2():
            i = 4 + b2ctr[0] % 2
            b2ctr[0] += 1
            return i

        def bankb():
            i = bctr[1] % 2
            bctr[1] += 1
            return i

        def ld_pp(dst, src_vec, key):
            P.dma("sp", w=[key], out=dst, in_=src_vec.rearrange("(c p) -> p c", p=128))

        for l in range(2):
            ld_pp(lnp[:, 4 * l + 0, :], ln1_g[l], ("lnp", 4 * l + 0))
            ld_pp(lnp[:, 4 * l + 1, :], ln1_b[l], ("lnp", 4 * l + 1))
            ld_pp(lnp[:, 4 * l + 2, :], ln2_g[l], ("lnp", 4 * l + 2))
            ld_pp(lnp[:, 4 * l + 3, :], ln2_b[l], ("lnp", 4 * l + 3))
        ld_pp(cpar[:, 0, :], cln_g[0], ("cpar", 0))
        ld_pp(cpar[:, 1, :], cln_b[0], ("cpar", 1))
        ld_pp(cpar[:, 2, :], b_dw[0], ("cpar", 2))
        ld_pp(cpar[:, 3, :], b_pw2[0], ("cpar", 3))
        ld_pp(cpar[:, 4, :], b_pw1[0][0:D], ("cpar", 4))
        ld_pp(cpar[:, 5, :], b_pw1[0][D:2 * D], ("cpar", 5))
        P.dma("sp", w=[("cbias",)], out=cbias[:], in_=c_cbias[:, :])
        P.dma("sp", w=[("padb",)], out=padb[:], in_=c_padb[:, :])
        P.dma("sp", w=[("pow2",)], out=pow2[:], in_=c_pow2[:, :])
        P.dma("sp", w=[("cfar",)], out=cfar[:], in_=bass.AP(relb.tensor, 31 * 16, [[0, 128], [1, 16]]))
        P.op("pool", Q_.memset, w=[("zerob",)], ap=zerob[:], constant=0.0)
        P.op("dve", V_.tensor_scalar, r=[("cpar", 5)], w=[("hb1g",)], out=hb1g[:, :], in0=cpar[:, 5, :], scalar1=0.5, scalar2=None,
             op0=ALU.mult)
        P.op("pool", Q_.memset, w=[("mhalf",)], ap=mhalf[:], constant=-0.5)
        for i in range(4):
            P.op("pool", Q_.memset, w=[("qz", i)], ap=qz[i][:], constant=0.0)
        for i in range(8):
            P.op("pool", Q_.memset, w=[("qiz", i)], ap=qiz[i][:], constant=0.0)
        P.op("pool", Q_.memset, w=[("Vones",)], ap=Vt[:, :, :, 64:65], constant=1.0)
        stgs = [xnb[0][:, :].rearrange("p (h k) -> p h k", k=128), xnb[1][:, :].rearrange("p (h k) -> p h k", k=128),
                hT32[:, :, 0:128], hT32[:, :, 128:256],
                score_f[:, 0:1024].rearrange("p (h k) -> p h k", k=128), score_f[:, 1024:2048].rearrange("p (h k) -> p h k", k=128)]
        for dlt in range(3):
            for hh in range(2):
                si_ = dlt * 2 + hh
                bst = stgs[si_]
                for h8 in range(8):
                    h = hh * 8 + h8
                    base = {0: 255, 1: 127, 2: 239}[dlt]
                    src = bass.AP(ztd.tensor, h * ZH + base, [[384, 128], [1, 128]])
                    P.dma("sp" if h8 % 2 == 0 else "act", w=[("stg6", si_)], out=bst[:, h8, :], in_=src)
                if dlt < 2:
                    P.op("dve", V_.tensor_copy, r=[("stg6", si_)], w=[("Bq",)], out=Bq[:, hh * 8:(hh + 1) * 8, dlt, :], in_=bst)
                else:
                    P.op("dve", V_.tensor_copy, r=[("stg6", si_)], w=[("Bqm",)], out=Bqm[:, hh * 8:(hh + 1) * 8, :], in_=bst)
        P.flush()
        if stop_after == "setup":
            P.final_wait()
            return nc

        passes = [("meta", 0, 0)] + [("x", b, gi) for b in range(n_seq) for gi in range(NGRP)]
        n_meta_chunks = cid["pw1_3"] + 1
        order = []
        for ps_ in passes:
            n = n_meta_chunks if ps_[0] == "meta" else NCH
            order.extend(range(n))
        wst = dict(issued=0, pos=0)

        def wissue(i):
            ch = order[i]
            c = chunks[ch]
            free = c["nk"] * c["w"]
            s = i % NS
            P.dma("sp", r=[("wscr", ch)], w=[("ws", s)], out=wslot[s][:, :free], in_=wscr[ch][:, :free])

        def wnext(expect, issue=True):
            while issue and wst["issued"] < min(len(order), wst["pos"] + NS):
                wissue(wst["issued"])
                wst["issued"] += 1
            i = wst["pos"]
            ch = order[i]
            assert chunks[ch]["nm"] == expect, (chunks[ch]["nm"], expect)
            c = chunks[ch]
            s = i % NS
            wst["pos"] += 1
            return s, wslot[s][:, :c["nk"] * c["w"]].rearrange("p (k n) -> p k n", n=c["w"])

        def linear_fm(wv, s, nm_, rhs_fn, nk, Gg, evac, rkeys):
            for mi in range(nm_):
                b = bank()
                for k in range(nk):
                    P.op("pe", T_.matmul, r=[("ws", s)] + rkeys, w=[("ps", b)],
                         out=psf[b][:, 0:Gg], lhsT=wv[:, k, mi * 128:(mi + 1) * 128], rhs=rhs_fn(k),
                         start=(k == 0), stop=(k == nk - 1))
                evac(mi, b)

        def layer_norm(src, srcname, gi_, bi_, gtab, mode, Gg, NTg, out_rows=None):
            banks = []
            for tt in range(NTg):
                bA, bB = bank(), bank()
                banks.append((bA, bB))
                for kc in range(KC):
                    b = bA if kc < 4 else bB
                    P.op("pe", T_.transpose, r=[(srcname, kc), ("ident32",)], w=[("ps", b)],
                         out=psf[b][:, (kc % 4) * 128:(kc % 4 + 1) * 128], in_=src[:, kc, tt * 128:(tt + 1) * 128],
                         identity=ident32[:])
                P.op("dve", V_.bn_stats, r=[("ps", bA)], w=[("stats", tt, 0)], out=stats[:, tt, 0:6], in_=psf[bA][:, :])
                P.op("dve", V_.bn_stats, r=[("ps", bB)], w=[("stats", tt, 1)], out=stats[:, tt, 6:12], in_=psf[bB][:, :])
                P.op("dve", V_.bn_aggr, r=[("stats", tt, 0), ("stats", tt, 1)], w=[("mv", tt)], out=mv[:, tt, :], in_=stats[:, tt, :])
            mvk = [("mv", tt) for tt in range(NTg)]
            P.op("dve", V_.tensor_scalar, r=mvk, w=[("vpe",)], out=vpe[:, 0:NTg], in0=mv[:, 0:NTg, 1],
                 scalar1=LN_EPS, scalar2=None, op0=ALU.add)
            if USE_POOL_POW:
                P.op("pool", Q_.tensor_tensor, r=[("vpe",), ("mhalf",)], w=[("rstd",)], out=rstd[:, 0:NTg], in0=vpe[:, 0:NTg],
                     in1=mhalf[:, 0:NTg], op=ALU.pow)
            else:
                P.op("act", A_.activation, r=[("vpe",)], w=[("vpe2",)], out=vpe[:, 0:NTg], in_=vpe[:, 0:NTg], func=AF.Sqrt)
                P.op("dve", V_.reciprocal, r=[("vpe2",)], w=[("rstd",)], out=rstd[:, 0:NTg], in_=vpe[:, 0:NTg])
            P.op("dve", V_.scalar_tensor_tensor, r=mvk + [("rstd",)], w=[("nmr",)], out=nmr[:, 0:NTg], in0=mv[:, 0:NTg, 0],
                 scalar=-1.0, in1=rstd[:, 0:NTg], op0=ALU.mult, op1=ALU.mult)
            for tt in range(NTg):
                bA, bB = banks[tt]
                xn = xnb[tt % 2]
                xi = tt % 2
                P.op("act", A_.activation, r=[("ps", bA), ("rstd",), ("nmr",)], w=[("xn", xi, 0)],
                     out=xn[:, 0:512], in_=psf[bA][:, :], func=AF.Identity, scale=rstd[:, tt:tt + 1], bias=nmr[:, tt:tt + 1])
                P.op("dve", V_.tensor_scalar, r=[("ps", bB), ("rstd",), ("nmr",)], w=[("xn", xi, 1)],
                     out=xn[:, 512:1024], in0=psf[bB][:, :], scalar1=rstd[:, tt:tt + 1], scalar2=nmr[:, tt:tt + 1],
                     op0=ALU.mult, op1=ALU.add)
            cbanks = []
            for tt in range(NTg):
                xn = xnb[tt % 2]
                xi = tt % 2
                cA, cB = (bank2(), bank2()) if tt % 2 == 0 else (bank(), bank())
                cbanks.append((cA, cB))
                for kc in range(KC):
                    b = cA if kc < 4 else cB
                    P.op("pe", T_.transpose, r=[("xn", xi, kc // 4), ("ident32",)], w=[("ps", b)],
                         out=psf[b][:, (kc % 4) * 128:(kc % 4 + 1) * 128], in_=xn[:, kc * 128:(kc + 1) * 128],
                         identity=ident32[:])
            if mode == "conv":
                for tt in range(NTg):
                    cA, cB = cbanks[tt]
                    for kc in range(KC):
                        b = cA if kc < 4 else cB
                        pv = psf[b][:, (kc % 4) * 128:(kc % 4 + 1) * 128]
                        P.op("act", A_.activation, r=[("ps", b), ("cpar", 0), ("cpar", 1)], w=[("mixT", kc)],
                             out=mixT[:, kc, tt * 128:(tt + 1) * 128], in_=pv, func=AF.Silu,
                             scale=cpar[:, 0, kc:kc + 1], bias=cpar[:, 1, kc:kc + 1])
            else:
                for (dst, dkey) in (((hT16, "h16"), (hT32, "h32")) if mode == "main" else ((hT32, "h32"),)):
                    for tt in range(NTg):
                        cA, cB = cbanks[tt]
                        for kc in range(KC):
                            b = cA if kc < 4 else cB
                            pv = psf[b][:, (kc % 4) * 128:(kc % 4 + 1) * 128]
                            if kc < 4:
                                P.op("dve", V_.tensor_scalar, r=[("ps", b), ("lnp", gi_), ("lnp", bi_)], w=[(dkey, kc)],
                                     out=dst[:, kc, tt * 128:(tt + 1) * 128], in0=pv, scalar1=gtab[:, gi_, kc:kc + 1],
                                     scalar2=gtab[:, bi_, kc:kc + 1], op0=ALU.mult, op1=ALU.add)
                            else:
                                P.op("act", A_.activation, r=[("ps", b), ("lnp", gi_), ("lnp", bi_)], w=[(dkey, kc)],
                                     out=dst[:, kc, tt * 128:(tt + 1) * 128], in_=pv, func=AF.Identity, scale=gtab[:, gi_, kc:kc + 1],
                                     bias=gtab[:, bi_, kc:kc + 1])
            if mode == "final":
                for tt in range(NTg):
                    xn = xnb[tt % 2]
                    xi = tt % 2
                    dA, dB = bank(), bank()
                    for kc in range(KC):
                        b = dA if kc < 4 else dB
                        P.op("pe", T_.transpose, r=[("h32", kc), ("ident32",)], w=[("ps", b)],
                             out=psf[b][:, (kc % 4) * 128:(kc % 4 + 1) * 128], in_=hT32[:, kc, tt * 128:(tt + 1) * 128],
                             identity=ident32[:])
                    P.op("act", A_.copy, r=[("ps", dA)], w=[("xn", xi, 0)], out=xn[:, 0:512], in_=psf[dA][:, :])
                    P.op("dve", V_.tensor_copy, r=[("ps", dB)], w=[("xn", xi, 1)], out=xn[:, 512:1024], in_=psf[dB][:, :])
                    P.dma("sp", r=[("xn", xi, 0), ("xn", xi, 1)], w=[("out",)], out=out_rows(tt), in_=xn[:, :])

        def ffn(l, Gg, NTg):
            P.fence(A_NAMES, B_NAMES)
            for j in range(6):
                wcols = 512 if j < 5 else 256
                sgi, wg = wnext("g%d_%d" % (l, j))
                sui, wu = wnext("u%d_%d" % (l, j), issue=False)
                for mi in range(wcols // 128):
                    m = j * 4 + mi
                    bg, bu = bank(), bank()
                    for k in range(KC):
                        P.op("pe", T_.matmul, r=[("ws", sgi), ("h16", k)], w=[("ps", bg)], out=psf[bg][:, 0:Gg],
                             lhsT=wg[:, k, mi * 128:(mi + 1) * 128], rhs=hT16[:, k, 0:Gg], start=(k == 0), stop=(k == KC - 1))
                    for k in range(KC):
                        P.op("pe", T_.matmul, r=[("ws", sui), ("h16", k)], w=[("ps", bu)], out=psf[bu][:, 0:Gg],
                             lhsT=wu[:, k, mi * 128:(mi + 1) * 128], rhs=hT16[:, k, 0:Gg], start=(k == 0), stop=(k == KC - 1))
                    si = m % 2
                    P.op("act", A_.activation, r=[("ps", bg)], w=[("sgt", si)], out=sgt[si][:, 0:Gg], in_=psf[bg][:, 0:Gg], func=AF.Silu)
                    P.op("dve", V_.tensor_tensor, r=[("sgt", si), ("ps", bu)], w=[("actT", m)], out=actT[:, m, 0:Gg],
                         in0=sgt[si][:, 0:Gg], in1=psf[bu][:, 0:Gg], op=ALU.mult)
            for m in range(8):
                s, wd = wnext("d%d_%d" % (l, m))
                b = bank()
                for k in range(FC):
                    P.op("pe", T_.matmul, r=[("ws", s), ("actT", k)], w=[("ps", b)], out=psf[b][:, 0:Gg],
                         lhsT=wd[:, k, :], rhs=actT[:, k, 0:Gg], start=(k == 0), stop=(k == FC - 1))
                P.op("dve", V_.scalar_tensor_tensor, r=[("ps", b), ("h32", m)], w=[("h32", m)], out=hT32[:, m, 0:Gg],
                     in0=hT32[:, m, 0:Gg], scalar=ALPHA, in1=psf[b][:, 0:Gg], op0=ALU.mult, op1=ALU.add)

        def prefetch_x(pidx):
            if pidx >= len(passes):
                return
            kind_, b_, g_ = passes[pidx]
            if kind_ == "meta":
                return
            if isinstance(stop_after, int) and pidx >= stop_after:
                return
            P.fence(["R", "dgw", "Pt"], ["xin"])
            for tt in range(NT):
                r0 = g_ * G + tt * 128
                P.dma("sp", w=[("xin", tt)], out=xin[:, tt, :], in_=x[b_, r0:r0 + 128, :])

        def emit_group(kind, bsel, gi, pidx=0):
            is_meta = kind == "meta"
            NTg = 1 if is_meta else NT
            Gg = NTg * 128
            T0 = -1 if is_meta else gi * NT
            slot0 = (T0 + 1) * 128
            P.fence(B_NAMES, A_NAMES)
            for tt in range(NTg):
                xi = tt % 2
                xn = xnb[xi]
                if is_meta:
                    P.op("pool", Q_.memset, w=[("xn", xi, 0), ("xn", xi, 1)], ap=xn[:, :], constant=0.0)
                    P.dma("sp", w=[("xn", xi, 0), ("xn", xi, 1)], out=xn[0:16, :], in_=meta[:, :])
                if os.environ.get("K_STOPPH") == "A0":
                    return
                bA, bB = bank(), bank()
                for kc in range(KC):
                    b = bA if kc < 4 else bB
                    if is_meta:
                        P.op("pe", T_.transpose, r=[("xn", xi, kc // 4), ("ident32",)], w=[("ps", b)],
                             out=psf[b][:, (kc % 4) * 128:(kc % 4 + 1) * 128], in_=xn[:, kc * 128:(kc + 1) * 128], identity=ident32[:])
                    else:
                        P.op("pe", T_.transpose, r=[("xin", tt), ("ident32",)], w=[("ps", b)],
                             out=psf[b][:, (kc % 4) * 128:(kc % 4 + 1) * 128], in_=xin[:, tt, kc * 128:(kc + 1) * 128], identity=ident32[:])
                if os.environ.get("K_STOPPH") == "A1":
                    return
                for hf, b in ((0, bA), (1, bB)):
                    pv = psf[b][:, :].rearrange("p (k n) -> p k n", n=128)
                    hk = [("h32", kc) for kc in range(hf * 4, hf * 4 + 4)]
                    hk16 = [("h16", kc) for kc in range(hf * 4, hf * 4 + 4)]
                    if os.environ.get("K_X") != "noact":
                        P.op("act", A_.copy, r=[("ps", b)], w=hk, out=hT32[:, hf * 4:hf * 4 + 4, tt * 128:(tt + 1) * 128], in_=pv)
                    if os.environ.get("K_X") != "nodve":
                        P.op("dve", V_.tensor_copy, r=[("ps", b)] + (hk if os.environ.get("K_X") == "ser" else []), w=hk16, out=hT16[:, hf * 4:hf * 4 + 4, tt * 128:(tt + 1) * 128], in_=pv)
            if os.environ.get("K_STOPPH") == "A":
                return
            if not is_meta:
                P.fence(["xin"], ["R", "dgw", "Pt"])
            hkeys = [("h16", k) for k in range(KC)]
            rh = lambda k: hT16[:, k, 0:Gg]
            s, wv = wnext("qi")
            def ev(mi, b):
                P.op("act", A_.copy, r=[("ps", b)], w=[("qiT", mi)], out=qiT[:, mi, 0:Gg], in_=psf[b][:, 0:Gg])
            linear_fm(wv, s, 4, rh, KC, Gg, ev, hkeys)
            s, wv = wnext("kiwi")
            def ev(mi, b):
                P.op("dve", V_.tensor_copy, r=[("ps", b)], w=[("kiT2", T0 + 1 + t_) for t_ in range(NTg)],
                     out=kiT2[:, slot0:slot0 + Gg], in_=psf[b][:, 0:Gg])
            linear_fm(wv, s, 1, rh, KC, Gg, ev, hkeys)
            for tt in range(NTg):
                b = bank()
                for k in range(KC):
                    P.op("pe", T_.matmul, r=[("ws", s), ("h16", k)], w=[("ps", b)], out=psf[b][:, 0:8],
                         lhsT=hT16[:, k, tt * 128:(tt + 1) * 128], rhs=wv[:, k, 128:136], start=(k == 0), stop=(k == KC - 1))
                P.op("dve", V_.tensor_scalar, r=[("ps", b)], w=[("wi", tt)], out=wi[:, tt, :], in0=psf[b][:, 0:8],
                     scalar1=WI_SCALE, scalar2=None, op0=ALU.mult)
            if os.environ.get("K_STOPPH") == "B":
                return
            Ss = [(T0 + tt + 2) * 128 for tt in range(NTg)]
            for tt in range(NTg):
                for hh in range(NIH):
                    half = hh % 2
                    P.op("dve", V_.tensor_copy, r=[("qiT", hh // 2)], w=[("qiz", hh)],
                         out=qiz[hh][half * 64:(half + 1) * 64, :], in_=qiT[half * 64:(half + 1) * 64, hh // 2, tt * 128:(tt + 1) * 128])
                    P.op("act", A_.activation, r=[("wi", tt), ("identb",)], w=[("dgw", tt, hh)], out=dgw[:, tt, hh, :], in_=identb[:, :],
                         func=AF.Copy, scale=wi[:, tt, hh:hh + 1])
                S = Ss[tt]
                nkb = (S + 511) // 512
                iitems = [(kb, hh) for kb in range(nkb) for hh in range(NIH)]
                accb = {}
                rinfo = {}

                def idx_l(i, tt=tt, S=S):
                    kb, hh = iitems[i]
                    c0, c1 = kb * 512, min(S, kb * 512 + 512)
                    b = bank()
                    kkeys = [("kiT2", j) for j in range(c0 // 128, c1 // 128)]
                    P.op("pe", T_.matmul, r=[("qiz", hh)] + kkeys, w=[("ps", b)], out=psf[b][:, 0:c1 - c0],
                         lhsT=qiz[hh][:, :], rhs=kiT2[:, c0:c1], start=True, stop=True)
                    ri = i % NRB
                    if i % 2 == 0:
                        P.op("act", A_.activation, r=[("ps", b)], w=[("R", ri)], out=Rb[ri][:, 0:c1 - c0],
                             in_=psf[b][:, 0:c1 - c0], func=AF.Relu)
                    else:
                        P.op("dve", V_.tensor_scalar, r=[("ps", b)], w=[("R", ri)], out=Rb[ri][:, 0:c1 - c0],
                             in0=psf[b][:, 0:c1 - c0], scalar1=0.0, scalar2=None, op0=ALU.max)
                    rinfo[i] = ri

                def idx_a(i, tt=tt, S=S):
                    kb, hh = iitems[i]
                    c0, c1 = kb * 512, min(S, kb * 512 + 512)
                    if hh == 0:
                        accb[kb] = bank2()
                    ab = accb[kb]
                    ri = rinfo[i]
                    P.op("pe", T_.matmul, r=[("R", ri), ("dgw", tt, hh)], w=[("ps", ab)], out=psf[ab][:, 0:c1 - c0],
                         lhsT=dgw[:, tt, hh, :], rhs=Rb[ri][:, 0:c1 - c0], start=(hh == 0), stop=(hh == NIH - 1))
                    if hh == NIH - 1:
                        P.op("act", A_.copy, r=[("ps", ab)], w=[("score", tt, kb)], out=score[:, tt, c0:c1], in_=psf[ab][:, 0:c1 - c0])

                ISK = 3
                for i in range(len(iitems) + ISK):
                    if i < len(iitems):
                        idx_l(i)
                    if i - ISK >= 0:
                        idx_a(i - ISK)
                skeys = [("score", tt, kb) for kb in range(nkb)]
                P.op("dve", V_.tensor_scalar, r=skeys, w=[("maskq", tt), ("Amx", tt)], out=maskq[:, tt, 0:S], in0=score[:, tt, 0:S],
                     scalar1=1.0, scalar2=None, op0=ALU.mult, op1=ALU.max, accum_out=Amx[:, tt, 0:1])
                P.op("dve", V_.tensor_scalar, r=skeys, w=[("maskq", tt), ("Amn", tt)], out=maskq[:, tt, 0:S], in0=score[:, tt, 0:S],
                     scalar1=-1.0, scalar2=None, op0=ALU.mult, op1=ALU.max, accum_out=Amx[:, tt, 1:2])
                P.op("dve", V_.tensor_tensor, r=[("Amx", tt), ("Amn", tt)], w=[("Asm", tt)], out=Asm[:, tt:tt + 1], in0=Amx[:, tt, 0:1],
                     in1=Amx[:, tt, 1:2], op=ALU.max)
                P.op("pool", Q_.tensor_tensor, r=[("score", tt, 0), ("padb",)], w=[("score", tt, 0)], out=score[:, tt, 0:128],
                     in0=score[:, tt, 0:128], in1=padb[:, :], op=ALU.add)
                kbd = (S - 128) // 512
                P.op("pool", Q_.tensor_tensor, r=[("score", tt, kbd), ("cbias",)], w=[("score", tt, kbd)], out=score[:, tt, S - 128:S],
                     in0=score[:, tt, S - 128:S], in1=cbias[:, :], op=ALU.add)
                P.op("dve", V_.tensor_scalar, r=[("Asm", tt), ("pow2",)], w=[("wtab", tt)], out=wtab[:, tt, :], in0=pow2[:, :],
                     scalar1=Asm[:, tt:tt + 1], scalar2=None, op0=ALU.mult)
            def proj_q(qh):
                s, wv = wnext("q%d" % qh)
                def ev(mi, b, qh=qh):
                    m = qh * 4 + mi
                    P.op("act", A_.copy, r=[("ps", b)], w=[("qT", m)], out=qT[:, m, 0:Gg], in_=psf[b][:, 0:Gg])
                linear_fm(wv, s, 4, rh, KC, Gg, ev, hkeys)

            def proj_k(kh):
                s, wv = wnext("k%d" % kh)
                def ev(mi, b, kh=kh):
                    m = kh * 4 + mi
                    P.op("act", A_.copy, r=[("ps", b)], w=[("KT", m, T0 + 1 + t_) for t_ in range(NTg)],
                         out=KT[:, m, slot0:slot0 + Gg], in_=psf[b][:, 0:Gg])
                linear_fm(wv, s, 4, rh, KC, Gg, ev, hkeys)

            def proj_v(vh):
                s, wv = wnext("v%d" % vh)
                for tt in range(NTg):
                    b = bank()
                    for k in range(KC):
                        P.op("pe", T_.matmul, r=[("ws", s), ("h16", k)], w=[("ps", b)], out=psf[b][:, :],
                             lhsT=hT16[:, k, tt * 128:(tt + 1) * 128], rhs=wv[:, k, :], start=(k == 0), stop=(k == KC - 1))
                    jt = T0 + 1 + tt
                    pv = psf[b][:, :].rearrange("p (h d) -> p h d", d=64)
                    P.op("act", A_.copy, r=[("ps", b)], w=[("V", jt, vh)], out=Vt[:, jt, vh * 8:(vh + 1) * 8, 0:64], in_=pv)

            proj_chunks = [lambda: proj_q(0), lambda: proj_q(1), lambda: proj_k(0), lambda: proj_k(1),
                           lambda: proj_v(0), lambda: proj_v(1)]
            allsk = [[("score", tt, kb) for kb in range((Ss[tt] + 511) // 512)] for tt in range(NTg)]
            split = (NTg == 2)
            P.op("dve", V_.memset, w=[("test",)], ap=test[:, :], constant=0.0)
            P.op("dve", V_.memset, w=[("thr",)], ap=thr[:, 0:1], constant=255.5)
            if split:
                P.op("dve", V_.memset, w=[("thr",)], ap=thr[:, 1:2], constant=float(511 - Ss[1]))
                P.op("dve", V_.tensor_scalar, r=[("wtab", 1)], w=[("wtab", 1)], out=wtab[:, 1, :], in0=wtab[:, 1, :], scalar1=-1.0,
                     scalar2=None, op0=ALU.mult)
            wk = [("wtab", tt) for tt in range(NTg)]
            for it in range(1, NIT + 1):
                for tt in range(NTg):
                    S = Ss[tt]
                    if split and tt == 1:
                        P.op("act", A_.activation, r=allsk[tt] + [("test",)], w=[("maskq", tt), ("cnt", tt)], out=maskq[:, tt, 0:S],
                             in_=score[:, tt, 0:S], func=AF.Sign, bias=test[:, 1:2], scale=1.0, accum_out=cnt[:, 1:2])
                    else:
                        P.op("dve", V_.tensor_scalar, r=allsk[tt] + [("test",)], w=[("maskq", tt), ("cnt", tt)], out=maskq[:, tt, 0:S],
                             in0=score[:, tt, 0:S], scalar1=test[:, tt:tt + 1], scalar2=None, op0=ALU.is_ge, op1=ALU.add,
                             accum_out=cnt[:, tt:tt + 1])
                ck = [("cnt", tt) for tt in range(NTg)]
                sub = 0.5 if it < NIT else 1.0
                dst, dkey = (test, "test") if it < NIT else (lo, "lo")
                P.op("dve", V_.tensor_tensor, r=ck + [("thr",)], w=[("sg",)], out=sg[:, 0:NTg], in0=cnt[:, 0:NTg], in1=thr[:, 0:NTg],
                     op=ALU.is_ge)
                P.op("dve", V_.scalar_tensor_tensor, r=[("sg",)] + wk, w=[("tmpb",)], out=tmpb[:, 0:NTg], in0=sg[:, 0:NTg], scalar=sub,
                     in1=wtab[:, 0:NTg, it], op0=ALU.subtract, op1=ALU.mult)
                P.op("dve", V_.tensor_tensor, r=[("tmpb",), ("test",)], w=[(dkey,)], out=dst[:, 0:NTg], in0=test[:, 0:NTg],
                     in1=tmpb[:, 0:NTg], op=ALU.add)
                if proj_chunks and it >= 2:
                    proj_chunks.pop(0)()
            while proj_chunks:
                proj_chunks.pop(0)()
            if split:
                P.op("dve", V_.tensor_scalar, r=[("lo",)], w=[("lo",)], out=lo[:, 1:2], in0=lo[:, 1:2], scalar1=-1.0, scalar2=None,
                     op0=ALU.mult)
            for tt in range(NTg):
                S = Ss[tt]
                P.op("dve", V_.tensor_scalar, r=allsk[tt] + [("lo",)], w=[("maskq", tt)], out=maskq[:, tt, 0:S], in0=score[:, tt, 0:S],
                     scalar1=lo[:, tt:tt + 1], scalar2=None, op0=ALU.is_ge)
            if os.environ.get("K_STOPPH") == "C1":
                return
            P.op("act", A_.preload_act_table, func=AF.Exp)
            njt = T0 + NTg + 1
            for j in range(njt):
                tmin = max(0, j - 1 - T0)
                bb = bankb()
                for tt in range(tmin, NTg):
                    P.op("pe", T_.transpose, r=[("maskq", tt), ("identb",)], w=[("psb", bb)], out=psb[bb][:, tt * 128:(tt + 1) * 128],
                         in_=maskq[:, tt, j * 128:(j + 1) * 128], identity=identb[:])
                P.op("act", A_.copy, r=[("psb", bb)], w=[("maskT", j)], out=maskT[:, j, tmin * 128:Gg], in_=psb[bb][:, tmin * 128:Gg])
            if debug and (not is_meta) and bsel == 0 and gi == int(os.environ.get("K_DBG_GI", "0")):
                P.dma("sp", r=allsk[0], w=[("dbg", 0)], out=dbg[0][:, 0:Ss[0]], in_=score[:, 0, 0:Ss[0]])
                P.dma("sp", r=[("lo",)], w=[("dbg", 1)], out=dbg[1][:, 0:NTg], in_=lo[:, 0:NTg])
                P.dma("sp", r=[("cnt", 0)], w=[("dbg", 1, 1)], out=dbg[1][:, 8:8 + NTg], in_=cnt[:, 0:NTg])
            if os.environ.get("K_STOPPH") == "C":
                return
            nfull = T0 + 2
            units = []
            j = 0
            while j < nfull:
                if j + 1 < nfull and Gg == G and 2 * Gg <= 512:
                    units.append([j, j + 1])
                    j += 2
                else:
                    units.append([j])
                    j += 1
            for j in range(nfull, njt):
                units.append([j])
            items = [(h, ui) for h in range(NH) for ui in range(len(units))]
            st_info = {}
            head_bo = {}

            def emit_st(i):
                h, ui = items[i]
                unit = units[ui]
                c = h // 2
                half = h % 2
                zi = half * 2 + (c % 2)
                if ui == 0:
                    for hn in ([0, 1] if h == 0 else [h + 1]):
                        if hn < NH:
                            cn, hfn = hn // 2, hn % 2
                            zn = hfn * 2 + (cn % 2)
                            P.op("dve", V_.tensor_copy, r=[("qT", cn)], w=[("qz", zn)], out=qz[zn][hfn * 64:(hfn + 1) * 64, 0:Gg],
                                 in_=qT[hfn * 64:(hfn + 1) * 64, cn, 0:Gg])
                bs = bank()
                pi = i % NPT
                lo_col, hi_col = None, None
                for k_, j in enumerate(unit):
                    tmin = max(0, j - 1 - T0)
                    c0 = tmin * 128
                    base = k_ * Gg
                    nb = []
                    for tt in range(tmin, NTg):
                        T = T0 + tt
                        if j == 0:
                            if T == -1:
                                nb.append((tt, Bq[:, h, 0, :]))
                            elif T == 0:
                                nb.append((tt, Bqm[:, h, :]))
                        else:
                            dl = T + 1 - j
                            if dl in (0, 1):
                                nb.append((tt, Bq[:, h, dl, :]))
                    P.op("pe", T_.matmul, r=[("KT", c, j), ("qz", zi)], w=[("ps", bs)], out=psf[bs][:, base + c0:base + Gg],
                         lhsT=KT[:, c, j * 128:(j + 1) * 128], rhs=qz[zi][:, c0:Gg], start=True, stop=(len(nb) == 0))
                    for ii, (tt, bq) in enumerate(nb):
                        P.op("pe", T_.matmul, r=[("Bq",), ("Bqm",), ("identb",)], w=[("ps", bs)],
                             out=psf[bs][:, base + tt * 128:base + (tt + 1) * 128],
                             lhsT=bq, rhs=identb[:, :], start=False, stop=(ii == len(nb) - 1))
                    if lo_col is None:
                        lo_col = base + c0
                    hi_col = base + Gg
                P.op("act", A_.activation, r=[("ps", bs), ("cfar",)], w=[("Pt", pi)], out=Pt[pi][:, lo_col:hi_col], in_=psf[bs][:, lo_col:hi_col],
                     func=AF.Exp, scale=ATT_SCALE, bias=cfar[:, h:h + 1])
                if len(unit) == 2:
                    mview = maskT[:, unit[0]:unit[0] + 2, :].rearrange("p j g -> p (j g)")
                else:
                    mview = maskT[:, unit[0], lo_col:hi_col]
                me = "pool" if i % 3 == 0 else "dve"
                ME = Q_ if me == "pool" else V_
                P.op(me, ME.tensor_tensor, r=[("Pt", pi)] + [("maskT", j) for j in unit], w=[("Pt", pi)], out=Pt[pi][:, lo_col:hi_col],
                     in0=Pt[pi][:, lo_col:hi_col], in1=mview, op=ALU.mult)
                st_info[i] = pi

            def emit_pv(i):
                h, ui = items[i]
                unit = units[ui]
                pi = st_info[i]
                if ui == 0:
                    bo = bank2()
                    head_bo[h] = bo
                    P.op("pe", T_.matmul, r=[("zerob",)], w=[("ps", bo)], out=psf[bo][:, 0:NTg * 65], lhsT=zerob[:, 0:128],
                         rhs=zerob[:, 0:NTg * 65], start=True, stop=False)
                bo = head_bo[h]
                for k_, j in enumerate(unit):
                    tmin = max(0, j - 1 - T0)
                    base = k_ * Gg
                    for tt in range(tmin, NTg):
                        last = (ui == len(units) - 1 and k_ == len(unit) - 1 and tt == NTg - 1)
                        P.op("pe", T_.matmul, r=[("Pt", pi), ("V", j, h // 8), ("Vones",)], w=[("ps", bo)],
                             out=psf[bo][:, tt * 65:(tt + 1) * 65], lhsT=Pt[pi][:, base + tt * 128:base + (tt + 1) * 128], rhs=Vt[:, j, h, :],
                             start=False, stop=last)
                if ui == len(units) - 1:
                    ov = psf[bo][:, 0:NTg * 65].rearrange("p (t d) -> p t d", d=65)
                    ri = h % 2
                    P.op("dve", V_.reciprocal, r=[("ps", bo)], w=[("rec", ri)], out=rec[ri][:, 0:NTg], in_=ov[:, :, 64])
                    for tt in range(NTg):
                        P.op("dve", V_.tensor_scalar, r=[("ps", bo), ("rec", ri)], w=[("attn", tt, h // 2)],
                             out=attn[:, tt, h * 64:(h + 1) * 64], in0=ov[:, tt, 0:64], scalar1=rec[ri][:, tt:tt + 1], scalar2=None,
                             op0=ALU.mult)

            SK = 4
            for i in range(len(items) + SK):
                if i < len(items):
                    emit_st(i)
                if i - SK >= 0:
                    emit_pv(i - SK)
            if os.environ.get("K_STOPPH") == "D":
                return
            prefetch_x(pidx + 1)
            P.op("act", A_.preload_act_table, func=AF.Silu)
            for tt in range(NTg):
                bb = bankb()
                for kc in range(KC):
                    P.op("pe", T_.transpose, r=[("attn", tt, kc), ("identb",)], w=[("psb", bb)], out=psb[bb][:, kc * 128:(kc + 1) * 128],
                         in_=attn[:, tt, kc * 128:(kc + 1) * 128], identity=identb[:])
                P.op("act", A_.copy, r=[("psb", bb)], w=[("mixT", kc) for kc in range(KC)], out=mixT[:, :, tt * 128:(tt + 1) * 128],
                     in_=psb[bb][:, :].rearrange("p (k n) -> p k n", n=128))
            mkeys = [("mixT", k) for k in range(KC)]
            rm = lambda k: mixT[:, k, 0:Gg]
            for oh_ in range(2):
                s, wv = wnext("wo%d" % oh_)
                def ev(mi, b, oh_=oh_):
                    m = oh_ * 4 + mi
                    P.op("dve", V_.scalar_tensor_tensor, r=[("ps", b), ("h32", m)], w=[("h32", m)], out=hT32[:, m, 0:Gg],
                         in0=hT32[:, m, 0:Gg], scalar=ALPHA, in1=psf[b][:, 0:Gg], op0=ALU.mult, op1=ALU.add)
                linear_fm(wv, s, 4, rm, KC, Gg, ev, mkeys)
            if os.environ.get("K_STOPPH") == "E":
                return
            def dump(idx):
                if debug and (not is_meta) and bsel == 0 and gi == int(os.environ.get("K_DBG_GI", "0")):
                    P.dma("sp", r=[("h32", k) for k in range(KC)], w=[("dbg", idx)],
                          out=dbg[idx][:, 0:KC * Gg].rearrange("p (k g) -> p k g", g=Gg), in_=hT32[:, :, 0:Gg])
            dump(2)
            layer_norm(hT32, "h32", 0, 1, lnp, "main", Gg, NTg)
            dump(3)
            ffn(0, Gg, NTg)
            dump(6)
            layer_norm(hT32, "h32", 2, 3, lnp, "main", Gg, NTg)
            dump(4)
            if is_meta:
                P.op("pool", Q_.memset, w=[("uT", k) for k in range(KC)], ap=uT[:, :, 0:30], constant=0.0)
            else:
                if gi == 0:
                    P.op("pool", Q_.tensor_copy, r=[("ucarry_meta",)], w=[("uT", k) for k in range(KC)], out=uT[:, :, 0:30], in_=ucarry_meta[:, :, :])
                else:
                    P.op("pool", Q_.tensor_copy, r=[("ucarry",)], w=[("uT", k) for k in range(KC)], out=uT[:, :, 0:30], in_=ucarry[:, :, :])
            for i in range(4):
                s, wv = wnext("pw1_%d" % i)
                for mi in range(2):
                    cg = i * 2 + mi
                    bv, bg = bank(), bank()
                    for k in range(KC):
                        P.op("pe", T_.matmul, r=[("ws", s), ("h16", k)], w=[("ps", bv)], out=psf[bv][:, 0:Gg],
                             lhsT=wv[:, k, mi * 128:(mi + 1) * 128], rhs=hT16[:, k, 0:Gg], start=(k == 0), stop=(k == KC - 1))
                    for k in range(KC):
                        P.op("pe", T_.matmul, r=[("ws", s), ("h16", k)], w=[("ps", bg)], out=psf[bg][:, 0:Gg],
                             lhsT=wv[:, k, 256 + mi * 128:256 + (mi + 1) * 128], rhs=hT16[:, k, 0:Gg], start=(k == 0), stop=(k == KC - 1))
                    si = cg % 2
                    P.op("act", A_.activation, r=[("ps", bg), ("cpar", 5)], w=[("sgt", si)], out=sgt[si][:, 0:Gg], in_=psf[bg][:, 0:Gg],
                         func=AF.Tanh, bias=hb1g[:, cg:cg + 1], scale=0.5)
                    P.op("pool", Q_.tensor_scalar, r=[("sgt", si)], w=[("sgt", si)], out=sgt[si][:, 0:Gg], in0=sgt[si][:, 0:Gg],
                         scalar1=0.5, scalar2=0.5, op0=ALU.mult, op1=ALU.add)
                    P.op("dve", V_.scalar_tensor_tensor, r=[("ps", bv), ("sgt", si), ("cpar", 4)], w=[("uT", cg)],
                         out=uT[:, cg, 30:30 + Gg], in0=psf[bv][:, 0:Gg], scalar=cpar[:, 4, cg:cg + 1], in1=sgt[si][:, 0:Gg],
                         op0=ALU.add, op1=ALU.mult)
            ukeys = [("uT", k) for k in range(KC)]
            if is_meta:
                P.op("pool", Q_.tensor_copy, r=ukeys, w=[("ucarry_meta",)], out=ucarry_meta[:, :, :], in_=uT[:, :, 16:46])
                return
            P.op("pool", Q_.tensor_copy, r=ukeys, w=[("ucarry",)], out=ucarry[:, :, :], in_=uT[:, :, Gg:Gg + 30])
            for c in range(8):
                s, wv = wnext("dg%d" % c)
                b = bank()
                for j in range(31):
                    P.op("pe", T_.matmul, r=[("ws", s), ("uT", c)], w=[("ps", b)], out=psf[b][:, 0:Gg], lhsT=wv[:, j, :],
                         rhs=uT[:, c, j:j + Gg], start=(j == 0), stop=(j == 30))
                P.op("act", A_.activation, r=[("ps", b), ("cpar", 2)], w=[("yT32", c)], out=yT32[:, c, 0:Gg], in_=psf[b][:, 0:Gg],
                     func=AF.Identity, bias=cpar[:, 2, c:c + 1], scale=1.0)
            layer_norm(yT32, "yT32", 0, 1, cpar, "conv", Gg, NTg)
            for oh_ in range(2):
                s, wv = wnext("pw2_%d" % oh_)
                def ev(mi, b, oh_=oh_):
                    m = oh_ * 4 + mi
                    ti = m % 2
                    P.op("act", A_.activation, r=[("ps", b), ("cpar", 3)], w=[("tmpe", ti)], out=tmpe[ti][:, 0:Gg], in_=psf[b][:, 0:Gg],
                         func=AF.Identity, bias=cpar[:, 3, m:m + 1], scale=1.0)
                    P.op("dve", V_.scalar_tensor_tensor, r=[("tmpe", ti), ("h32", m)], w=[("h32", m)], out=hT32[:, m, 0:Gg],
                         in0=hT32[:, m, 0:Gg], scalar=ALPHA, in1=tmpe[ti][:, 0:Gg], op0=ALU.mult, op1=ALU.add)
                linear_fm(wv, s, 4, rm, KC, Gg, ev, mkeys)
            dump(5)
            layer_norm(hT32, "h32", 4, 5, lnp, "main", Gg, NTg)
            dump(7)
            ffn(1, Gg, NTg)
            orow = lambda tt: out[bsel, gi * G + tt * 128: gi * G + (tt + 1) * 128, :]
            layer_norm(hT32, "h32", 6, 7, lnp, "final", Gg, NTg, out_rows=orow)

        for pi_, (kind, bsel, gi) in enumerate(passes):
            if isinstance(stop_after, int) and pi_ >= stop_after:
                break
            emit_group(kind, bsel, gi, pi_)
        P.flush()
        P.final_wait()
        if info is not None:
            info["ms"] = dict(P.ms_count)
            info["dmax"] = max(16 * c for c in P.dcount)
    return nc


_CACHE = {}


def kernel(**inputs):
    n_cores = 8
    x = np.ascontiguousarray(inputs["x"], dtype=np.float32)
    B = x.shape[0]
    per = B // n_cores
    consts = host_consts()
    if "nc" not in _CACHE:
        _CACHE["nc"] = build_program(n_seq=per)
    nc = _CACHE["nc"]
    shared = {k: np.ascontiguousarray(v, dtype=np.float32) for k, v in inputs.items() if k != "x"}
    in_maps = []
    for c in range(n_cores):
        m = dict(shared)
        m.update(consts)
        m["x"] = np.ascontiguousarray(x[c * per:(c + 1) * per])
        in_maps.append(m)
    res = run_bass_kernel_spmd(nc, in_maps, core_ids=list(range(n_cores)))
    outs = [np.asarray(r["out"], dtype=np.float32) for r in res.results]
    return np.concatenate(outs, axis=0)
```
